# Optimizing a Trainium2 kernel written in Bass

```python
import math
import jax, jax.numpy as jnp
from jax import lax
import numpy as np

D_MODEL = 1024
BATCH = 8
SEQ = 4096
DEPTH = 1

ATT_HEADS = 8
ATT_HEAD_DIM = D_MODEL // ATT_HEADS
ATT_WIDTH = ATT_HEADS * ATT_HEAD_DIM
MOBA_BLOCK = 256
MOBA_TOPK = 3
Q_CHUNK = 16
REL_BUCKETS = 32
REL_MAX_DIST = 128
ML_HEADS = 4
ML_HEAD_DIM = D_MODEL // ML_HEADS
ML_WIDTH = ML_HEADS * ML_HEAD_DIM
ML_CHUNK = 64
CONV_WIDTH = 4
EPS = 1e-6
NEG_INF = -1e30

SPLIT = [ATT_WIDTH] * 4 + [ML_WIDTH] * 5 + [ML_HEADS, ML_HEADS] + [D_MODEL, D_MODEL]
D_IN = sum(SPLIT)
SPLIT_IDX = tuple(int(i) for i in np.cumsum(SPLIT)[:-1])

kernel_name = 'moba_mlstm_gated_hybrid'


def _rmsnorm(x, w):
    xf = x.astype(jnp.float32)
    y = xf * lax.rsqrt(jnp.mean(xf * xf, axis=-1, keepdims=True) + EPS)
    return (y * w.astype(jnp.float32)).astype(x.dtype)


def _rel_bucket(dist):
    max_exact = REL_BUCKETS // 2
    n = jnp.maximum(dist, 0)
    nf = jnp.maximum(n, 1).astype(jnp.float32)
    large = max_exact + (jnp.log(nf / max_exact) / math.log(REL_MAX_DIST / max_exact)
                         * (REL_BUCKETS - max_exact)).astype(jnp.int32)
    large = jnp.minimum(large, REL_BUCKETS - 1)
    return jnp.where(n < max_exact, n, large)


def _moba_attention(q, k, v, rel_bias):
    B, H, S, d = q.shape
    nb = -(-S // MOBA_BLOCK)
    s_pad = nb * MOBA_BLOCK
    topk = min(MOBA_TOPK, nb)
    pad = ((0, 0), (0, 0), (0, s_pad - S), (0, 0))
    kp = jnp.pad(k, pad)
    vp = jnp.pad(v, pad)
    k_blocks = kp.reshape(B, H, nb, MOBA_BLOCK, d)
    v_blocks = vp.reshape(B, H, nb, MOBA_BLOCK, d)
    k_mean = jnp.mean(k_blocks.astype(jnp.float32), axis=3)
    bias_hb = rel_bias.T.astype(jnp.float32)
    scale = d ** -0.5
    bi = jnp.arange(B)[:, None, None, None]
    hi = jnp.arange(H)[None, :, None, None]
    hi5 = jnp.arange(H)[None, :, None, None, None]
    blk_ids = jnp.arange(nb)
    offs = jnp.arange(MOBA_BLOCK)

    def chunk(start):
        qc = lax.dynamic_slice_in_dim(q, start, Q_CHUNK, axis=2).astype(jnp.float32)
        qpos = start + jnp.arange(Q_CHUNK)
        qblk = start // MOBA_BLOCK
        gate = jnp.einsum('bhqd,bhnd->bhqn', qc, k_mean)
        gate = jnp.where(blk_ids < qblk, gate, NEG_INF)
        _, idx = lax.top_k(gate, topk)
        sel_ok = idx < qblk
        k_sel = k_blocks[bi, hi, idx]
        v_sel = v_blocks[bi, hi, idx]
        kpos_sel = idx[..., None] * MOBA_BLOCK + offs
        bias_sel = bias_hb[hi5, _rel_bucket(qpos[:, None, None] - kpos_sel)]
        logit_sel = jnp.einsum('bhqd,bhqkjd->bhqkj', qc, k_sel) * scale + bias_sel
        logit_sel = jnp.where(sel_ok[..., None], logit_sel, NEG_INF)
        logit_sel = logit_sel.reshape(B, H, Q_CHUNK, topk * MOBA_BLOCK)
        own_start = qblk * MOBA_BLOCK
        k_own = lax.dynamic_slice_in_dim(kp, own_start, MOBA_BLOCK, axis=2)
        v_own = lax.dynamic_slice_in_dim(vp, own_start, MOBA_BLOCK, axis=2)
        dist_own = qpos[:, None] - (own_start + offs)[None, :]
        bias_own = bias_hb[:, _rel_bucket(dist_own)]
        logit_own = jnp.einsum('bhqd,bhjd->bhqj', qc, k_own) * scale + bias_own
        logit_own = jnp.where(dist_own >= 0, logit_own, NEG_INF)
        p = jax.nn.softmax(jnp.concatenate([logit_sel, logit_own], axis=-1), axis=-1)
        p_sel = p[..., :topk * MOBA_BLOCK].reshape(B, H, Q_CHUNK, topk, MOBA_BLOCK)
        p_own = p[..., topk * MOBA_BLOCK:]
        o = (jnp.einsum('bhqkj,bhqkjd->bhqd', p_sel, v_sel)
             + jnp.einsum('bhqj,bhjd->bhqd', p_own, v_own))
        return o.astype(q.dtype)

    starts = jnp.arange(S // Q_CHUNK) * Q_CHUNK
    out = lax.map(chunk, starts)
    return jnp.moveaxis(out, 0, 2).reshape(B, H, S, d)


def _causal_dwconv(u, w, b):
    y = lax.conv_general_dilated(u, w[:, None, :].astype(u.dtype), (1,), [(CONV_WIDTH - 1, 0)],
                                 dimension_numbers=('NWC', 'WIO', 'NWC'),
                                 feature_group_count=u.shape[-1])
    return y + b.astype(u.dtype)


def _mlstm_chunkwise(q, k, v, ig, fg):
    B, NH, S, dh = q.shape
    L = ML_CHUNK
    nc = S // L

    def to_chunks(t):
        return jnp.moveaxis(t.reshape((B, NH, nc, L) + t.shape[3:]), 2, 0)

    logf = jax.nn.log_sigmoid(fg)
    causal = jnp.tril(jnp.ones((L, L), dtype=bool))

    def step(carry, xs):
        C, n, m = carry
        qc, kc, vc, igc, lfc = xs
        b = jnp.cumsum(lfc, axis=-1)
        log_d = jnp.where(causal, b[..., :, None] - b[..., None, :] + igc[..., None, :], -jnp.inf)
        m_t = jnp.maximum(b + m[..., None], jnp.max(log_d, axis=-1))
        d_mat = jnp.exp(log_d - m_t[..., None])
        inter = jnp.exp(b + m[..., None] - m_t)
        s = jnp.einsum('bhtd,bhsd->bhts', qc, kc) * d_mat
        num = (inter[..., None] * jnp.einsum('bhtd,bhde->bhte', qc, C)
               + jnp.einsum('bhts,bhse->bhte', s, vc))
        den = inter * jnp.einsum('bhtd,bhd->bht', qc, n) + jnp.sum(s, axis=-1)
        h = num / jnp.maximum(jnp.abs(den), jnp.exp(-m_t))[..., None]
        b_last = b[..., -1]
        w_log = b_last[..., None] - b + igc
        m_new = jnp.maximum(b_last + m, jnp.max(w_log, axis=-1))
        decay = jnp.exp(b_last + m - m_new)
        w_s = jnp.exp(w_log - m_new[..., None])
        C_new = decay[..., None, None] * C + jnp.einsum('bhs,bhsd,bhse->bhde', w_s, kc, vc)
        n_new = decay[..., None] * n + jnp.einsum('bhs,bhsd->bhd', w_s, kc)
        return (C_new, n_new, m_new), h

    init = (jnp.zeros((B, NH, dh, dh), jnp.float32),
            jnp.zeros((B, NH, dh), jnp.float32),
            jnp.zeros((B, NH), jnp.float32))
    _, hs = lax.scan(step, init, (to_chunks(q), to_chunks(k), to_chunks(v),
                                  to_chunks(ig), to_chunks(logf)))
    return jnp.moveaxis(hs, 0, 2).reshape(B, NH, S, dh)


def _hybrid_layer(x, c, w_ada, b_ada, norm_w, w_in, q_norm_w, k_norm_w, rel_bias, conv_w, conv_b,
                  b_igate, b_fgate, ml_norm_w, w_att_proj, w_ml_proj, w_out):
    B, S, _ = x.shape
    f32 = jnp.float32
    ada = c @ w_ada + b_ada
    shift, scale, gate = jnp.split(ada[:, None, :], 3, axis=-1)
    h = _rmsnorm(x, norm_w) * (1 + scale) + shift
    (qa, ka, va, za, qm, km, vm, zm, om, ig, fg, ga, gm) = jnp.split(h @ w_in, SPLIT_IDX, axis=-1)

    def heads(t, nh):
        return t.reshape(B, S, nh, -1).transpose(0, 2, 1, 3)

    def merge(t):
        return t.transpose(0, 2, 1, 3).reshape(B, S, -1)

    qa = _rmsnorm(heads(qa, ATT_HEADS), q_norm_w)
    ka = _rmsnorm(heads(ka, ATT_HEADS), k_norm_w)
    ya = merge(_moba_attention(qa, ka, heads(va, ATT_HEADS), rel_bias))
    ya = (ya * jax.nn.silu(za)) @ w_att_proj

    qk = jax.nn.silu(_causal_dwconv(jnp.concatenate([qm, km], axis=-1), conv_w, conv_b))
    qm, km = jnp.split(qk, 2, axis=-1)
    hm = _mlstm_chunkwise(heads(qm, ML_HEADS).astype(f32),
                          heads(km, ML_HEADS).astype(f32) * (ML_HEAD_DIM ** -0.5),
                          heads(vm, ML_HEADS).astype(f32),
                          (ig + b_igate).astype(f32).transpose(0, 2, 1),
                          (fg + b_fgate).astype(f32).transpose(0, 2, 1))
    hm = jax.nn.sigmoid(heads(om, ML_HEADS).astype(f32)) * hm
    hm = _rmsnorm(hm, ml_norm_w.reshape(ML_HEADS, 1, ML_HEAD_DIM))
    ym = (merge(hm).astype(x.dtype) * jax.nn.silu(zm)) @ w_ml_proj

    y = jax.nn.sigmoid(ga) * ya + jax.nn.sigmoid(gm) * ym
    return x + gate * (y @ w_out)


def setup_inputs(seed: int = 0) -> dict:
    key = jax.random.key(seed)
    ks = jax.random.split(key, 18)
    f32 = jnp.float32

    def nrm(k, shape, s):
        return jax.random.normal(k, shape, f32) * s

    D = D_MODEL
    return {
        'x': nrm(ks[0], (BATCH, SEQ, D), 1.0),
        'c': nrm(ks[1], (BATCH, D), 1.0),
        'w_ada': nrm(ks[2], (DEPTH, D, 3 * D), 0.5 * D ** -0.5),
        'b_ada': nrm(ks[3], (DEPTH, 3 * D), 0.02),
        'norm_w': 1.0 + nrm(ks[4], (DEPTH, D), 0.02),
        'w_in': nrm(ks[5], (DEPTH, D, D_IN), D ** -0.5),
        'q_norm_w': 1.0 + nrm(ks[6], (DEPTH, ATT_HEAD_DIM), 0.02),
        'k_norm_w': 1.0 + nrm(ks[7], (DEPTH, ATT_HEAD_DIM), 0.02),
        'rel_bias': nrm(ks[8], (REL_BUCKETS, ATT_HEADS), 0.5),
        'conv_w': nrm(ks[9], (DEPTH, CONV_WIDTH, 2 * ML_WIDTH), CONV_WIDTH ** -0.5),
        'conv_b': nrm(ks[10], (DEPTH, 2 * ML_WIDTH), 0.02),
        'b_igate': nrm(ks[11], (DEPTH, ML_HEADS), 0.1),
        'b_fgate': jnp.linspace(3.0, 6.0, ML_HEADS, dtype=f32)[None, :] + nrm(ks[12], (DEPTH, ML_HEADS), 0.1),
        'ml_norm_w': 1.0 + nrm(ks[13], (DEPTH, ML_WIDTH), 0.02),
        'w_att_proj': nrm(ks[14], (DEPTH, ATT_WIDTH, D), ATT_WIDTH ** -0.5),
        'w_ml_proj': nrm(ks[15], (DEPTH, ML_WIDTH, D), ML_WIDTH ** -0.5),
        'w_out': nrm(ks[16], (DEPTH, D, D), D ** -0.5),
    }


def reference(x, c, w_ada, b_ada, norm_w, w_in, q_norm_w, k_norm_w, rel_bias, conv_w, conv_b,
              b_igate, b_fgate, ml_norm_w, w_att_proj, w_ml_proj, w_out):
    for l in range(DEPTH):
        x = _hybrid_layer(x, c, w_ada[l], b_ada[l], norm_w[l], w_in[l], q_norm_w[l], k_norm_w[l],
                          rel_bias, conv_w[l], conv_b[l], b_igate[l], b_fgate[l], ml_norm_w[l],
                          w_att_proj[l], w_ml_proj[l], w_out[l])
    return x
```

```python
import contextlib
import numpy as np
import ml_dtypes
import concourse.bass as bass
import concourse.mybir as mybir
from concourse.bass_utils import run_bass_kernel_spmd

F32 = mybir.dt.float32
BF16 = mybir.dt.bfloat16
AF = mybir.ActivationFunctionType
ALU = mybir.AluOpType
AX = mybir.AxisListType

T = 4096
D = 1024
P = 128
NKC = 8
NG = 8
GS = 512
EPS = 1e-6
NEGV = -30000.0
ATT_SCALE = 128.0 ** -0.5
ML_KSCALE = 256.0 ** -0.5
COL_QA, COL_KA, COL_VA, COL_ZA = 0, 1024, 2048, 3072
COL_QM, COL_KM, COL_VM, COL_ZM, COL_OM = 4096, 5120, 6144, 7168, 8192
COL_IG, COL_FG, COL_GA, COL_GM = 9216, 9220, 9224, 10248
D_IN = 11272


class Buf:
    __slots__ = ("w", "r", "name", "psum")

    def __init__(self, name="", psum=False):
        self.w = None
        self.r = {}
        self.name = name
        self.psum = psum


class Sched:
    ENGS = ("pe", "act", "dve", "pool", "sp")

    def __init__(self, nc, stack):
        self.nc = nc
        self.stack = stack
        self.ops = {e: [] for e in self.ENGS}
        self.cnt = {e: 0 for e in self.ENGS}
        self.waited = {e: {} for e in self.ENGS}
        self.sems = {}
        self.dcnt = {}
        for e in self.ENGS:
            self.sems[e] = stack.enter_context(nc.semaphore("s_" + e))

    def dma_sem(self, name):
        self.sems[name] = self.stack.enter_context(self.nc.semaphore("d_" + name))
        self.dcnt[name] = 0
        return name

    def dma_batch(self, eng, sem, items):
        n = len(items)
        toks = []
        for i, (fn, reads, writes) in enumerate(items):
            toks.append(self.op(eng, fn, reads=reads, writes=writes, dma=sem, dma_extra=n - 1 - i))
        return toks

    def op(self, eng, fn, reads=(), writes=(), dma=None, dma_extra=0):
        waits = {}
        wd = self.waited[eng]

        def need(dep):
            if dep is None:
                return
            sk, val = dep
            if sk == "pe" and eng == "pe":
                return
            if dma is not None and sk == dma and val > self.dcnt[dma]:
                return
            if wd.get(sk, 0) >= val:
                return
            if waits.get(sk, 0) < val:
                waits[sk] = val

        for b in reads:
            need(b.w)
            if b.psum:
                for sk, v in b.r.items():
                    if sk != eng:
                        need((sk, v))
        for b in writes:
            need(b.w)
            for sk, v in b.r.items():
                need((sk, v))
        for sk, v in waits.items():
            wd[sk] = v
        if dma is None:
            self.cnt[eng] += 1
            tok = (eng, self.cnt[eng])
            inc = (eng, 1)
        else:
            self.dcnt[dma] += 16
            tok = (dma, self.dcnt[dma] + 16 * dma_extra)
            inc = (dma, 16)
        for b in writes:
            b.w = tok
            b.r = {}
        for b in reads:
            if b.r.get(tok[0], 0) < tok[1]:
                b.r[tok[0]] = tok[1]
        self.ops[eng].append((tuple(waits.items()), fn, inc))
        return tok

    def barrier(self):
        toks = [(e, self.cnt[e]) for e in self.ENGS if self.cnt[e] > 0]
        toks += [(k, v) for k, v in self.dcnt.items() if v > 0]
        for e in self.ENGS:
            w = []
            for sk, v in toks:
                if sk == e:
                    continue
                if self.waited[e].get(sk, 0) < v:
                    self.waited[e][sk] = v
                    w.append((sk, v))
            self.ops[e].append((tuple(w), None, None))

    def emit(self):
        nc = self.nc
        sems = self.sems
        ops_now = self.ops
        self.ops = {e: [] for e in self.ENGS}

        def replay(ename, eng):
            for waits, fn, inc in ops_now[ename]:
                for sk, v in waits:
                    eng.wait_ge(sems[sk], v)
                if fn is None:
                    continue
                ins = fn(eng)
                ins.then_inc(sems[inc[0]], inc[1])

        with nc.Block() as block:
            @block.tensor
            def _(e):
                replay("pe", e)

            @block.scalar
            def _(e):
                replay("act", e)

            @block.vector
            def _(e):
                replay("dve", e)

            @block.gpsimd
            def _(e):
                replay("pool", e)

            @block.sync
            def _(e):
                replay("sp", e)


class Ring:
    def __init__(self, K, name, n, shape, dt, psum=False, dma=False):
        self.n = n
        self.t = []
        self.b = []
        self.s = []
        for i in range(n):
            if psum:
                self.t.append(K.psum(f"{name}{i}", shape, dt))
            else:
                self.t.append(K.sb(f"{name}{i}", shape, dt))
            self.b.append(Buf(f"{name}{i}", psum=psum))
            self.s.append(K.S.dma_sem(f"{name}{i}_{K.uid()}") if dma else None)
        self.i = 0

    def next(self):
        i = self.i % self.n
        self.i += 1
        return self.t[i], self.b[i], self.s[i]


class K:
    def __init__(self, nc, S, stack):
        self.nc = nc
        self.S = S
        self.stack = stack
        self.pstack = stack
        self._uid = 0

    def uid(self):
        self._uid += 1
        return self._uid

    def sb(self, name, shape, dt):
        return self.pstack.enter_context(self.nc.sbuf_tensor(f"{name}_{self.uid()}", shape, dt))

    def psum(self, name, shape, dt):
        return self.pstack.enter_context(self.nc.psum_tensor(f"{name}_{self.uid()}", shape, dt))


def build(stop_after=None, dbg_shape=None):
    nc = bass.Bass("TRN2", target_bir_lowering=False)

    def din(name, shape, dt=F32):
        return nc.dram_tensor(name, shape, dt, kind="ExternalInput").ap()

    x = din("x", [T, D])
    ccol = din("ccol", [P, 8])
    w_ada = din("w_ada", [D, 3 * D])
    b_ada = din("b_ada", [1, 3 * D])
    norm_w = din("norm_w", [1, D])
    w_in = din("w_in", [D, D_IN])
    qnw = din("qnw", [1, P])
    knw = din("knw", [1, P])
    utab = din("utab", [8, P, 1024])
    bias31 = din("bias31", [1, 8])
    convw = din("convw", [P, 16, 4])
    convb = din("convb", [P, 16])
    b_ig = din("b_ig", [4, 1])
    b_fg = din("b_fg", [4, 1])
    mlnw = din("mlnw", [1, D])
    w_att = din("w_att", [D, D])
    w_ml = din("w_ml", [D, D])
    w_out = din("w_out", [D, D])
    c_ident = din("c_ident", [P, P])
    c_caus = din("c_caus", [P, P])
    c_ind = din("c_ind", [16, 16 * P])
    out = nc.dram_tensor("out", [T, D], F32, kind="ExternalOutput").ap()
    dbg = None
    if dbg_shape is not None:
        dbg = nc.dram_tensor("dbg", list(dbg_shape), F32, kind="ExternalOutput").ap()
    yaT_d = nc.dram_tensor("yaT_d", [8, P, T], BF16, kind="Internal").ap()
    ymT_d = nc.dram_tensor("ymT_d", [8, P, T], BF16, kind="Internal").ap()
    sga_d = nc.dram_tensor("sga_d", [8, P, T], F32, kind="Internal").ap()
    sgm_d = nc.dram_tensor("sgm_d", [8, P, T], F32, kind="Internal").ap()

    w_in_r = w_in.rearrange("(kc p) n -> p kc n", p=P)

    with contextlib.ExitStack() as st:
        S = Sched(nc, st)
        k = K(nc, S, st)
        op = S.op
        ident = k.sb("ident", [P, P], F32)
        ident_bf = k.sb("ident_bf", [P, P], BF16)
        ones_bf = k.sb("ones_bf", [P, P], BF16)
        ones_f = k.sb("ones_f", [P, P], F32)
        caus = k.sb("caus", [P, P], F32)
        ind_bf = k.sb("ind_bf", [16, 16 * P], BF16)
        gate_half = k.sb("gate_half", [P, D], F32)
        mlw_bc = k.sb("mlw_bc", [P, D], F32)
        qw_bc = k.sb("qw_bc", [P, P], F32)
        kw_bc = k.sb("kw_bc", [P, P], F32)
        b31_bc = k.sb("b31_bc", [P, 8], F32)
        convw_sb = k.sb("convw_sb", [P, 16, 4], F32)
        convb_sb = k.sb("convb_sb", [P, 16], F32)
        wtok = k.sb("wtok", [P, 128], F32)
        cltok = k.sb("cltok", [P, 128], F32)
        decbc = k.sb("decbc", [P, 128], F32)
        Bc = Buf("consts")
        Bgate = Buf("gate_half")
        Bml = Buf("mlgates")
        hst = st.enter_context(contextlib.ExitStack())
        k.pstack = hst
        hT = k.sb("hT", [P, NKC, T], BF16)
        hTb = [Buf(f"hT{g}") for g in range(NG)]
        k.pstack = st
        dsem = S.dma_sem("const")
        dsem_out = S.dma_sem("dbgout")

        cl = [(ident[:], c_ident[:, :]), (caus[:], c_caus[:, :]),
              (mlw_bc[:], mlnw[0:1, :].partition_broadcast(P)), (qw_bc[:], qnw[0:1, :].partition_broadcast(P)),
              (kw_bc[:], knw[0:1, :].partition_broadcast(P)), (b31_bc[:], bias31[0:1, :].partition_broadcast(P)),
              (convw_sb[:], convw[:, :, :]), (convb_sb[:], convb[:, :])]
        S.dma_batch("sp", dsem, [((lambda e, d_=d_, s_=s_: e.dma_start(out=d_, in_=s_)), [], [Bc]) for d_, s_ in cl])
        op("dve", lambda e: e.tensor_copy(ident_bf[:], ident[:]), reads=[Bc], writes=[Bc])
        op("pool", lambda e: e.dma_start(out=ind_bf[:], in_=c_ind[:, :]), writes=[Bc], dma=S.dma_sem("cind"))
        op("dve", lambda e: e.memset(ones_bf[:], 1.0), writes=[Bc])
        op("dve", lambda e: e.memset(ones_f[:], 1.0), writes=[Bc])

        def finish(src_fn=None):
            toks = []
            if src_fn is not None:
                toks = src_fn()
            S.ops["sp"].append((tuple(toks), None, None))
            S.emit()

        def dump(dst, src, bufs):
            return op("sp", lambda e: e.dma_start(out=dst, in_=src), reads=bufs, dma=dsem_out)

        with contextlib.ExitStack() as ph:
            k.pstack = ph
            adarow = k.sb("adarow", [P, 3 * D], F32)
            Bada = Buf("adarow")
            ccol_sb = k.sb("ccol_sb", [P, 8], F32)
            cbc = k.sb("cbc", [P, 8, P], F32)
            nwbc = k.sb("nwbc", [P, D], F32)
            Abc = k.sb("Abc", [P, D], F32)
            junk = k.sb("junk", [P, D], F32)
            ssq = k.sb("ssq", [P, 32], F32)
            rs = k.sb("rs", [P, 32], F32)
            Bcc = Buf("ccol")
            Bcbc = Buf("cbc")
            Bnw = Buf("nwbc")
            BA = Buf("Abc")
            Bjunk = Buf("junk")
            wst = Ring(k, "wada", 4, [P, 8, 256], F32, dma=True)
            pa = Ring(k, "pa", 2, [P, 512], F32, psum=True)
            ptr = Ring(k, "ptr", 6, [P, 512], F32, psum=True)
            xr = Ring(k, "xt", 4, [P, D], F32, dma=True)
            xnr = Ring(k, "xn", 4, [P, D], F32)
            op("sp", lambda e: e.dma_start(out=ccol_sb[:], in_=ccol[:, :]), writes=[Bcc], dma=S.dma_sem("ccol"))
            op("sp", lambda e: e.dma_start(out=adarow[:], in_=b_ada[0:1, :].partition_broadcast(P)), writes=[Bada], dma=S.dma_sem("bada"))
            op("sp", lambda e: e.dma_start(out=nwbc[:], in_=norm_w[0:1, :].partition_broadcast(P)), writes=[Bnw], dma=S.dma_sem("nwbc"))
            op("dve", lambda e: e.tensor_copy(cbc[:], ccol_sb[:].unsqueeze(2).to_broadcast([P, 8, P])), reads=[Bcc], writes=[Bcbc])
            w_ada_r = w_ada.rearrange("(kc p) n -> p kc n", p=P)
            xpre = []
            for tt in range(4):
                xt, xb, xs = xr.next()
                op("sp", lambda e, xt=xt, tt=tt: e.dma_start(out=xt[:], in_=x[tt * P:(tt + 1) * P, :]), writes=[xb], dma=xs)
                xpre.append((xt, xb, xs))
            for cg in range(12):
                wt, wb, ws = wst.next()
                op("sp" if cg % 2 == 0 else "act", lambda e, wt=wt, cg=cg: e.dma_start(out=wt[:], in_=w_ada_r[:, :, cg * 256:(cg + 1) * 256]), writes=[wb], dma=ws)
                pt_, pb, _ = pa.next()
                for kc in range(8):
                    op("pe", lambda e, pt_=pt_, wt=wt, kc=kc: e.matmul(pt_[:, 0:256], lhsT=cbc[:, kc, :], rhs=wt[:, kc, :], start=(kc == 0), stop=(kc == 7)),
                       reads=[Bcbc, wb], writes=[pb])
                op("dve", lambda e, pt_=pt_, cg=cg: e.tensor_tensor(adarow[:, cg * 256:(cg + 1) * 256], pt_[:, 0:256], adarow[:, cg * 256:(cg + 1) * 256], ALU.add),
                   reads=[pb, Bada], writes=[Bada])
            op("dve", lambda e: e.scalar_tensor_tensor(out=Abc[:], in0=adarow[:, D:2 * D], scalar=1.0, in1=nwbc[:], op0=ALU.add, op1=ALU.mult),
               reads=[Bada, Bnw], writes=[BA])
            op("dve", lambda e: e.tensor_scalar(gate_half[:], adarow[:, 2 * D:3 * D], 0.5, None, ALU.mult), reads=[Bada], writes=[Bgate])
            Bsst = [Buf(f"ssq{t}") for t in range(32)]
            lagq = []

            def stage2(tt, banks):
                g = tt // 4
                for half, (pt_, pb) in enumerate(banks):
                    op("act", lambda e, pt_=pt_, half=half, tt=tt: e.activation(hT[:, half * 4:half * 4 + 4, tt * P:(tt + 1) * P],
                                                                               pt_[:, 0:512].rearrange("p (a b) -> p a b", b=P), AF.Copy),
                       reads=[pb], writes=[hTb[g]])

            for tt in range(32):
                if tt < 4:
                    xt, xb, xs = xpre[tt]
                else:
                    xt, xb, xs = xr.next()
                    op("sp", lambda e, xt=xt, tt=tt: e.dma_start(out=xt[:], in_=x[tt * P:(tt + 1) * P, :]), writes=[xb], dma=xs)
                op("act", lambda e, xt=xt, tt=tt: e.activation(junk[:], xt[:], AF.Square, accum_out=ssq[:, tt:tt + 1]), reads=[xb], writes=[Bjunk, Bsst[tt]])
                op("act", lambda e, tt=tt: e.activation(rs[:, tt:tt + 1], ssq[:, tt:tt + 1], AF.Sqrt, bias=EPS, scale=1.0 / D), reads=[Bsst[tt]], writes=[Bsst[tt]])
                op("dve", lambda e, tt=tt: e.reciprocal(rs[:, tt:tt + 1], rs[:, tt:tt + 1]), reads=[Bsst[tt]], writes=[Bsst[tt]])
                xn, xnb, _ = xnr.next()
                op("dve", lambda e, xn=xn, xt=xt, tt=tt: e.scalar_tensor_tensor(out=xn[:], in0=xt[:], scalar=rs[:, tt:tt + 1], in1=Abc[:], op0=ALU.mult, op1=ALU.mult),
                   reads=[xb, Bsst[tt], BA], writes=[xnb])
                op("pool", lambda e, xn=xn: e.tensor_tensor(xn[:], xn[:], adarow[:, 0:D], ALU.add), reads=[xnb, Bada], writes=[xnb])
                banks = []
                for half in range(2):
                    pt_, pb, _ = ptr.next()
                    for j in range(4):
                        kc = half * 4 + j
                        op("pe", lambda e, pt_=pt_, xn=xn, kc=kc, j=j: e.transpose(pt_[:, j * P:(j + 1) * P], xn[:, kc * P:(kc + 1) * P], ident[:]),
                           reads=[xnb, Bc], writes=[pb])
                    banks.append((pt_, pb))
                lagq.append((tt, banks))
                if len(lagq) > 2:
                    stage2(*lagq.pop(0))
            while lagq:
                stage2(*lagq.pop(0))
            S.barrier()
            if stop_after == "ph1":
                tmpf = k.sb("tmpf", [P, T], F32)
                Bt = Buf("tmpf")
                toks = []
                for kc in range(8):
                    op("dve", lambda e, kc=kc: e.tensor_copy(tmpf[:], hT[:, kc, :]), reads=hTb, writes=[Bt])
                    toks.append(dump(dbg[kc * P:(kc + 1) * P, :], tmpf[:], [Bt]))
                toks.append(dump(dbg[8 * P:9 * P, 0:D], gate_half[:], [Bgate]))
                S.ops["sp"].append((tuple(toks), None, None))
                S.emit()
                return nc
            S.emit()
        with contextlib.ExitStack() as ph:
            k.pstack = ph
            Wg = k.sb("Wg", [P, 8, 8], BF16)
            BWg = Buf("Wg")
            wsem = S.dma_sem("wg")
            op("pool", lambda e: e.dma_start(out=Wg[:], in_=w_in_r[:, :, COL_IG:COL_IG + 8]), writes=[BWg], dma=wsem)
            Wga = k.sb("Wga", [P, 8, D], BF16)
            Wgm = k.sb("Wgm", [P, 8, D], BF16)
            BWga = Buf("Wga")
            BWgm = Buf("Wgm")
            wsem2 = S.dma_sem("wga")
            wsem3 = S.dma_sem("wgm")
            op("pool", lambda e: e.dma_start(out=Wga[:], in_=w_in_r[:, :, COL_GA:COL_GA + D]), writes=[BWga], dma=wsem2)
            op("pool", lambda e: e.dma_start(out=Wgm[:], in_=w_in_r[:, :, COL_GM:COL_GM + D]), writes=[BWgm], dma=wsem3)
            bi_sb = k.sb("bi_sb", [4, 1], F32)
            bf_sb = k.sb("bf_sb", [4, 1], F32)
            negbf = k.sb("negbf", [4, 1], F32)
            Bb = Buf("gbias")
            S.dma_batch("sp", S.dma_sem("gb"), [((lambda e: e.dma_start(out=bi_sb[:], in_=b_ig[:, :])), [], [Bb]),
                                                ((lambda e: e.dma_start(out=bf_sb[:], in_=b_fg[:, :])), [], [Bb])])
            op("dve", lambda e: e.tensor_scalar(negbf[:], bf_sb[:], -1.0, None, ALU.mult), reads=[Bb], writes=[Bb])
            bufA = k.sb("bufA", [4, T], F32)
            bufB = k.sb("bufB", [4, T], F32)
            bufC = k.sb("bufC", [4, T], F32)
            BA_, BB_, BC_ = Buf("bufA"), Buf("bufB"), Buf("bufC")
            cm = k.sb("cm", [4, 32], F32)
            Mx = k.sb("Mx", [4, 32], F32)
            Mp = k.sb("Mp", [4, 32], F32)
            dec = k.sb("dec", [4, 32], F32)
            X4 = k.sb("X4", [4, 32, 4], F32)
            Bsm = Buf("gsmall")
            pg = Ring(k, "pg", 2, [P, 512], F32, psum=True)
            ptok = k.psum("ptok", [P, 512], F32)
            Bptok = Buf("ptok", psum=True)
            pdec = k.psum("pdec", [P, 512], F32)
            Bpdec = Buf("pdec", psum=True)
            for g in range(NG):
                pf, pfb, _ = pg.next()
                for kc in range(8):
                    op("pe", lambda e, pf=pf, kc=kc, g=g: e.matmul(pf[0:4, :], lhsT=Wg[:, kc, 4:8], rhs=hT[:, kc, g * GS:(g + 1) * GS], start=(kc == 0), stop=(kc == 7)),
                       reads=[BWg, hTb[g]], writes=[pfb])
                op("act", lambda e, pf=pf, g=g: e.activation(bufA[:, g * GS:(g + 1) * GS], pf[0:4, :], AF.Exp, bias=negbf[:], scale=-1.0), reads=[pfb, Bb], writes=[BA_])
                pi, pib, _ = pg.next()
                for kc in range(8):
                    op("pe", lambda e, pi=pi, kc=kc, g=g: e.matmul(pi[0:4, :], lhsT=Wg[:, kc, 0:4], rhs=hT[:, kc, g * GS:(g + 1) * GS], start=(kc == 0), stop=(kc == 7)),
                       reads=[BWg, hTb[g]], writes=[pib])
                op("act", lambda e, pi=pi, g=g: e.activation(bufC[:, g * GS:(g + 1) * GS], pi[0:4, :], AF.Identity, bias=bi_sb[:]), reads=[pib, Bb], writes=[BC_])
            op("act", lambda e: e.activation(bufA[:], bufA[:], AF.Ln, bias=1.0, scale=1.0), reads=[BA_], writes=[BA_])
            op("dve", lambda e: e.tensor_tensor_scan(bufB[:], bufA[:], bufA[:], 0.0, ALU.add, ALU.max), reads=[BA_], writes=[BB_])
            op("dve", lambda e: e.tensor_tensor(bufC[:], bufC[:], bufB[:], ALU.add), reads=[BC_, BB_], writes=[BC_])
            op("dve", lambda e: e.tensor_reduce(out=cm[:], in_=bufC[:].rearrange("p (c l) -> p c l", l=P), axis=AX.X, op=ALU.max), reads=[BC_], writes=[Bsm])
            op("dve", lambda e: e.tensor_tensor_scan(Mx[:], cm[:], cm[:], 0.0, ALU.max, ALU.max), reads=[Bsm], writes=[Bsm])
            op("dve", lambda e: e.memset(Mp[:, 0:1], 0.0), reads=[Bsm], writes=[Bsm])
            op("dve", lambda e: e.tensor_copy(Mp[:, 1:32], Mx[:, 0:31]), reads=[Bsm], writes=[Bsm])
            op("dve", lambda e: e.tensor_tensor(dec[:], Mp[:], Mx[:], ALU.subtract), reads=[Bsm], writes=[Bsm])
            op("act", lambda e: e.activation(dec[:], dec[:], AF.Exp), reads=[Bsm], writes=[Bsm])
            v3 = lambda t_: t_[:].rearrange("p (c l) -> p c l", l=P)
            Mb = lambda: Mx[:].unsqueeze(2).to_broadcast([4, 32, P])
            op("dve", lambda e: e.tensor_tensor(v3(bufA), v3(bufC), Mb(), ALU.subtract), reads=[BC_, Bsm, BA_], writes=[BA_])
            op("act", lambda e: e.activation(bufA[:], bufA[:], AF.Exp), reads=[BA_], writes=[BA_])
            op("dve", lambda e: e.tensor_tensor(v3(bufB), v3(bufB), Mb(), ALU.subtract), reads=[BB_, Bsm], writes=[BB_])
            op("act", lambda e: e.activation(bufB[:], bufB[:], AF.Exp), reads=[BB_], writes=[BB_])
            for c in range(32):
                op("pe", lambda e, c=c: e.transpose(ptok[:, c * 4:(c + 1) * 4], bufA[0:4, c * P:(c + 1) * P], ident[0:4, 0:4]), reads=[BA_, Bc], writes=[Bptok])
                op("pe", lambda e, c=c: e.transpose(ptok[:, 128 + c * 4:128 + (c + 1) * 4], bufB[0:4, c * P:(c + 1) * P], ident[0:4, 0:4]), reads=[BB_, Bc], writes=[Bptok])
            op("dve", lambda e: e.tensor_copy(wtok[:], ptok[:, 0:128]), reads=[Bptok], writes=[Bml])
            op("dve", lambda e: e.tensor_copy(cltok[:], ptok[:, 128:256]), reads=[Bptok], writes=[Bml])
            op("dve", lambda e: e.tensor_tensor(X4[:], dec[:].unsqueeze(2).to_broadcast([4, 32, 4]), ident[0:4, 0:4].unsqueeze(1).to_broadcast([4, 32, 4]), ALU.mult),
               reads=[Bsm, Bc], writes=[Bsm])
            op("pe", lambda e: e.matmul(pdec[:, 0:128], lhsT=ones_f[0:4, :], rhs=X4[:].rearrange("p c h -> p (c h)"), start=True, stop=True), reads=[Bsm, Bc], writes=[Bpdec])
            op("dve", lambda e: e.tensor_copy(decbc[:], pdec[:, 0:128]), reads=[Bpdec], writes=[Bml])
            pgg = Ring(k, "pgg", 4, [P, 512], F32, psum=True)
            sgr = Ring(k, "sg", 4, [P, 512], F32, dma=True)
            Bsga = Buf("sga_d")
            for g in range(NG):
                for cc in range(8):
                    for W_, Wb_, dst in ((Wga, BWga, sga_d), (Wgm, BWgm, sgm_d)):
                        pb_, pbb, _ = pgg.next()
                        for kc in range(8):
                            op("pe", lambda e, pb_=pb_, W_=W_, kc=kc, cc=cc, g=g: e.matmul(pb_[:], lhsT=W_[:, kc, cc * P:(cc + 1) * P], rhs=hT[:, kc, g * GS:(g + 1) * GS],
                                                                                        start=(kc == 0), stop=(kc == 7)),
                               reads=[Wb_, hTb[g]], writes=[pbb])
                        sg, sgb, sgs = sgr.next()
                        op("act", lambda e, sg=sg, pb_=pb_: e.activation(sg[:], pb_[:], AF.Tanh, scale=0.5), reads=[pbb], writes=[sgb])
                        op("sp", lambda e, sg=sg, dst=dst, cc=cc, g=g: e.dma_start(out=dst[cc, :, g * GS:(g + 1) * GS], in_=sg[:]), reads=[sgb], dma=sgs)
            S.barrier()
            if stop_after == "ph2":
                toks = []
                toks.append(dump(dbg[0:P, 0:128], wtok[:], [Bml]))
                toks.append(dump(dbg[0:P, 128:256], cltok[:], [Bml]))
                toks.append(dump(dbg[0:P, 256:384], decbc[:], [Bml]))
                tmp = k.sb("tmpd", [P, 512], F32)
                Bt = Buf("tmpd")
                ds2 = S.dma_sem("dbg2")
                for i, (src, cc, g) in enumerate(((sga_d, 3, 5), (sgm_d, 6, 2))):
                    op("sp", lambda e, src=src, cc=cc, g=g: e.dma_start(out=tmp[:], in_=src[cc, :, g * GS:(g + 1) * GS]), writes=[Bt], dma=ds2)
                    toks.append(dump(dbg[P * (i + 1):P * (i + 2), 0:512], tmp[:], [Bt]))
                S.ops["sp"].append((tuple(toks), None, None))
                S.emit()
                return nc
            S.emit()
        with contextlib.ExitStack() as ph:
            k.pstack = ph
            QKT = k.sb("QKT", [P, 4, T], BF16)
            Vall = k.sb("Vall", [P, 32, 256], BF16)
            kmS = k.sb("kmS", [P, 64], F32)
            tmpk = k.sb("tmpk", [P, 2, 16], F32)
            kmT = k.sb("kmT", [P, 2, 16], BF16)
            kmL = k.sb("kmL", [P, 2, 16], BF16)
            W3 = k.sb("W3", [P, 8, 3, 256], BF16)
            BW3 = Buf("W3")
            w3sem = S.dma_sem("w3")
            BQK = [Buf(f"QK{g}") for g in range(NG)]
            BV = [Buf(f"V{g}") for g in range(NG)]
            Bkm = Buf("km")
            Wzr = Ring(k, "Wz", 2, [P, 8, P], BF16, dma=True)
            Utr = Ring(k, "Ut", 2, [P, 1024], F32, dma=True)
            Bya = Buf("yaT_d")
            for hp in range(4):
                if hp == 0:
                    S.dma_batch("pool", w3sem, [((lambda e, j=j, col=col: e.dma_start(out=W3[:, :, j, :], in_=w_in_r[:, :, col:col + 256])), [], [BW3])
                                                for j, col in enumerate((COL_QA, COL_KA, COL_VA))])
                with contextlib.ExitStack() as sa:
                    k.pstack = sa
                    bX = Ring(k, "bX", 2, [P, 512], F32, psum=True)
                    bY = Ring(k, "bY", 2, [P, 512], F32, psum=True)
                    bZ = Ring(k, "bZ", 2, [P, 1024], BF16, psum=True)
                    pkm = k.psum("pkm", [P, 512], F32)
                    Bpkm = Buf("pkm", psum=True)
                    sqr = Ring(k, "sq", 2, [P, 512], F32)
                    s4r = Ring(k, "ssq4", 2, [P, 4], F32)
                    qknr = Ring(k, "qkn", 3, [P, 512], BF16)
                    prevT = None

                    def emitT(tt, qkn, qknb):
                        g = tt // 4
                        bz, bzb, _ = bZ.next()
                        for j in range(4):
                            op("pe", lambda e, bz=bz, qkn=qkn, j=j: e.transpose(bz[:, j * P:(j + 1) * P], qkn[:, j * P:(j + 1) * P], ident_bf[:]), reads=[qknb, Bc], writes=[bzb])
                        op("act", lambda e, bz=bz, tt=tt: e.activation(QKT[:, 0:4, tt * P:(tt + 1) * P], bz[:, 0:512].rearrange("p (a b) -> p a b", b=P), AF.Copy),
                           reads=[bzb], writes=[BQK[g]])
                        for hh in range(2):
                            op("pe", lambda e, qkn=qkn, hh=hh, tt=tt: e.matmul(pkm[:, hh * 32 + tt:hh * 32 + tt + 1], lhsT=qkn[:, (2 + hh) * P:(3 + hh) * P], rhs=ones_bf[:, 0:1],
                                                                            start=True, stop=True),
                               reads=[qknb, Bc], writes=[Bpkm])

                    for tt in range(32):
                        g = tt // 4
                        bx, bxb, _ = bX.next()
                        for j in range(2):
                            for kc in range(8):
                                op("pe", lambda e, bx=bx, j=j, kc=kc, tt=tt: e.matmul(bx[:, j * 256:(j + 1) * 256], lhsT=hT[:, kc, tt * P:(tt + 1) * P], rhs=W3[:, kc, j, :],
                                                                                    start=(kc == 0), stop=(kc == 7)),
                                   reads=[hTb[g], BW3], writes=[bxb])
                        by, byb, _ = bY.next()
                        for kc in range(8):
                            op("pe", lambda e, by=by, kc=kc, tt=tt: e.matmul(by[:, 0:256], lhsT=hT[:, kc, tt * P:(tt + 1) * P], rhs=W3[:, kc, 2, :], start=(kc == 0), stop=(kc == 7)),
                               reads=[hTb[g], BW3], writes=[byb])
                        sq, sqb, _ = sqr.next()
                        s4, s4b, _ = s4r.next()
                        op("act", lambda e, sq=sq, bx=bx: e.activation(sq[:], bx[:], AF.Square), reads=[bxb], writes=[sqb])
                        op("dve", lambda e, sq=sq, s4=s4: e.tensor_reduce(out=s4[:], in_=sq[:].rearrange("p (a b) -> p a b", b=P), axis=AX.X, op=ALU.add), reads=[sqb], writes=[s4b])
                        op("act", lambda e, s4=s4: e.activation(s4[:], s4[:], AF.Sqrt, bias=EPS, scale=1.0 / P), reads=[s4b], writes=[s4b])
                        op("dve", lambda e, s4=s4: e.reciprocal(s4[:], s4[:]), reads=[s4b], writes=[s4b])
                        qkn, qknb, _ = qknr.next()
                        for j in range(4):
                            wbc = qw_bc if j < 2 else kw_bc
                            op("dve", lambda e, qkn=qkn, bx=bx, s4=s4, j=j, wbc=wbc: e.scalar_tensor_tensor(out=qkn[:, j * P:(j + 1) * P], in0=bx[:, j * P:(j + 1) * P], scalar=s4[:, j:j + 1],
                                                                                                          in1=wbc[:], op0=ALU.mult, op1=ALU.mult),
                               reads=[bxb, s4b, Bc], writes=[qknb])
                        op("act", lambda e, by=by, tt=tt: e.activation(Vall[:, tt, :], by[:, 0:256], AF.Copy), reads=[byb], writes=[BV[g]])
                        if prevT is not None:
                            emitT(*prevT)
                        prevT = (tt, qkn, qknb)
                    emitT(*prevT)
                    op("dve", lambda e: e.tensor_copy(kmS[:], pkm[:, 0:64]), reads=[Bpkm], writes=[Bkm])
                    kv = lambda i: kmS[:].rearrange("p (h b two) -> p h b two", h=2, two=2)[:, :, :, i]
                    op("dve", lambda e: e.tensor_tensor(tmpk[:], kv(0), kv(1), ALU.add), reads=[Bkm], writes=[Bkm])
                    op("dve", lambda e: e.tensor_scalar(tmpk[:], tmpk[:], 1.0 / 256.0, None, ALU.mult), reads=[Bkm], writes=[Bkm])
                    op("dve", lambda e: e.tensor_copy(kmT[:], tmpk[:]), reads=[Bkm], writes=[Bkm])
                    op("dve", lambda e: e.tensor_tensor(kmL[:], tmpk[:], kmT[:], ALU.subtract), reads=[Bkm], writes=[Bkm])
                    S.barrier()
                    S.emit()
                with contextlib.ExitStack() as sbk:
                    k.pstack = sbk
                    bS = Ring(k, "bS", 3, [P, 512], F32, psum=True)
                    bO = Ring(k, "bO", 2, [P, 512], F32, psum=True)
                    bSumr = Ring(k, "bSum", 2, [P, 512], F32, psum=True)
                    pnm = k.psum("pnm", [P, 1024], BF16)
                    Bpnm = Buf("pnm", psum=True)
                    tzr = Ring(k, "tz", 2, [P, 512], F32)
                    zsr = Ring(k, "zs", 3, [P, 512], F32)
                    gsbr = Ring(k, "gsb", 2, [P, 16], F32)
                    gallr = Ring(k, "gall", 2, [P, 64], F32)
                    t8r = Ring(k, "top8", 2, [P, 8], F32)
                    nmr = Ring(k, "nm", 8, [P, 16], BF16)
                    nmTr = Ring(k, "nmT", 3, [16, 512], BF16)
                    PTr = Ring(k, "PT", 4, [P, 512], BF16)
                    tmpdr = Ring(k, "tmpd", 2, [P, 512], F32)
                    recr = Ring(k, "rec", 2, [P, 512], F32)
                    osbr = Ring(k, "osb", 2, [P, 512], F32)
                    yagr = Ring(k, "yag", 2, [P, 512], BF16, dma=True)
                    heads = []
                    for hh in range(2):
                        hd = 2 * hp + hh
                        Wz, Wzb, Wzs = Wzr.next()
                        op("pool", lambda e, Wz=Wz, hd=hd: e.dma_start(out=Wz[:], in_=w_in_r[:, :, COL_ZA + hd * P:COL_ZA + (hd + 1) * P]), writes=[Wzb], dma=Wzs)
                        Ut, Utb, Uts = Utr.next()
                        op("sp", lambda e, Ut=Ut, hd=hd: e.dma_start(out=Ut[:], in_=utab[hd, :, :]), writes=[Utb], dma=Uts)
                        heads.append((hd, Wz, Wzb, Ut, Utb))
                    if hp + 1 < 4:
                        S.dma_batch("pool", w3sem, [((lambda e, j=j, col=col, hp=hp: e.dma_start(out=W3[:, :, j, :], in_=w_in_r[:, :, col + (hp + 1) * 256:col + (hp + 2) * 256])), [], [BW3])
                                                    for j, col in enumerate((COL_QA, COL_KA, COL_VA))])
                    LOOK = 2
                    for hh in range(2):
                        hd, Wz, Wzb, Ut, Utb = heads[hh]
                        gst = {}

                        def proA(g, hh=hh, Wz=Wz, Wzb=Wzb, gst=gst):
                            st_ = {}
                            bz_, bzb_, _ = bS.next()
                            for kc in range(8):
                                op("pe", lambda e, bz_=bz_, kc=kc, Wz=Wz, g=g: e.matmul(bz_[:], lhsT=Wz[:, kc, :], rhs=hT[:, kc, g * GS:(g + 1) * GS], start=(kc == 0), stop=(kc == 7)),
                                   reads=[Wzb, hTb[g]], writes=[bzb_])
                            tz, tzb, _ = tzr.next()
                            zs, zsb, _ = zsr.next()
                            op("act", lambda e, tz=tz, bz_=bz_: e.activation(tz[:], bz_[:], AF.Tanh, scale=0.5), reads=[bzb_], writes=[tzb])
                            op("dve", lambda e, zs=zs, tz=tz, bz_=bz_: e.scalar_tensor_tensor(out=zs[:], in0=tz[:], scalar=1.0, in1=bz_[:], op0=ALU.add, op1=ALU.mult),
                               reads=[tzb, bzb_], writes=[zsb])
                            st_["zs"] = (zs, zsb)
                            st_["nms"] = []
                            if g >= 2:
                                pg_, pgb_, _ = bS.next()
                                for qi in range(4):
                                    tq = 4 * g + qi
                                    op("pe", lambda e, pg_=pg_, qi=qi, tq=tq, hh=hh: e.matmul(pg_[:, qi * 16:(qi + 1) * 16], lhsT=QKT[:, hh, tq * P:(tq + 1) * P], rhs=kmT[:, hh, :], start=True, stop=False),
                                       reads=[BQK[g], Bkm], writes=[pgb_])
                                    op("pe", lambda e, pg_=pg_, qi=qi, tq=tq, hh=hh: e.matmul(pg_[:, qi * 16:(qi + 1) * 16], lhsT=QKT[:, hh, tq * P:(tq + 1) * P], rhs=kmL[:, hh, :], start=False, stop=True),
                                       reads=[BQK[g], Bkm], writes=[pgb_])
                                gall, gallb, _ = gallr.next()
                                op("dve", lambda e, gall=gall, pg_=pg_: e.tensor_copy(gall[:], pg_[:, 0:64]), reads=[pgb_], writes=[gallb])
                                for qi in range(4):
                                    qblk = 2 * g + qi // 2
                                    gsb, gsbb, _ = gsbr.next()
                                    t8, t8b, _ = t8r.next()
                                    nm, nmb, _ = nmr.next()
                                    op("dve", lambda e, gsb=gsb: e.memset(gsb[:], -1.0e30), writes=[gsbb])
                                    op("dve", lambda e, gsb=gsb, qi=qi, qblk=qblk, gall=gall: e.tensor_copy(gsb[:, 0:qblk], gall[:, qi * 16:qi * 16 + qblk]), reads=[gallb], writes=[gsbb])
                                    op("dve", lambda e, gsb=gsb, t8=t8: e.max(t8[:], gsb[:]), reads=[gsbb], writes=[t8b])
                                    op("dve", lambda e, nm=nm: e.memset(nm[:], 0.0), writes=[nmb])
                                    op("dve", lambda e, nm=nm, gsb=gsb, t8=t8, qblk=qblk: e.tensor_scalar(nm[:, 0:qblk], gsb[:, 0:qblk], t8[:, 2:3], NEGV, ALU.is_lt, ALU.mult),
                                       reads=[gsbb, t8b], writes=[nmb])
                                    st_["nms"].append((nm, nmb))
                            gst[g] = st_

                        def proB(g, gst=gst):
                            st_ = gst[g]
                            st_["nmT"] = (None, None)
                            if g >= 2:
                                for qi, (nm, nmb) in enumerate(st_["nms"]):
                                    op("pe", lambda e, nm=nm, qi=qi: e.transpose(pnm[0:16, qi * P:(qi + 1) * P], nm[:], ident_bf[:]), reads=[nmb, Bc], writes=[Bpnm])
                                nmT, nmTb, _ = nmTr.next()
                                op("act", lambda e, nmT=nmT: e.activation(nmT[:], pnm[0:16, 0:512], AF.Copy), reads=[Bpnm], writes=[nmTb])
                                st_["nmT"] = (nmT, nmTb)

                        def epilogue(g, bo, bob, bsum, bsumb, hd=hd, gst=gst):
                            zs, zsb = gst[g]["zs"]
                            rec, recb, _ = recr.next()
                            osb, osbb, _ = osbr.next()
                            yag, yagb, yags = yagr.next()
                            op("act", lambda e, osb=osb, bo=bo: e.activation(osb[:], bo[:], AF.Copy), reads=[bob], writes=[osbb])
                            op("dve", lambda e, rec=rec, bsum=bsum: e.reciprocal(rec[:], bsum[:]), reads=[bsumb], writes=[recb])
                            op("pool", lambda e, osb=osb, rec=rec: e.tensor_tensor(osb[:], osb[:], rec[:], ALU.mult), reads=[osbb, recb], writes=[osbb])
                            op("pool", lambda e, yag=yag, osb=osb, zs=zs: e.tensor_tensor(yag[:], osb[:], zs[:], ALU.mult), reads=[osbb, zsb], writes=[yagb])
                            op("sp", lambda e, yag=yag, g=g, hd=hd: e.dma_start(out=yaT_d[hd, :, g * GS:(g + 1) * GS], in_=yag[:]), reads=[yagb], writes=[Bya], dma=yags)

                        pending = []

                        def pop_pv(hh=hh):
                            g, kt, NKT, c0, Nq, pt, ptb, bo, bob, bsum, bsumb = pending.pop(0)
                            op("pe", lambda e, bo=bo, c0=c0, kt=kt, hh=hh, pt=pt, Nq=Nq, NKT=NKT: e.matmul(bo[:, c0:GS], lhsT=Vall[:, kt, hh * P:(hh + 1) * P], rhs=pt[:, 0:Nq], start=(kt == 0), stop=(kt == NKT - 1)),
                               reads=[BV[kt // 4], ptb], writes=[bob])
                            op("pe", lambda e, bsum=bsum, c0=c0, kt=kt, pt=pt, Nq=Nq, NKT=NKT: e.matmul(bsum[:, c0:GS], lhsT=ones_bf[:], rhs=pt[:, 0:Nq], start=(kt == 0), stop=(kt == NKT - 1)),
                               reads=[Bc, ptb], writes=[bsumb])
                            if kt == NKT - 1:
                                epilogue(g, bo, bob, bsum, bsumb)

                        proA(0)
                        proB(0)
                        proA(1)
                        for g in range(NG):
                            NKT = 4 * (g + 1)
                            bo, bob, _ = bO.next()
                            bsum, bsumb, _ = bSumr.next()
                            for kt in range(NKT):
                                if kt == max(2, NKT - 2) and g + 1 < NG:
                                    proB(g + 1)
                                nmT, nmTb = gst[g]["nmT"]
                                j = kt - 4 * g
                                c0 = P * j if j >= 1 else 0
                                Nq = GS - c0
                                masked = (g >= 2) and (kt // 2 <= 2 * g)
                                bs, bsb, _ = bS.next()
                                op("pe", lambda e, bs=bs, kt=kt, g=g, c0=c0, Nq=Nq, masked=masked, hh=hh: e.matmul(bs[:, 0:Nq], lhsT=QKT[:, 2 + hh, kt * P:(kt + 1) * P],
                                                                                                         rhs=QKT[:, hh, g * GS + c0:(g + 1) * GS], start=True, stop=(not masked)),
                                   reads=[BQK[kt // 4], BQK[g]], writes=[bsb])
                                if masked:
                                    bk = kt // 2
                                    op("pe", lambda e, bs=bs, bk=bk, nmT=nmT, c0=c0, Nq=Nq: e.matmul(bs[:, 0:Nq], lhsT=ind_bf[0:16, bk * P:(bk + 1) * P], rhs=nmT[0:16, c0:GS],
                                                                                                  start=False, stop=True),
                                       reads=[Bc, nmTb], writes=[bsb])
                                pt, ptb, _ = PTr.next()
                                if j >= -1:
                                    td, tdb, _ = tmpdr.next()
                                    u0 = 384 - P * j + c0
                                    op("dve", lambda e, td=td, bs=bs, u0=u0, Nq=Nq, Ut=Ut: e.scalar_tensor_tensor(out=td[:, 0:Nq], in0=bs[:, 0:Nq], scalar=ATT_SCALE, in1=Ut[:, u0:u0 + Nq],
                                                                                                        op0=ALU.mult, op1=ALU.add),
                                       reads=[bsb, Utb], writes=[tdb])
                                    op("act", lambda e, pt=pt, td=td, Nq=Nq: e.activation(pt[:, 0:Nq], td[:, 0:Nq], AF.Exp), reads=[tdb], writes=[ptb])
                                else:
                                    op("act", lambda e, pt=pt, bs=bs, hd=hd: e.activation(pt[:], bs[:], AF.Exp, bias=b31_bc[:, hd:hd + 1], scale=ATT_SCALE), reads=[bsb, Bc], writes=[ptb])
                                pending.append((g, kt, NKT, c0, Nq, pt, ptb, bo, bob, bsum, bsumb))
                                if len(pending) > LOOK:
                                    pop_pv()
                            if g + 2 < NG:
                                proA(g + 2)
                        while pending:
                            pop_pv()
                    S.barrier()
                    if stop_after == "ph3" and hp == 0:
                        toks = []
                        tb = k.sb("tmpd", [P, 1024], BF16)
                        tf = k.sb("tmpf", [P, 1024], F32)
                        Bt = Buf("tmpd")
                        Bt2 = Buf("tmpf")
                        ds2 = S.dma_sem("dbg2")
                        for hd in range(2):
                            for q4 in range(4):
                                op("sp", lambda e, hd=hd, q4=q4: e.dma_start(out=tb[:], in_=yaT_d[hd, :, q4 * 1024:(q4 + 1) * 1024]), reads=[Bya], writes=[Bt], dma=ds2)
                                op("dve", lambda e: e.tensor_copy(tf[:], tb[:]), reads=[Bt], writes=[Bt2])
                                toks.append(dump(dbg[hd * P:(hd + 1) * P, q4 * 1024:(q4 + 1) * 1024], tf[:], [Bt2]))
                        for i in range(4):
                            for q4 in range(4):
                                op("dve", lambda e, i=i, q4=q4: e.tensor_copy(tf[:], QKT[:, i, q4 * 1024:(q4 + 1) * 1024]), reads=BQK, writes=[Bt2])
                                toks.append(dump(dbg[(2 + i) * P:(3 + i) * P, q4 * 1024:(q4 + 1) * 1024], tf[:], [Bt2]))
                        S.ops["sp"].append((tuple(toks), None, None))
                        S.emit()
                        return nc
                    S.emit()
                k.pstack = ph
        k.pstack = hst
        Wa = k.sb("Wa", [P, 8, D], BF16)
        Wm = k.sb("Wm", [P, 8, D], BF16)
        Wo = k.sb("Wo", [P, 8, D], BF16)
        BWa, BWm, BWo = Buf("Wa"), Buf("Wm"), Buf("Wo")
        with contextlib.ExitStack() as ph:
            k.pstack = ph
            W5 = k.sb("W5", [P, 8, 3, 256], BF16)
            Wqk = [k.sb(f"Wqk{i}", [P, 8, 2, 256], BF16) for i in range(2)]
            BWqk = [[Buf(f"Wqk{i}_{j}") for j in range(2)] for i in range(2)]
            wqksem = [[S.dma_sem(f"wqk{i}_{j}") for j in range(2)] for i in range(2)]

            def load_qk(mh_):
                i = mh_ % 2
                for j, col in enumerate((COL_QM, COL_KM)):
                    op("pool", lambda e, j=j, col=col: e.dma_start(out=Wqk[i][:, :, j, :], in_=w_in_r[:, :, col + mh_ * 256:col + (mh_ + 1) * 256]), writes=[BWqk[i][j]], dma=wqksem[i][j])
            BW5s = [Buf(f"W5_{j}") for j in range(5)]
            w5sems = [S.dma_sem(f"w5_{j}") for j in range(5)]
            Cst = k.sb("Cst", [P, 2, 258], F32)
            Cbf = k.sb("Cbf", [P, 2, 258], BF16)
            BC = Buf("Cst")
            BCbf = Buf("Cbf")
            pre = [[k.sb(f"pre{w}{dc}", [P, 516], BF16) for dc in range(2)] for w in range(2)]
            Bpre = [[Buf(f"pre{w}{dc}") for dc in range(2)] for w in range(2)]
            QmTr = Ring(k, "QmT", 2, [P, 2, 512], BF16)
            KmTr = Ring(k, "KmT", 2, [P, 2, 512], BF16)
            Ktokr = Ring(k, "Ktok", 2, [P, 4, 256], BF16)
            Vpr = Ring(k, "Vp", 2, [P, 258], BF16)
            Smr = Ring(k, "Sm", 2, [P, P], BF16)
            thozr = Ring(k, "thoz", 2, [P, 256], F32)
            dnr = Ring(k, "dn", 2, [P, 1], F32)
            hm2r = Ring(k, "hm2", 2, [P, 4, 256], F32)
            szr = Ring(k, "sz", 2, [P, 4, 256], F32)
            ssqmr = Ring(k, "ssqm", 2, [P, 4], F32)
            junk2 = k.sb("junk2", [P, 256], F32)
            Bjunk2 = Buf("junk2")
            t1r = Ring(k, "t1", 1, [P, 256], F32)
            ymgr = Ring(k, "ymg", 2, [P, 256], BF16)
            ymTsr = Ring(k, "ymTs", 2, [P, 2, 512], BF16, dma=True)
            bQK = Ring(k, "bQK", 2, [P, 512], F32, psum=True)
            bT = k.psum("bT", [P, 1024], BF16)
            BbT = Buf("bT", psum=True)
            bT2 = bT
            BbT2 = BbT
            diag = k.sb("diag", [P, 16, P], BF16)
            Bdiag = Buf("diag")
            caus_s = k.sb("caus_s", [P, P], F32)
            op("dve", lambda e: e.tensor_scalar(caus_s[:], caus[:], ML_KSCALE, None, ALU.mult), reads=[Bc], writes=[Bdiag])
            bV = k.psum("bV", [P, 512], F32)
            BbV = Buf("bV", psum=True)
            bOZ = k.psum("bOZ", [P, 512], F32)
            BbOZ = Buf("bOZ", psum=True)
            bOut = k.psum("bOut", [P, 512], F32)
            BbOut = Buf("bOut", psum=True)
            bCp = [k.psum(f"bC{i}", [P, 512], F32) for i in range(2)]
            BbCp = [Buf(f"bC{i}", psum=True) for i in range(2)]
            Bym = Buf("ymT_d")
            for mh in range(4):
                if mh == 0:
                    load_qk(0)
                for j, col in enumerate((COL_QM, COL_KM, COL_VM, COL_ZM, COL_OM)):
                    if j < 2:
                        continue
                    op("pool", lambda e, j=j, col=col, mh=mh: e.dma_start(out=W5[:, :, j - 2, :], in_=w_in_r[:, :, col + mh * 256:col + (mh + 1) * 256]), writes=[BW5s[j]], dma=w5sems[j])
                if mh + 1 < 4:
                    load_qk(mh + 1)
                if mh == 0:
                    for W_, Wb_, src in ((Wa, BWa, w_att), (Wm, BWm, w_ml), (Wo, BWo, w_out)):
                        sm_ = S.dma_sem(f"w5_{k.uid()}")
                        op("pool", lambda e, W_=W_, src=src: e.dma_start(out=W_[:], in_=src.rearrange("(kc p) n -> p kc n", p=P)), writes=[Wb_], dma=sm_)
                if mh == 2:
                    op("dve", lambda e: e.tensor_scalar(Wa[:], Wa[:], 0.5, None, ALU.mult), reads=[BWa], writes=[BWa])
                op("dve", lambda e: e.memset(Cst[:], 0.0), writes=[BC])
                for w in range(2):
                    for dc in range(2):
                        op("dve", lambda e, w=w, dc=dc: e.memset(pre[w][dc][:, 0:3], 0.0), writes=[Bpre[w][dc]])
                for w in range(2):
                    for dc in range(2):
                        for j in range(4):
                            op("dve", lambda e, w=w, dc=dc, j=j, mh=mh: e.tensor_scalar(diag[:, (w * 2 + dc) * 4 + j, :], ident[:], convw_sb[:, w * 8 + mh * 2 + dc, j:j + 1], None, ALU.mult),
                               reads=[Bc], writes=[Bdiag])

                def qk_piece(g, w, dc, dstT, dstb, mh=mh):
                    bq, bqb, _ = bQK.next()
                    for kc in range(8):
                        op("pe", lambda e, bq=bq, kc=kc: e.matmul(bq[:], lhsT=Wqk[mh % 2][:, kc, w, dc * P:(dc + 1) * P], rhs=hT[:, kc, g * GS:(g + 1) * GS], start=(kc == 0), stop=(kc == 7)),
                           reads=[BWqk[mh % 2][w], hTb[g]], writes=[bqb])
                    pr = pre[w][dc]
                    prb = Bpre[w][dc]
                    cidx = w * 8 + mh * 2 + dc
                    didx = (w * 2 + dc) * 4
                    op("act", lambda e: e.activation(pr[:, 3:515], bq[:], AF.Copy), reads=[bqb], writes=[prb])

                    def part2():
                        bcv, bcvb, _ = bQK.next()
                        for j in range(4):
                            op("pe", lambda e, j=j: e.matmul(bcv[:], lhsT=diag[:, didx + j, :], rhs=pr[:, j:j + 512], start=(j == 0), stop=(j == 3)), reads=[Bdiag, prb], writes=[bcvb])
                        op("pool", lambda e: e.tensor_copy(pr[:, 0:3], pr[:, 512:515]), reads=[prb], writes=[prb])
                        op("act", lambda e: e.activation(dstT[:, dc, :], bcv[:], AF.Silu, bias=convb_sb[:, cidx:cidx + 1], scale=1.0), reads=[bcvb, Bc], writes=[dstb])
                    return part2

                pieces = [(w, dc) for w in range(2) for dc in range(2)]
                nxt = (QmTr.next(), KmTr.next())
                for (w, dc) in pieces:
                    dd = nxt[w]
                    qk_piece(0, w, dc, dd[0], dd[1])()

                def gend_dve(gs, c4, mh=mh):
                    hm2, hm2b, sz, szb, ssqm, ssqmb = gs["bufs"]
                    t1, t1b, _ = t1r.next()
                    ymg, ymgb, _ = ymgr.next()
                    op("dve", lambda e: e.scalar_tensor_tensor(out=t1[:], in0=hm2[:, c4, :], scalar=ssqm[:, c4:c4 + 1], in1=mlw_bc[:, mh * 256:(mh + 1) * 256], op0=ALU.mult, op1=ALU.mult),
                       reads=[hm2b, ssqmb, Bc], writes=[t1b])
                    op("pool", lambda e: e.tensor_tensor(ymg[:], t1[:], sz[:, c4, :], ALU.mult), reads=[t1b, szb], writes=[ymgb])
                    gs.setdefault("ymgs", {})[c4] = (ymg, ymgb)

                def gend_sqrt(gs):
                    hm2, hm2b, sz, szb, ssqm, ssqmb = gs["bufs"]
                    op("act", lambda e: e.activation(ssqm[:], ssqm[:], AF.Sqrt, bias=EPS, scale=1.0 / 256.0), reads=[ssqmb], writes=[ssqmb])
                    op("dve", lambda e: e.reciprocal(ssqm[:], ssqm[:]), reads=[ssqmb], writes=[ssqmb])

                def gend_pe(gs, c4, mh=mh):
                    ymg, ymgb = gs["ymgs"][c4]
                    g_ = gs["g"]
                    for dc in range(2):
                        op("pe", lambda e, dc=dc: e.transpose(bT2[:, (dc * 4 + c4) * P:(dc * 4 + c4 + 1) * P], ymg[:, dc * P:(dc + 1) * P], ident_bf[:]), reads=[ymgb, Bc], writes=[BbT2])
                    if c4 == 3:
                        ymTs, ymTsb, ymTss = ymTsr.next()
                        op("act", lambda e: e.activation(ymTs[:], bT2[:, 0:1024].rearrange("p (a b) -> p a b", b=512), AF.Copy), reads=[BbT2], writes=[ymTsb])
                        S.dma_batch("sp", ymTss, [((lambda e, dc=dc: e.dma_start(out=ymT_d[mh * 2 + dc, :, g_ * GS:(g_ + 1) * GS], in_=ymTs[:, dc, :])), [ymTsb], [Bym])
                                                  for dc in range(2)])

                prevg = None
                pend_tail = []
                for g in range(NG):
                    (QmT, QmTb, _), (KmT, KmTb, _) = nxt
                    if g + 1 < NG:
                        nxt = (QmTr.next(), KmTr.next())
                    Ktok, Ktokb, _ = Ktokr.next()
                    for c4 in range(4):
                        for dc in range(2):
                            op("pe", lambda e, KmT=KmT, c4=c4, dc=dc: e.transpose(bT[:, c4 * 256 + dc * P:c4 * 256 + (dc + 1) * P], KmT[:, dc, c4 * P:(c4 + 1) * P], ident_bf[:]),
                               reads=[KmTb, Bc], writes=[BbT])
                    op("act", lambda e, Ktok=Ktok: e.activation(Ktok[:], bT[:, 0:1024].rearrange("p (a b) -> p a b", b=256), AF.Copy), reads=[BbT], writes=[Ktokb])
                    hm2, hm2b, _ = hm2r.next()
                    sz, szb, _ = szr.next()
                    ssqm, ssqmb, _ = ssqmr.next()
                    curg = {"g": g, "bufs": (hm2, hm2b, sz, szb, ssqm, ssqmb)}
                    for c4 in range(4):
                        ci = g * 4 + c4
                        t0 = ci * P
                        col = ci * 4 + mh
                        part2 = None
                        if g + 1 < NG:
                            w, dc = pieces[c4]
                            dd = nxt[w]
                            part2 = qk_piece(g + 1, w, dc, dd[0], dd[1])
                        while pend_tail:
                            pend_tail.pop(0)()
                        if prevg is not None:
                            if c4 == 1:
                                gend_dve(prevg, 0)
                            if c4 >= 1:
                                gend_dve(prevg, c4)
                        for dc in range(2):
                            op("pe", lambda e, KmT=KmT, QmT=QmT, dc=dc, c4=c4: e.matmul(bV[:, 256:384], lhsT=KmT[:, dc, c4 * P:(c4 + 1) * P], rhs=QmT[:, dc, c4 * P:(c4 + 1) * P],
                                                                                     start=(dc == 0), stop=(dc == 1)),
                               reads=[KmTb, QmTb], writes=[BbV])
                        for kc in range(8):
                            op("pe", lambda e, kc=kc, t0=t0: e.matmul(bV[:, 0:256], lhsT=hT[:, kc, t0:t0 + P], rhs=W5[:, kc, 0, :], start=(kc == 0), stop=(kc == 7)),
                               reads=[hTb[g], BW5s[2]], writes=[BbV])
                        Sm, Smb, _ = Smr.next()
                        op("dve", lambda e, Sm=Sm: e.tensor_tensor(Sm[:], bV[:, 256:384], caus_s[:], ALU.mult), reads=[BbV, Bdiag], writes=[Smb])
                        Vp, Vpb, _ = Vpr.next()
                        op("act", lambda e, Vp=Vp, col=col: e.activation(Vp[:, 0:256], bV[:, 0:256], AF.Copy, scale=wtok[:, col:col + 1]), reads=[BbV, Bml], writes=[Vpb])
                        op("pool", lambda e, Vp=Vp, col=col: e.tensor_copy(Vp[:, 256:258], wtok[:, col:col + 1].to_broadcast([P, 2])), reads=[Bml], writes=[Vpb])
                        if ci > 0:
                            op("pool", lambda e, col=col: e.tensor_scalar(Cbf[:], Cst[:], decbc[:, col:col + 1], ML_KSCALE, ALU.mult, ALU.mult), reads=[BC, Bml], writes=[BCbf])
                        if part2 is not None:
                            part2()
                        for jj, wi in enumerate((4, 3)):
                            for kc in range(8):
                                op("pe", lambda e, kc=kc, t0=t0, jj=jj, wi=wi: e.matmul(bOZ[:, jj * 256:(jj + 1) * 256], lhsT=hT[:, kc, t0:t0 + P], rhs=W5[:, kc, wi - 2, :], start=(kc == 0), stop=(kc == 7)),
                                   reads=[hTb[g], BW5s[wi]], writes=[BbOZ])
                        thoz, thozb, _ = thozr.next()
                        op("act", lambda e, thoz=thoz: e.activation(thoz[:, 0:256], bOZ[:, 0:256], AF.Tanh, scale=0.5), reads=[BbOZ], writes=[thozb])
                        op("act", lambda e, sz=sz, c4=c4: e.activation(sz[:, c4, :], bOZ[:, 256:512], AF.Silu), reads=[BbOZ], writes=[szb])
                        op("pool", lambda e, thoz=thoz: e.tensor_scalar(thoz[:, 0:256], thoz[:, 0:256], 0.5, 0.5, ALU.mult, ALU.add), reads=[thozb], writes=[thozb])
                        if ci > 0:
                            for dc in range(2):
                                op("pe", lambda e, QmT=QmT, dc=dc, c4=c4: e.matmul(bOut[:, 0:257], lhsT=QmT[:, dc, c4 * P:(c4 + 1) * P], rhs=Cbf[:, dc, 0:257], start=(dc == 0), stop=False),
                                   reads=[QmTb, BCbf], writes=[BbOut])
                        op("pe", lambda e, Sm=Sm, Vp=Vp, ci=ci: e.matmul(bOut[:, 0:257], lhsT=Sm[:], rhs=Vp[:, 0:257], start=(ci == 0), stop=True), reads=[Smb, Vpb], writes=[BbOut])
                        for dkc in range(2):
                            op("pe", lambda e, Ktok=Ktok, c4=c4, dkc=dkc, Vp=Vp: e.matmul(bCp[dkc][:, 0:257], lhsT=Ktok[:, c4, dkc * P:(dkc + 1) * P], rhs=Vp[:, 0:257], start=True, stop=True),
                               reads=[Ktokb, Vpb], writes=[BbCp[dkc]])
                            op("dve", lambda e, dkc=dkc, col=col: e.scalar_tensor_tensor(out=Cst[:, dkc, 0:257], in0=Cst[:, dkc, 0:257], scalar=decbc[:, col:col + 1], in1=bCp[dkc][:, 0:257],
                                                                                      op0=ALU.mult, op1=ALU.add),
                               reads=[BC, Bml, BbCp[dkc]], writes=[BC])
                        def tail(hm2=hm2, hm2b=hm2b, ssqm=ssqm, ssqmb=ssqmb, c4=c4, col=col, thoz=thoz, thozb=thozb):
                            dn, dnb, _ = dnr.next()
                            op("dve", lambda e: e.tensor_scalar(dn[:], bOut[:, 256:257], -1.0, cltok[:, col:col + 1], ALU.mult, ALU.max), reads=[BbOut, Bml], writes=[dnb])
                            op("dve", lambda e: e.tensor_tensor(dn[:], dn[:], bOut[:, 256:257], ALU.max), reads=[BbOut, dnb], writes=[dnb])
                            op("dve", lambda e: e.reciprocal(dn[:], dn[:]), reads=[dnb], writes=[dnb])
                            op("dve", lambda e: e.scalar_tensor_tensor(out=hm2[:, c4, :], in0=bOut[:, 0:256], scalar=dn[:, 0:1], in1=thoz[:, 0:256], op0=ALU.mult, op1=ALU.mult),
                               reads=[BbOut, dnb, thozb], writes=[hm2b])
                            op("act", lambda e: e.activation(junk2[:], hm2[:, c4, :], AF.Square, accum_out=ssqm[:, c4:c4 + 1]), reads=[hm2b], writes=[Bjunk2, ssqmb])
                        pend_tail.append(tail)
                        if prevg is not None:
                            if c4 == 0:
                                gend_sqrt(prevg)
                            else:
                                gend_pe(prevg, c4 - 1)
                                if c4 == 3:
                                    gend_pe(prevg, 3)
                    prevg = curg
                while pend_tail:
                    pend_tail.pop(0)()
                gend_sqrt(prevg)
                for c4 in range(4):
                    gend_dve(prevg, c4)
                    gend_pe(prevg, c4)
                if stop_after == "ph4" and mh == 0:
                    S.barrier()
                    toks = []
                    tb = k.sb("tmpd", [P, 1024], BF16)
                    tf = k.sb("tmpf", [P, 1024], F32)
                    Bt = Buf("tmpd")
                    Bt2 = Buf("tmpf")
                    ds2 = S.dma_sem("dbg2")
                    for kc in range(2):
                        for q4 in range(4):
                            op("sp", lambda e, kc=kc, q4=q4: e.dma_start(out=tb[:], in_=ymT_d[kc, :, q4 * 1024:(q4 + 1) * 1024]), reads=[Bym], writes=[Bt], dma=ds2)
                            op("dve", lambda e: e.tensor_copy(tf[:], tb[:]), reads=[Bt], writes=[Bt2])
                            toks.append(dump(dbg[kc * P:(kc + 1) * P, q4 * 1024:(q4 + 1) * 1024], tf[:], [Bt2]))
                    S.ops["sp"].append((tuple(toks), None, None))
                    S.emit()
                    return nc
            S.barrier()
            S.emit()
        with contextlib.ExitStack() as ph:
            k.pstack = ph
            yaTr = Ring(k, "yaTg", 2, [P, 8, GS], BF16, dma=True)
            ymTr = Ring(k, "ymTg", 2, [P, 8, GS], BF16, dma=True)
            sgar = Ring(k, "sga", 3, [P, GS], F32, dma=True)
            sgmr = Ring(k, "sgm", 3, [P, GS], F32, dma=True)
            y1r = Ring(k, "y1", 2, [P, GS], F32)
            y2r = Ring(k, "y2", 2, [P, GS], F32)
            yTr = Ring(k, "yT", 1, [P, 8, GS], BF16)
            xr5 = Ring(k, "x5", 2, [P, D], F32, dma=True)
            otr = Ring(k, "ot", 2, [P, D], F32, dma=True)
            bA = Ring(k, "bA", 3, [P, 512], F32, psum=True)
            bM = Ring(k, "bM", 3, [P, 512], F32, psum=True)
            bF = Ring(k, "bF", 2, [P, 512], F32, psum=True)
            def load_branch(g):
                yaTg, yaTgb, yas = yaTr.next()
                ymTg, ymTgb, yms = ymTr.next()
                op("sp", lambda e: e.dma_start(out=yaTg[:], in_=yaT_d[:, :, g * GS:(g + 1) * GS].rearrange("k p t -> p k t")), writes=[yaTgb], dma=yas)
                op("sp", lambda e: e.dma_start(out=ymTg[:], in_=ymT_d[:, :, g * GS:(g + 1) * GS].rearrange("k p t -> p k t")), writes=[ymTgb], dma=yms)
                return yaTg, yaTgb, ymTg, ymTgb

            nxt_br = load_branch(0)
            for g in range(NG):
                yaTg, yaTgb, ymTg, ymTgb = nxt_br
                if g + 1 < NG:
                    nxt_br = load_branch(g + 1)
                yT, yTb, _ = yTr.next()
                for cc in range(8):
                    sga, sgab, sgas = sgar.next()
                    sgm, sgmb, sgms = sgmr.next()
                    op("sp", lambda e, sga=sga, cc=cc, g=g: e.dma_start(out=sga[:], in_=sga_d[cc, :, g * GS:(g + 1) * GS]), writes=[sgab], dma=sgas)
                    op("sp", lambda e, sgm=sgm, cc=cc, g=g: e.dma_start(out=sgm[:], in_=sgm_d[cc, :, g * GS:(g + 1) * GS]), writes=[sgmb], dma=sgms)
                    ba, bab, _ = bA.next()
                    bm, bmb, _ = bM.next()
                    for kc in range(8):
                        op("pe", lambda e, ba=ba, kc=kc, cc=cc, yaTg=yaTg: e.matmul(ba[:], lhsT=Wa[:, kc, cc * P:(cc + 1) * P], rhs=yaTg[:, kc, :], start=(kc == 0), stop=(kc == 7)),
                           reads=[BWa, yaTgb], writes=[bab])
                    for kc in range(8):
                        op("pe", lambda e, bm=bm, kc=kc, cc=cc, ymTg=ymTg: e.matmul(bm[:], lhsT=Wm[:, kc, cc * P:(cc + 1) * P], rhs=ymTg[:, kc, :], start=(kc == 0), stop=(kc == 7)),
                           reads=[BWm, ymTgb], writes=[bmb])
                    y1, y1b, _ = y1r.next()
                    y2, y2b, _ = y2r.next()
                    op("dve", lambda e, y1=y1, sga=sga, ba=ba: e.scalar_tensor_tensor(out=y1[:], in0=sga[:], scalar=1.0, in1=ba[:], op0=ALU.add, op1=ALU.mult), reads=[sgab, bab], writes=[y1b])
                    op("dve", lambda e, y2=y2, sgm=sgm, bm=bm: e.scalar_tensor_tensor(out=y2[:], in0=sgm[:], scalar=1.0, in1=bm[:], op0=ALU.add, op1=ALU.mult), reads=[sgmb, bmb], writes=[y2b])
                    op("pool", lambda e, yT=yT, cc=cc, y1=y1, y2=y2: e.tensor_tensor(yT[:, cc, :], y1[:], y2[:], ALU.add), reads=[y1b, y2b], writes=[yTb])
                for tt in range(4):
                    ti = g * 4 + tt
                    xt, xb, xs = xr5.next()
                    ot, otb, ots = otr.next()
                    op("sp", lambda e, xt=xt, ti=ti: e.dma_start(out=xt[:], in_=x[ti * P:(ti + 1) * P, :]), writes=[xb], dma=xs)
                    for og in range(2):
                        bf_, bfb, _ = bF.next()
                        for cc in range(8):
                            op("pe", lambda e, bf_=bf_, cc=cc, tt=tt, og=og, yT=yT: e.matmul(bf_[:], lhsT=yT[:, cc, tt * P:(tt + 1) * P], rhs=Wo[:, cc, og * 512:(og + 1) * 512], start=(cc == 0), stop=(cc == 7)),
                               reads=[yTb, BWo], writes=[bfb])
                        op("dve", lambda e, ot=ot, bf_=bf_, og=og: e.tensor_tensor(ot[:, og * 512:(og + 1) * 512], bf_[:], gate_half[:, og * 512:(og + 1) * 512], ALU.mult), reads=[bfb, Bgate], writes=[otb])
                    op("pool", lambda e, ot=ot, xt=xt: e.tensor_tensor(ot[:], ot[:], xt[:], ALU.add), reads=[otb, xb], writes=[otb])
                    op("act", lambda e, ot=ot, ti=ti: e.dma_start(out=out[ti * P:(ti + 1) * P, :], in_=ot[:]), reads=[otb], dma=ots)
            S.barrier()
            S.emit()
    return nc


def _host_inputs(inputs):
    f = np.float32
    x = np.ascontiguousarray(inputs["x"], dtype=f)
    c = np.asarray(inputs["c"], dtype=f)
    rel_bias = np.asarray(inputs["rel_bias"], dtype=f)
    dist = np.arange(0, 1024)
    max_exact = 16
    nf = np.maximum(dist, 1).astype(np.float32)
    large = max_exact + (np.log(nf / max_exact) / np.log(128 / max_exact) * (32 - max_exact)).astype(np.int32)
    large = np.minimum(large, 31)
    bucket = np.where(dist < max_exact, dist, large)
    kk = np.arange(128)[:, None]
    jj = np.arange(1024)[None, :]
    dd = jj - 384 - kk
    valid = dd >= 0
    bidx = bucket[np.clip(dd, 0, 1023)]
    utab = np.empty((8, 128, 1024), dtype=f)
    for h in range(8):
        g = rel_bias[:, h][bidx]
        utab[h] = np.where(valid, g, f(NEGV))
    conv_w = np.asarray(inputs["conv_w"], dtype=f)[0]
    conv_b = np.asarray(inputs["conv_b"], dtype=f)[0]
    convw = np.ascontiguousarray(conv_w.T.reshape(16, 128, 4).transpose(1, 0, 2))
    convb = np.ascontiguousarray(conv_b.reshape(16, 128).T)
    ident = np.eye(128, dtype=f)
    caus = np.triu(np.ones((128, 128), dtype=f))
    ind = np.zeros((16, 16, 128), dtype=f)
    for b in range(16):
        ind[b, b, :] = 1.0
    common = {
        "w_ada": np.ascontiguousarray(inputs["w_ada"][0], dtype=f),
        "b_ada": np.ascontiguousarray(inputs["b_ada"][0:1], dtype=f),
        "norm_w": np.ascontiguousarray(inputs["norm_w"][0:1], dtype=f),
        "w_in": np.ascontiguousarray(inputs["w_in"][0], dtype=f),
        "qnw": np.ascontiguousarray(inputs["q_norm_w"][0:1], dtype=f),
        "knw": np.ascontiguousarray(inputs["k_norm_w"][0:1], dtype=f),
        "utab": utab,
        "bias31": np.ascontiguousarray(rel_bias[31:32, :]),
        "convw": convw,
        "convb": convb,
        "b_ig": np.ascontiguousarray(np.asarray(inputs["b_igate"], dtype=f)[0].reshape(4, 1)),
        "b_fg": np.ascontiguousarray(np.asarray(inputs["b_fgate"], dtype=f)[0].reshape(4, 1)),
        "mlnw": np.ascontiguousarray(inputs["ml_norm_w"][0:1], dtype=f),
        "w_att": np.ascontiguousarray(inputs["w_att_proj"][0], dtype=f),
        "w_ml": np.ascontiguousarray(inputs["w_ml_proj"][0], dtype=f),
        "w_out": np.ascontiguousarray(inputs["w_out"][0], dtype=f),
        "c_ident": ident,
        "c_caus": caus,
        "c_ind": ind.reshape(16, 16 * 128),
    }
    maps = []
    for b in range(x.shape[0]):
        m = dict(common)
        m["x"] = x[b]
        m["ccol"] = np.ascontiguousarray(c[b].reshape(8, 128).T)
        maps.append(m)
    return maps


def kernel(**inputs):
    maps = _host_inputs(inputs)
    nc = build()
    res = run_bass_kernel_spmd(nc, maps, core_ids=list(range(8)))
    return np.stack([np.asarray(r["out"], dtype=np.float32) for r in res.results], axis=0)
```

```python
import contextlib
import numpy as np
import ml_dtypes
import concourse.bass as bass
import concourse.mybir as mybir
from concourse.bass_utils import run_bass_kernel_spmd

F32 = mybir.dt.float32
BF16 = mybir.dt.bfloat16
AF = mybir.ActivationFunctionType
ALU = mybir.AluOpType
AX = mybir.AxisListType

T = 4096
D = 1024
P = 128
NKC = 8
NG = 8
GS = 512
EPS = 1e-6
NEGV = -30000.0
ATT_SCALE = 128.0 ** -0.5
ML_KSCALE = 256.0 ** -0.5
COL_QA, COL_KA, COL_VA, COL_ZA = 0, 1024, 2048, 3072
COL_QM, COL_KM, COL_VM, COL_ZM, COL_OM = 4096, 5120, 6144, 7168, 8192
COL_IG, COL_FG, COL_GA, COL_GM = 9216, 9220, 9224, 10248
D_IN = 11272


class Buf:
    __slots__ = ("w", "r", "name", "psum")

    def __init__(self, name="", psum=False):
        self.w = None
        self.r = {}
        self.name = name
        self.psum = psum


class Sched:
    ENGS = ("pe", "act", "dve", "pool", "sp")

    def __init__(self, nc, stack):
        self.nc = nc
        self.stack = stack
        self.ops = {e: [] for e in self.ENGS}
        self.cnt = {e: 0 for e in self.ENGS}
        self.waited = {e: {} for e in self.ENGS}
        self.sems = {}
        self.dcnt = {}
        for e in self.ENGS:
            self.sems[e] = stack.enter_context(nc.semaphore("s_" + e))

    def dma_sem(self, name):
        self.sems[name] = self.stack.enter_context(self.nc.semaphore("d_" + name))
        self.dcnt[name] = 0
        return name

    def dma_batch(self, eng, sem, items):
        n = len(items)
        toks = []
        for i, (fn, reads, writes) in enumerate(items):
            toks.append(self.op(eng, fn, reads=reads, writes=writes, dma=sem, dma_extra=n - 1 - i))
        return toks

    def op(self, eng, fn, reads=(), writes=(), dma=None, dma_extra=0):
        waits = {}
        wd = self.waited[eng]

        def need(dep):
            if dep is None:
                return
            sk, val = dep
            if sk == "pe" and eng == "pe":
                return
            if dma is not None and sk == dma and val > self.dcnt[dma]:
                return
            if wd.get(sk, 0) >= val:
                return
            if waits.get(sk, 0) < val:
                waits[sk] = val

        for b in reads:
            need(b.w)
            if b.psum:
                for sk, v in b.r.items():
                    if sk != eng:
                        need((sk, v))
        for b in writes:
            need(b.w)
            for sk, v in b.r.items():
                need((sk, v))
        for sk, v in waits.items():
            wd[sk] = v
        if dma is None:
            self.cnt[eng] += 1
            tok = (eng, self.cnt[eng])
            inc = (eng, 1)
        else:
            self.dcnt[dma] += 16
            tok = (dma, self.dcnt[dma] + 16 * dma_extra)
            inc = (dma, 16)
        for b in writes:
            b.w = tok
            b.r = {}
        for b in reads:
            if b.r.get(tok[0], 0) < tok[1]:
                b.r[tok[0]] = tok[1]
        self.ops[eng].append((tuple(waits.items()), fn, inc))
        return tok

    def barrier(self):
        toks = [(e, self.cnt[e]) for e in self.ENGS if self.cnt[e] > 0]
        toks += [(k, v) for k, v in self.dcnt.items() if v > 0]
        for e in self.ENGS:
            w = []
            for sk, v in toks:
                if sk == e:
                    continue
                if self.waited[e].get(sk, 0) < v:
                    self.waited[e][sk] = v
                    w.append((sk, v))
            self.ops[e].append((tuple(w), None, None))

    def emit(self):
        nc = self.nc
        sems = self.sems
        ops_now = self.ops
        self.ops = {e: [] for e in self.ENGS}

        def replay(ename, eng):
            for waits, fn, inc in ops_now[ename]:
                for sk, v in waits:
                    eng.wait_ge(sems[sk], v)
                if fn is None:
                    continue
                ins = fn(eng)
                ins.then_inc(sems[inc[0]], inc[1])

        with nc.Block() as block:
            @block.tensor
            def _(e):
                replay("pe", e)

            @block.scalar
            def _(e):
                replay("act", e)

            @block.vector
            def _(e):
                replay("dve", e)

            @block.gpsimd
            def _(e):
                replay("pool", e)

            @block.sync
            def _(e):
                replay("sp", e)


class Ring:
    def __init__(self, K, name, n, shape, dt, psum=False, dma=False):
        self.n = n
        self.t = []
        self.b = []
        self.s = []
        for i in range(n):
            if psum:
                self.t.append(K.psum(f"{name}{i}", shape, dt))
            else:
                self.t.append(K.sb(f"{name}{i}", shape, dt))
            self.b.append(Buf(f"{name}{i}", psum=psum))
            self.s.append(K.S.dma_sem(f"{name}{i}_{K.uid()}") if dma else None)
        self.i = 0

    def next(self):
        i = self.i % self.n
        self.i += 1
        return self.t[i], self.b[i], self.s[i]


class K:
    def __init__(self, nc, S, stack):
        self.nc = nc
        self.S = S
        self.stack = stack
        self.pstack = stack
        self._uid = 0

    def uid(self):
        self._uid += 1
        return self._uid

    def sb(self, name, shape, dt):
        return self.pstack.enter_context(self.nc.sbuf_tensor(f"{name}_{self.uid()}", shape, dt))

    def psum(self, name, shape, dt):
        return self.pstack.enter_context(self.nc.psum_tensor(f"{name}_{self.uid()}", shape, dt))


def build(stop_after=None, dbg_shape=None):
    nc = bass.Bass("TRN2", target_bir_lowering=False)

    def din(name, shape, dt=F32):
        return nc.dram_tensor(name, shape, dt, kind="ExternalInput").ap()

    x = din("x", [T, D])
    ccol = din("ccol", [P, 8])
    w_ada = din("w_ada", [D, 3 * D])
    b_ada = din("b_ada", [1, 3 * D])
    norm_w = din("norm_w", [1, D])
    w_in = din("w_in", [D, D_IN])
    qnw = din("qnw", [1, P])
    knw = din("knw", [1, P])
    utab = din("utab", [8, P, 1024])
    bias31 = din("bias31", [1, 8])
    convw = din("convw", [P, 16, 4])
    convb = din("convb", [P, 16])
    b_ig = din("b_ig", [4, 1])
    b_fg = din("b_fg", [4, 1])
    mlnw = din("mlnw", [1, D])
    w_att = din("w_att", [D, D])
    w_ml = din("w_ml", [D, D])
    w_out = din("w_out", [D, D])
    c_ident = din("c_ident", [P, P])
    c_caus = din("c_caus", [P, P])
    c_ind = din("c_ind", [16, 16 * P])
    out = nc.dram_tensor("out", [T, D], F32, kind="ExternalOutput").ap()
    dbg = None
    if dbg_shape is not None:
        dbg = nc.dram_tensor("dbg", list(dbg_shape), F32, kind="ExternalOutput").ap()
    yaT_d = nc.dram_tensor("yaT_d", [8, P, T], BF16, kind="Internal").ap()
    ymT_d = nc.dram_tensor("ymT_d", [8, P, T], BF16, kind="Internal").ap()
    sga_d = nc.dram_tensor("sga_d", [8, P, T], F32, kind="Internal").ap()
    sgm_d = nc.dram_tensor("sgm_d", [8, P, T], F32, kind="Internal").ap()

    w_in_r = w_in.rearrange("(kc p) n -> p kc n", p=P)

    with contextlib.ExitStack() as st:
        S = Sched(nc, st)
        k = K(nc, S, st)
        op = S.op
        ident = k.sb("ident", [P, P], F32)
        ident_bf = k.sb("ident_bf", [P, P], BF16)
        ones_bf = k.sb("ones_bf", [P, P], BF16)
        ones_f = k.sb("ones_f", [P, P], F32)
        caus = k.sb("caus", [P, P], F32)
        ind_bf = k.sb("ind_bf", [16, 16 * P], BF16)
        gate_half = k.sb("gate_half", [P, D], F32)
        mlw_bc = k.sb("mlw_bc", [P, D], F32)
        qw_bc = k.sb("qw_bc", [P, P], F32)
        kw_bc = k.sb("kw_bc", [P, P], F32)
        b31_bc = k.sb("b31_bc", [P, 8], F32)
        convw_sb = k.sb("convw_sb", [P, 16, 4], F32)
        convb_sb = k.sb("convb_sb", [P, 16], F32)
        wtok = k.sb("wtok", [P, 128], F32)
        cltok = k.sb("cltok", [P, 128], F32)
        decbc = k.sb("decbc", [P, 128], F32)
        Bc = Buf("consts")
        Bgate = Buf("gate_half")
        Bml = Buf("mlgates")
        hst = st.enter_context(contextlib.ExitStack())
        k.pstack = hst
        hT = k.sb("hT", [P, NKC, T], BF16)
        hTb = [Buf(f"hT{g}") for g in range(NG)]
        k.pstack = st
        dsem = S.dma_sem("const")
        dsem_out = S.dma_sem("dbgout")

        cl = [(ident[:], c_ident[:, :]), (caus[:], c_caus[:, :]),
              (mlw_bc[:], mlnw[0:1, :].partition_broadcast(P)), (qw_bc[:], qnw[0:1, :].partition_broadcast(P)),
              (kw_bc[:], knw[0:1, :].partition_broadcast(P)), (b31_bc[:], bias31[0:1, :].partition_broadcast(P)),
              (convw_sb[:], convw[:, :, :]), (convb_sb[:], convb[:, :])]
        S.dma_batch("sp", dsem, [((lambda e, d_=d_, s_=s_: e.dma_start(out=d_, in_=s_)), [], [Bc]) for d_, s_ in cl])
        op("dve", lambda e: e.tensor_copy(ident_bf[:], ident[:]), reads=[Bc], writes=[Bc])
        op("pool", lambda e: e.dma_start(out=ind_bf[:], in_=c_ind[:, :]), writes=[Bc], dma=S.dma_sem("cind"))
        op("dve", lambda e: e.memset(ones_bf[:], 1.0), writes=[Bc])
        op("dve", lambda e: e.memset(ones_f[:], 1.0), writes=[Bc])

        def finish(src_fn=None):
            toks = []
            if src_fn is not None:
                toks = src_fn()
            S.ops["sp"].append((tuple(toks), None, None))
            S.emit()

        def dump(dst, src, bufs):
            return op("sp", lambda e: e.dma_start(out=dst, in_=src), reads=bufs, dma=dsem_out)

        with contextlib.ExitStack() as ph:
            k.pstack = ph
            adarow = k.sb("adarow", [P, 3 * D], F32)
            Bada = Buf("adarow")
            ccol_sb = k.sb("ccol_sb", [P, 8], F32)
            cbc = k.sb("cbc", [P, 8, P], F32)
            nwbc = k.sb("nwbc", [P, D], F32)
            Abc = k.sb("Abc", [P, D], F32)
            junk = k.sb("junk", [P, D], F32)
            ssq = k.sb("ssq", [P, 32], F32)
            rs = k.sb("rs", [P, 32], F32)
            Bcc = Buf("ccol")
            Bcbc = Buf("cbc")
            Bnw = Buf("nwbc")
            BA = Buf("Abc")
            Bjunk = Buf("junk")
            wst = Ring(k, "wada", 4, [P, 8, 256], F32, dma=True)
            pa = Ring(k, "pa", 2, [P, 512], F32, psum=True)
            ptr = Ring(k, "ptr", 6, [P, 512], F32, psum=True)
            xr = Ring(k, "xt", 4, [P, D], F32, dma=True)
            xnr = Ring(k, "xn", 4, [P, D], F32)
            op("sp", lambda e: e.dma_start(out=ccol_sb[:], in_=ccol[:, :]), writes=[Bcc], dma=S.dma_sem("ccol"))
            op("sp", lambda e: e.dma_start(out=adarow[:], in_=b_ada[0:1, :].partition_broadcast(P)), writes=[Bada], dma=S.dma_sem("bada"))
            op("sp", lambda e: e.dma_start(out=nwbc[:], in_=norm_w[0:1, :].partition_broadcast(P)), writes=[Bnw], dma=S.dma_sem("nwbc"))
            op("dve", lambda e: e.tensor_copy(cbc[:], ccol_sb[:].unsqueeze(2).to_broadcast([P, 8, P])), reads=[Bcc], writes=[Bcbc])
            w_ada_r = w_ada.rearrange("(kc p) n -> p kc n", p=P)
            xpre = []
            for tt in range(4):
                xt, xb, xs = xr.next()
                op("sp", lambda e, xt=xt, tt=tt: e.dma_start(out=xt[:], in_=x[tt * P:(tt + 1) * P, :]), writes=[xb], dma=xs)
                xpre.append((xt, xb, xs))
            Bsh, Bsc, Bg_ = Buf("ada_shift"), Buf("ada_scale"), Buf("ada_gate")

            def ada_group(cg, bufp):
                wt, wb, ws = wst.next()
                op("sp" if cg % 2 == 0 else "act", lambda e: e.dma_start(out=wt[:], in_=w_ada_r[:, :, cg * 256:(cg + 1) * 256]), writes=[wb], dma=ws)
                pt_, pb, _ = pa.next()
                for kc in range(8):
                    op("pe", lambda e, kc=kc: e.matmul(pt_[:, 0:256], lhsT=cbc[:, kc, :], rhs=wt[:, kc, :], start=(kc == 0), stop=(kc == 7)), reads=[Bcbc, wb], writes=[pb])
                op("dve", lambda e: e.tensor_tensor(adarow[:, cg * 256:(cg + 1) * 256], pt_[:, 0:256], adarow[:, cg * 256:(cg + 1) * 256], ALU.add), reads=[pb, Bada], writes=[bufp])

            for cg in (4, 5, 6, 7):
                ada_group(cg, Bsc)
            for cg in (0, 1, 2, 3):
                ada_group(cg, Bsh)
            op("dve", lambda e: e.scalar_tensor_tensor(out=Abc[:], in0=adarow[:, D:2 * D], scalar=1.0, in1=nwbc[:], op0=ALU.add, op1=ALU.mult),
               reads=[Bsc, Bnw], writes=[BA])

            def emit_gate():
                for cg in (8, 9, 10, 11):
                    ada_group(cg, Bg_)
                op("dve", lambda e: e.tensor_scalar(gate_half[:], adarow[:, 2 * D:3 * D], 0.5, None, ALU.mult), reads=[Bg_], writes=[Bgate])

            Bsst = [Buf(f"ssq{t}") for t in range(32)]
            lagq = []

            def stage2(tt, banks):
                g = tt // 4
                for half, (pt_, pb) in enumerate(banks):
                    op("act", lambda e, pt_=pt_, half=half, tt=tt: e.activation(hT[:, half * 4:half * 4 + 4, tt * P:(tt + 1) * P],
                                                                               pt_[:, 0:512].rearrange("p (a b) -> p a b", b=P), AF.Copy),
                       reads=[pb], writes=[hTb[g]])

            for tt in range(32):
                if tt == 8:
                    emit_gate()
                if tt < 4:
                    xt, xb, xs = xpre[tt]
                else:
                    xt, xb, xs = xr.next()
                    op("sp", lambda e, xt=xt, tt=tt: e.dma_start(out=xt[:], in_=x[tt * P:(tt + 1) * P, :]), writes=[xb], dma=xs)
                op("act", lambda e, xt=xt, tt=tt: e.activation(junk[:], xt[:], AF.Square, accum_out=ssq[:, tt:tt + 1]), reads=[xb], writes=[Bjunk, Bsst[tt]])
                op("act", lambda e, tt=tt: e.activation(rs[:, tt:tt + 1], ssq[:, tt:tt + 1], AF.Sqrt, bias=EPS, scale=1.0 / D), reads=[Bsst[tt]], writes=[Bsst[tt]])
                op("dve", lambda e, tt=tt: e.reciprocal(rs[:, tt:tt + 1], rs[:, tt:tt + 1]), reads=[Bsst[tt]], writes=[Bsst[tt]])
                xn, xnb, _ = xnr.next()
                op("dve", lambda e, xn=xn, xt=xt, tt=tt: e.scalar_tensor_tensor(out=xn[:], in0=xt[:], scalar=rs[:, tt:tt + 1], in1=Abc[:], op0=ALU.mult, op1=ALU.mult),
                   reads=[xb, Bsst[tt], BA], writes=[xnb])
                op("pool", lambda e, xn=xn: e.tensor_tensor(xn[:], xn[:], adarow[:, 0:D], ALU.add), reads=[xnb, Bsh], writes=[xnb])
                banks = []
                for half in range(2):
                    pt_, pb, _ = ptr.next()
                    for j in range(4):
                        kc = half * 4 + j
                        op("pe", lambda e, pt_=pt_, xn=xn, kc=kc, j=j: e.transpose(pt_[:, j * P:(j + 1) * P], xn[:, kc * P:(kc + 1) * P], ident[:]),
                           reads=[xnb, Bc], writes=[pb])
                    banks.append((pt_, pb))
                lagq.append((tt, banks))
                if len(lagq) > 2:
                    stage2(*lagq.pop(0))
            while lagq:
                stage2(*lagq.pop(0))
            S.barrier()
            if stop_after == "ph1":
                tmpf = k.sb("tmpf", [P, T], F32)
                Bt = Buf("tmpf")
                toks = []
                for kc in range(8):
                    op("dve", lambda e, kc=kc: e.tensor_copy(tmpf[:], hT[:, kc, :]), reads=hTb, writes=[Bt])
                    toks.append(dump(dbg[kc * P:(kc + 1) * P, :], tmpf[:], [Bt]))
                toks.append(dump(dbg[8 * P:9 * P, 0:D], gate_half[:], [Bgate]))
                S.ops["sp"].append((tuple(toks), None, None))
                S.emit()
                return nc
            S.emit()
        with contextlib.ExitStack() as ph:
            k.pstack = ph
            Wg = k.sb("Wg", [P, 8, 8], BF16)
            BWg = Buf("Wg")
            wsem = S.dma_sem("wg")
            op("pool", lambda e: e.dma_start(out=Wg[:], in_=w_in_r[:, :, COL_IG:COL_IG + 8]), writes=[BWg], dma=wsem)
            Wga = k.sb("Wga", [P, 8, D], BF16)
            Wgm = k.sb("Wgm", [P, 8, D], BF16)
            BWga = Buf("Wga")
            BWgm = Buf("Wgm")
            wsem2 = S.dma_sem("wga")
            wsem3 = S.dma_sem("wgm")
            op("pool", lambda e: e.dma_start(out=Wga[:], in_=w_in_r[:, :, COL_GA:COL_GA + D]), writes=[BWga], dma=wsem2)
            op("pool", lambda e: e.dma_start(out=Wgm[:], in_=w_in_r[:, :, COL_GM:COL_GM + D]), writes=[BWgm], dma=wsem3)
            bi_sb = k.sb("bi_sb", [4, 1], F32)
            bf_sb = k.sb("bf_sb", [4, 1], F32)
            negbf = k.sb("negbf", [4, 1], F32)
            Bb = Buf("gbias")
            S.dma_batch("sp", S.dma_sem("gb"), [((lambda e: e.dma_start(out=bi_sb[:], in_=b_ig[:, :])), [], [Bb]),
                                                ((lambda e: e.dma_start(out=bf_sb[:], in_=b_fg[:, :])), [], [Bb])])
            op("dve", lambda e: e.tensor_scalar(negbf[:], bf_sb[:], -1.0, None, ALU.mult), reads=[Bb], writes=[Bb])
            bufA = k.sb("bufA", [4, T], F32)
            bufB = k.sb("bufB", [4, T], F32)
            bufC = k.sb("bufC", [4, T], F32)
            BA_, BB_, BC_ = Buf("bufA"), Buf("bufB"), Buf("bufC")
            cm = k.sb("cm", [4, 32], F32)
            Mx = k.sb("Mx", [4, 32], F32)
            Mp = k.sb("Mp", [4, 32], F32)
            dec = k.sb("dec", [4, 32], F32)
            X4 = k.sb("X4", [4, 32, 4], F32)
            Bsm = Buf("gsmall")
            pg = Ring(k, "pg", 2, [P, 512], F32, psum=True)
            ptok = k.psum("ptok", [P, 512], F32)
            Bptok = Buf("ptok", psum=True)
            pdec = k.psum("pdec", [P, 512], F32)
            Bpdec = Buf("pdec", psum=True)
            for g in range(NG):
                pf, pfb, _ = pg.next()
                for kc in range(8):
                    op("pe", lambda e, pf=pf, kc=kc, g=g: e.matmul(pf[0:4, :], lhsT=Wg[:, kc, 4:8], rhs=hT[:, kc, g * GS:(g + 1) * GS], start=(kc == 0), stop=(kc == 7)),
                       reads=[BWg, hTb[g]], writes=[pfb])
                op("act", lambda e, pf=pf, g=g: e.activation(bufA[:, g * GS:(g + 1) * GS], pf[0:4, :], AF.Exp, bias=negbf[:], scale=-1.0), reads=[pfb, Bb], writes=[BA_])
                pi, pib, _ = pg.next()
                for kc in range(8):
                    op("pe", lambda e, pi=pi, kc=kc, g=g: e.matmul(pi[0:4, :], lhsT=Wg[:, kc, 0:4], rhs=hT[:, kc, g * GS:(g + 1) * GS], start=(kc == 0), stop=(kc == 7)),
                       reads=[BWg, hTb[g]], writes=[pib])
                op("act", lambda e, pi=pi, g=g: e.activation(bufC[:, g * GS:(g + 1) * GS], pi[0:4, :], AF.Identity, bias=bi_sb[:]), reads=[pib, Bb], writes=[BC_])
            op("act", lambda e: e.activation(bufA[:], bufA[:], AF.Ln, bias=1.0, scale=1.0), reads=[BA_], writes=[BA_])
            op("dve", lambda e: e.tensor_tensor_scan(bufB[:], bufA[:], bufA[:], 0.0, ALU.add, ALU.max), reads=[BA_], writes=[BB_])
            op("dve", lambda e: e.tensor_tensor(bufC[:], bufC[:], bufB[:], ALU.add), reads=[BC_, BB_], writes=[BC_])
            op("dve", lambda e: e.tensor_reduce(out=cm[:], in_=bufC[:].rearrange("p (c l) -> p c l", l=P), axis=AX.X, op=ALU.max), reads=[BC_], writes=[Bsm])
            op("dve", lambda e: e.tensor_tensor_scan(Mx[:], cm[:], cm[:], 0.0, ALU.max, ALU.max), reads=[Bsm], writes=[Bsm])
            op("dve", lambda e: e.memset(Mp[:, 0:1], 0.0), reads=[Bsm], writes=[Bsm])
            op("dve", lambda e: e.tensor_copy(Mp[:, 1:32], Mx[:, 0:31]), reads=[Bsm], writes=[Bsm])
            op("dve", lambda e: e.tensor_tensor(dec[:], Mp[:], Mx[:], ALU.subtract), reads=[Bsm], writes=[Bsm])
            op("act", lambda e: e.activation(dec[:], dec[:], AF.Exp), reads=[Bsm], writes=[Bsm])
            v3 = lambda t_: t_[:].rearrange("p (c l) -> p c l", l=P)
            Mb = lambda: Mx[:].unsqueeze(2).to_broadcast([4, 32, P])
            op("dve", lambda e: e.tensor_tensor(v3(bufA), v3(bufC), Mb(), ALU.subtract), reads=[BC_, Bsm, BA_], writes=[BA_])
            op("act", lambda e: e.activation(bufA[:], bufA[:], AF.Exp), reads=[BA_], writes=[BA_])
            op("dve", lambda e: e.tensor_tensor(v3(bufB), v3(bufB), Mb(), ALU.subtract), reads=[BB_, Bsm], writes=[BB_])
            op("act", lambda e: e.activation(bufB[:], bufB[:], AF.Exp), reads=[BB_], writes=[BB_])
            for c in range(32):
                op("pe", lambda e, c=c: e.transpose(ptok[:, c * 4:(c + 1) * 4], bufA[0:4, c * P:(c + 1) * P], ident[0:4, 0:4]), reads=[BA_, Bc], writes=[Bptok])
                op("pe", lambda e, c=c: e.transpose(ptok[:, 128 + c * 4:128 + (c + 1) * 4], bufB[0:4, c * P:(c + 1) * P], ident[0:4, 0:4]), reads=[BB_, Bc], writes=[Bptok])
            op("dve", lambda e: e.tensor_copy(wtok[:], ptok[:, 0:128]), reads=[Bptok], writes=[Bml])
            op("dve", lambda e: e.tensor_copy(cltok[:], ptok[:, 128:256]), reads=[Bptok], writes=[Bml])
            op("dve", lambda e: e.tensor_tensor(X4[:], dec[:].unsqueeze(2).to_broadcast([4, 32, 4]), ident[0:4, 0:4].unsqueeze(1).to_broadcast([4, 32, 4]), ALU.mult),
               reads=[Bsm, Bc], writes=[Bsm])
            op("pe", lambda e: e.matmul(pdec[:, 0:128], lhsT=ones_f[0:4, :], rhs=X4[:].rearrange("p c h -> p (c h)"), start=True, stop=True), reads=[Bsm, Bc], writes=[Bpdec])
            op("dve", lambda e: e.tensor_copy(decbc[:], pdec[:, 0:128]), reads=[Bpdec], writes=[Bml])
            pgg = Ring(k, "pgg", 4, [P, 512], F32, psum=True)
            sgr = Ring(k, "sg", 4, [P, 512], F32, dma=True)
            Bsga = Buf("sga_d")
            for g in range(NG):
                for cc in range(8):
                    for W_, Wb_, dst in ((Wga, BWga, sga_d), (Wgm, BWgm, sgm_d)):
                        pb_, pbb, _ = pgg.next()
                        for kc in range(8):
                            op("pe", lambda e, pb_=pb_, W_=W_, kc=kc, cc=cc, g=g: e.matmul(pb_[:], lhsT=W_[:, kc, cc * P:(cc + 1) * P], rhs=hT[:, kc, g * GS:(g + 1) * GS],
                                                                                        start=(kc == 0), stop=(kc == 7)),
                               reads=[Wb_, hTb[g]], writes=[pbb])
                        sg, sgb, sgs = sgr.next()
                        op("act", lambda e, sg=sg, pb_=pb_: e.activation(sg[:], pb_[:], AF.Tanh, scale=0.5), reads=[pbb], writes=[sgb])
                        op("sp", lambda e, sg=sg, dst=dst, cc=cc, g=g: e.dma_start(out=dst[cc, :, g * GS:(g + 1) * GS], in_=sg[:]), reads=[sgb], dma=sgs)
            S.barrier()
            if stop_after == "ph2":
                toks = []
                toks.append(dump(dbg[0:P, 0:128], wtok[:], [Bml]))
                toks.append(dump(dbg[0:P, 128:256], cltok[:], [Bml]))
                toks.append(dump(dbg[0:P, 256:384], decbc[:], [Bml]))
                tmp = k.sb("tmpd", [P, 512], F32)
                Bt = Buf("tmpd")
                ds2 = S.dma_sem("dbg2")
                for i, (src, cc, g) in enumerate(((sga_d, 3, 5), (sgm_d, 6, 2))):
                    op("sp", lambda e, src=src, cc=cc, g=g: e.dma_start(out=tmp[:], in_=src[cc, :, g * GS:(g + 1) * GS]), writes=[Bt], dma=ds2)
                    toks.append(dump(dbg[P * (i + 1):P * (i + 2), 0:512], tmp[:], [Bt]))
                S.ops["sp"].append((tuple(toks), None, None))
                S.emit()
                return nc
            S.emit()
        with contextlib.ExitStack() as ph:
            k.pstack = ph
            QKT = k.sb("QKT", [P, 4, T], BF16)
            Vall = k.sb("Vall", [P, 32, 256], BF16)
            kmS = k.sb("kmS", [P, 64], F32)
            tmpk = k.sb("tmpk", [P, 2, 16], F32)
            kmT = k.sb("kmT", [P, 2, 16], BF16)
            kmL = k.sb("kmL", [P, 2, 16], BF16)
            W3 = k.sb("W3", [P, 8, 3, 256], BF16)
            BW3 = Buf("W3")
            w3sem = S.dma_sem("w3")
            BQK = [Buf(f"QK{g}") for g in range(NG)]
            BV = [Buf(f"V{g}") for g in range(NG)]
            Bkm = Buf("km")
            Wzr = Ring(k, "Wz", 2, [P, 8, P], BF16, dma=True)
            Utr = Ring(k, "Ut", 2, [P, 1024], F32, dma=True)
            Bya = Buf("yaT_d")
            for hp in range(4):
                if hp == 0:
                    S.dma_batch("pool", w3sem, [((lambda e, j=j, col=col: e.dma_start(out=W3[:, :, j, :], in_=w_in_r[:, :, col:col + 256])), [], [BW3])
                                                for j, col in enumerate((COL_QA, COL_KA, COL_VA))])
                with contextlib.ExitStack() as sa:
                    k.pstack = sa
                    bX = Ring(k, "bX", 2, [P, 512], F32, psum=True)
                    bY = Ring(k, "bY", 2, [P, 512], F32, psum=True)
                    bZ = Ring(k, "bZ", 2, [P, 1024], BF16, psum=True)
                    pkm = k.psum("pkm", [P, 512], F32)
                    Bpkm = Buf("pkm", psum=True)
                    sqr = Ring(k, "sq", 2, [P, 512], F32)
                    s4r = Ring(k, "ssq4", 2, [P, 4], F32)
                    qknr = Ring(k, "qkn", 3, [P, 512], BF16)
                    prevT = None

                    def emitT(tt, qkn, qknb):
                        g = tt // 4
                        bz, bzb, _ = bZ.next()
                        for j in range(4):
                            op("pe", lambda e, bz=bz, qkn=qkn, j=j: e.transpose(bz[:, j * P:(j + 1) * P], qkn[:, j * P:(j + 1) * P], ident_bf[:]), reads=[qknb, Bc], writes=[bzb])
                        op("act", lambda e, bz=bz, tt=tt: e.activation(QKT[:, 0:4, tt * P:(tt + 1) * P], bz[:, 0:512].rearrange("p (a b) -> p a b", b=P), AF.Copy),
                           reads=[bzb], writes=[BQK[g]])
                        for hh in range(2):
                            op("pe", lambda e, qkn=qkn, hh=hh, tt=tt: e.matmul(pkm[:, hh * 32 + tt:hh * 32 + tt + 1], lhsT=qkn[:, (2 + hh) * P:(3 + hh) * P], rhs=ones_bf[:, 0:1],
                                                                            start=True, stop=True),
                               reads=[qknb, Bc], writes=[Bpkm])

                    for tt in range(32):
                        g = tt // 4
                        bx, bxb, _ = bX.next()
                        for j in range(2):
                            for kc in range(8):
                                op("pe", lambda e, bx=bx, j=j, kc=kc, tt=tt: e.matmul(bx[:, j * 256:(j + 1) * 256], lhsT=hT[:, kc, tt * P:(tt + 1) * P], rhs=W3[:, kc, j, :],
                                                                                    start=(kc == 0), stop=(kc == 7)),
                                   reads=[hTb[g], BW3], writes=[bxb])
                        by, byb, _ = bY.next()
                        for kc in range(8):
                            op("pe", lambda e, by=by, kc=kc, tt=tt: e.matmul(by[:, 0:256], lhsT=hT[:, kc, tt * P:(tt + 1) * P], rhs=W3[:, kc, 2, :], start=(kc == 0), stop=(kc == 7)),
                               reads=[hTb[g], BW3], writes=[byb])
                        sq, sqb, _ = sqr.next()
                        s4, s4b, _ = s4r.next()
                        op("act", lambda e, sq=sq, bx=bx: e.activation(sq[:], bx[:], AF.Square), reads=[bxb], writes=[sqb])
                        op("dve", lambda e, sq=sq, s4=s4: e.tensor_reduce(out=s4[:], in_=sq[:].rearrange("p (a b) -> p a b", b=P), axis=AX.X, op=ALU.add), reads=[sqb], writes=[s4b])
                        op("act", lambda e, s4=s4: e.activation(s4[:], s4[:], AF.Sqrt, bias=EPS, scale=1.0 / P), reads=[s4b], writes=[s4b])
                        op("dve", lambda e, s4=s4: e.reciprocal(s4[:], s4[:]), reads=[s4b], writes=[s4b])
                        qkn, qknb, _ = qknr.next()
                        for j in range(4):
                            wbc = qw_bc if j < 2 else kw_bc
                            op("dve", lambda e, qkn=qkn, bx=bx, s4=s4, j=j, wbc=wbc: e.scalar_tensor_tensor(out=qkn[:, j * P:(j + 1) * P], in0=bx[:, j * P:(j + 1) * P], scalar=s4[:, j:j + 1],
                                                                                                          in1=wbc[:], op0=ALU.mult, op1=ALU.mult),
                               reads=[bxb, s4b, Bc], writes=[qknb])
                        op("act", lambda e, by=by, tt=tt: e.activation(Vall[:, tt, :], by[:, 0:256], AF.Copy), reads=[byb], writes=[BV[g]])
                        if prevT is not None:
                            emitT(*prevT)
                        prevT = (tt, qkn, qknb)
                    emitT(*prevT)
                    op("dve", lambda e: e.tensor_copy(kmS[:], pkm[:, 0:64]), reads=[Bpkm], writes=[Bkm])
                    kv = lambda i: kmS[:].rearrange("p (h b two) -> p h b two", h=2, two=2)[:, :, :, i]
                    op("dve", lambda e: e.tensor_tensor(tmpk[:], kv(0), kv(1), ALU.add), reads=[Bkm], writes=[Bkm])
                    op("dve", lambda e: e.tensor_scalar(tmpk[:], tmpk[:], 1.0 / 256.0, None, ALU.mult), reads=[Bkm], writes=[Bkm])
                    op("dve", lambda e: e.tensor_copy(kmT[:], tmpk[:]), reads=[Bkm], writes=[Bkm])
                    op("dve", lambda e: e.tensor_tensor(kmL[:], tmpk[:], kmT[:], ALU.subtract), reads=[Bkm], writes=[Bkm])
                    S.barrier()
                    S.emit()
                with contextlib.ExitStack() as sbk:
                    k.pstack = sbk
                    bS = Ring(k, "bS", 3, [P, 512], F32, psum=True)
                    bO = Ring(k, "bO", 2, [P, 512], F32, psum=True)
                    bSumr = Ring(k, "bSum", 2, [P, 512], F32, psum=True)
                    pnm = k.psum("pnm", [P, 1024], BF16)
                    Bpnm = Buf("pnm", psum=True)
                    tzr = Ring(k, "tz", 2, [P, 512], F32)
                    zsr = Ring(k, "zs", 3, [P, 512], F32)
                    gsbr = Ring(k, "gsb", 2, [P, 16], F32)
                    gallr = Ring(k, "gall", 2, [P, 64], F32)
                    t8r = Ring(k, "top8", 2, [P, 8], F32)
                    nmr = Ring(k, "nm", 8, [P, 16], BF16)
                    nmTr = Ring(k, "nmT", 3, [16, 512], BF16)
                    PTr = Ring(k, "PT", 4, [P, 512], BF16)
                    tmpdr = Ring(k, "tmpd", 2, [P, 512], F32)
                    recr = Ring(k, "rec", 2, [P, 512], F32)
                    osbr = Ring(k, "osb", 2, [P, 512], F32)
                    yagr = Ring(k, "yag", 2, [P, 512], BF16, dma=True)
                    heads = []
                    for hh in range(2):
                        hd = 2 * hp + hh
                        Wz, Wzb, Wzs = Wzr.next()
                        op("pool", lambda e, Wz=Wz, hd=hd: e.dma_start(out=Wz[:], in_=w_in_r[:, :, COL_ZA + hd * P:COL_ZA + (hd + 1) * P]), writes=[Wzb], dma=Wzs)
                        Ut, Utb, Uts = Utr.next()
                        op("sp", lambda e, Ut=Ut, hd=hd: e.dma_start(out=Ut[:], in_=utab[hd, :, :]), writes=[Utb], dma=Uts)
                        heads.append((hd, Wz, Wzb, Ut, Utb))
                    if hp + 1 < 4:
                        S.dma_batch("pool", w3sem, [((lambda e, j=j, col=col, hp=hp: e.dma_start(out=W3[:, :, j, :], in_=w_in_r[:, :, col + (hp + 1) * 256:col + (hp + 2) * 256])), [], [BW3])
                                                    for j, col in enumerate((COL_QA, COL_KA, COL_VA))])
                    LOOK = 2
                    for hh in range(2):
                        hd, Wz, Wzb, Ut, Utb = heads[hh]
                        gst = {}

                        def proA(g, hh=hh, Wz=Wz, Wzb=Wzb, gst=gst):
                            st_ = {}
                            bz_, bzb_, _ = bS.next()
                            for kc in range(8):
                                op("pe", lambda e, bz_=bz_, kc=kc, Wz=Wz, g=g: e.matmul(bz_[:], lhsT=Wz[:, kc, :], rhs=hT[:, kc, g * GS:(g + 1) * GS], start=(kc == 0), stop=(kc == 7)),
                                   reads=[Wzb, hTb[g]], writes=[bzb_])
                            tz, tzb, _ = tzr.next()
                            zs, zsb, _ = zsr.next()
                            op("act", lambda e, tz=tz, bz_=bz_: e.activation(tz[:], bz_[:], AF.Tanh, scale=0.5), reads=[bzb_], writes=[tzb])
                            op("dve", lambda e, zs=zs, tz=tz, bz_=bz_: e.scalar_tensor_tensor(out=zs[:], in0=tz[:], scalar=1.0, in1=bz_[:], op0=ALU.add, op1=ALU.mult),
                               reads=[tzb, bzb_], writes=[zsb])
                            st_["zs"] = (zs, zsb)
                            st_["nms"] = []
                            if g >= 2:
                                pg_, pgb_, _ = bS.next()
                                for qi in range(4):
                                    tq = 4 * g + qi
                                    op("pe", lambda e, pg_=pg_, qi=qi, tq=tq, hh=hh: e.matmul(pg_[:, qi * 16:(qi + 1) * 16], lhsT=QKT[:, hh, tq * P:(tq + 1) * P], rhs=kmT[:, hh, :], start=True, stop=False),
                                       reads=[BQK[g], Bkm], writes=[pgb_])
                                    op("pe", lambda e, pg_=pg_, qi=qi, tq=tq, hh=hh: e.matmul(pg_[:, qi * 16:(qi + 1) * 16], lhsT=QKT[:, hh, tq * P:(tq + 1) * P], rhs=kmL[:, hh, :], start=False, stop=True),
                                       reads=[BQK[g], Bkm], writes=[pgb_])
                                gall, gallb, _ = gallr.next()
                                op("dve", lambda e, gall=gall, pg_=pg_: e.tensor_copy(gall[:], pg_[:, 0:64]), reads=[pgb_], writes=[gallb])
                                for qi in range(4):
                                    qblk = 2 * g + qi // 2
                                    gsb, gsbb, _ = gsbr.next()
                                    t8, t8b, _ = t8r.next()
                                    nm, nmb, _ = nmr.next()
                                    op("dve", lambda e, gsb=gsb: e.memset(gsb[:], -1.0e30), writes=[gsbb])
                                    op("dve", lambda e, gsb=gsb, qi=qi, qblk=qblk, gall=gall: e.tensor_copy(gsb[:, 0:qblk], gall[:, qi * 16:qi * 16 + qblk]), reads=[gallb], writes=[gsbb])
                                    op("dve", lambda e, gsb=gsb, t8=t8: e.max(t8[:], gsb[:]), reads=[gsbb], writes=[t8b])
                                    op("dve", lambda e, nm=nm: e.memset(nm[:], 0.0), writes=[nmb])
                                    op("dve", lambda e, nm=nm, gsb=gsb, t8=t8, qblk=qblk: e.tensor_scalar(nm[:, 0:qblk], gsb[:, 0:qblk], t8[:, 2:3], NEGV, ALU.is_lt, ALU.mult),
                                       reads=[gsbb, t8b], writes=[nmb])
                                    st_["nms"].append((nm, nmb))
                            gst[g] = st_

                        def proB(g, gst=gst):
                            st_ = gst[g]
                            st_["nmT"] = (None, None)
                            if g >= 2:
                                for qi, (nm, nmb) in enumerate(st_["nms"]):
                                    op("pe", lambda e, nm=nm, qi=qi: e.transpose(pnm[0:16, qi * P:(qi + 1) * P], nm[:], ident_bf[:]), reads=[nmb, Bc], writes=[Bpnm])
                                nmT, nmTb, _ = nmTr.next()
                                op("act", lambda e, nmT=nmT: e.activation(nmT[:], pnm[0:16, 0:512], AF.Copy), reads=[Bpnm], writes=[nmTb])
                                st_["nmT"] = (nmT, nmTb)

                        def epilogue(g, bo, bob, bsum, bsumb, hd=hd, gst=gst):
                            zs, zsb = gst[g]["zs"]
                            rec, recb, _ = recr.next()
                            osb, osbb, _ = osbr.next()
                            yag, yagb, yags = yagr.next()
                            op("act", lambda e, osb=osb, bo=bo: e.activation(osb[:], bo[:], AF.Copy), reads=[bob], writes=[osbb])
                            op("dve", lambda e, rec=rec, bsum=bsum: e.reciprocal(rec[:], bsum[:]), reads=[bsumb], writes=[recb])
                            op("pool", lambda e, osb=osb, rec=rec: e.tensor_tensor(osb[:], osb[:], rec[:], ALU.mult), reads=[osbb, recb], writes=[osbb])
                            op("pool", lambda e, yag=yag, osb=osb, zs=zs: e.tensor_tensor(yag[:], osb[:], zs[:], ALU.mult), reads=[osbb, zsb], writes=[yagb])
                            op("sp", lambda e, yag=yag, g=g, hd=hd: e.dma_start(out=yaT_d[hd, :, g * GS:(g + 1) * GS], in_=yag[:]), reads=[yagb], writes=[Bya], dma=yags)

                        pending = []

                        def pop_pv(hh=hh):
                            g, kt, NKT, c0, Nq, pt, ptb, bo, bob, bsum, bsumb = pending.pop(0)
                            op("pe", lambda e, bo=bo, c0=c0, kt=kt, hh=hh, pt=pt, Nq=Nq, NKT=NKT: e.matmul(bo[:, c0:GS], lhsT=Vall[:, kt, hh * P:(hh + 1) * P], rhs=pt[:, 0:Nq], start=(kt == 0), stop=(kt == NKT - 1)),
                               reads=[BV[kt // 4], ptb], writes=[bob])
                            op("pe", lambda e, bsum=bsum, c0=c0, kt=kt, pt=pt, Nq=Nq, NKT=NKT: e.matmul(bsum[:, c0:GS], lhsT=ones_bf[:], rhs=pt[:, 0:Nq], start=(kt == 0), stop=(kt == NKT - 1)),
                               reads=[Bc, ptb], writes=[bsumb])
                            if kt == NKT - 1:
                                epilogue(g, bo, bob, bsum, bsumb)

                        proA(0)
                        proB(0)
                        proA(1)
                        for g in range(NG):
                            NKT = 4 * (g + 1)
                            bo, bob, _ = bO.next()
                            bsum, bsumb, _ = bSumr.next()
                            for kt in range(NKT):
                                if kt == max(2, NKT - 2) and g + 1 < NG:
                                    proB(g + 1)
                                nmT, nmTb = gst[g]["nmT"]
                                j = kt - 4 * g
                                c0 = P * j if j >= 1 else 0
                                Nq = GS - c0
                                masked = (g >= 2) and (kt // 2 <= 2 * g)
                                bs, bsb, _ = bS.next()
                                op("pe", lambda e, bs=bs, kt=kt, g=g, c0=c0, Nq=Nq, masked=masked, hh=hh: e.matmul(bs[:, 0:Nq], lhsT=QKT[:, 2 + hh, kt * P:(kt + 1) * P],
                                                                                                         rhs=QKT[:, hh, g * GS + c0:(g + 1) * GS], start=True, stop=(not masked)),
                                   reads=[BQK[kt // 4], BQK[g]], writes=[bsb])
                                if masked:
                                    bk = kt // 2
                                    op("pe", lambda e, bs=bs, bk=bk, nmT=nmT, c0=c0, Nq=Nq: e.matmul(bs[:, 0:Nq], lhsT=ind_bf[0:16, bk * P:(bk + 1) * P], rhs=nmT[0:16, c0:GS],
                                                                                                  start=False, stop=True),
                                       reads=[Bc, nmTb], writes=[bsb])
                                pt, ptb, _ = PTr.next()
                                if j >= -1:
                                    td, tdb, _ = tmpdr.next()
                                    u0 = 384 - P * j + c0
                                    op("dve", lambda e, td=td, bs=bs, u0=u0, Nq=Nq, Ut=Ut: e.scalar_tensor_tensor(out=td[:, 0:Nq], in0=bs[:, 0:Nq], scalar=ATT_SCALE, in1=Ut[:, u0:u0 + Nq],
                                                                                                        op0=ALU.mult, op1=ALU.add),
                                       reads=[bsb, Utb], writes=[tdb])
                                    op("act", lambda e, pt=pt, td=td, Nq=Nq: e.activation(pt[:, 0:Nq], td[:, 0:Nq], AF.Exp), reads=[tdb], writes=[ptb])
                                else:
                                    op("act", lambda e, pt=pt, bs=bs, hd=hd: e.activation(pt[:], bs[:], AF.Exp, bias=b31_bc[:, hd:hd + 1], scale=ATT_SCALE), reads=[bsb, Bc], writes=[ptb])
                                pending.append((g, kt, NKT, c0, Nq, pt, ptb, bo, bob, bsum, bsumb))
                                if len(pending) > LOOK:
                                    pop_pv()
                            if g + 2 < NG:
                                proA(g + 2)
                        while pending:
                            pop_pv()
                    S.barrier()
                    if stop_after == "ph3" and hp == 0:
                        toks = []
                        tb = k.sb("tmpd", [P, 1024], BF16)
                        tf = k.sb("tmpf", [P, 1024], F32)
                        Bt = Buf("tmpd")
                        Bt2 = Buf("tmpf")
                        ds2 = S.dma_sem("dbg2")
                        for hd in range(2):
                            for q4 in range(4):
                                op("sp", lambda e, hd=hd, q4=q4: e.dma_start(out=tb[:], in_=yaT_d[hd, :, q4 * 1024:(q4 + 1) * 1024]), reads=[Bya], writes=[Bt], dma=ds2)
                                op("dve", lambda e: e.tensor_copy(tf[:], tb[:]), reads=[Bt], writes=[Bt2])
                                toks.append(dump(dbg[hd * P:(hd + 1) * P, q4 * 1024:(q4 + 1) * 1024], tf[:], [Bt2]))
                        for i in range(4):
                            for q4 in range(4):
                                op("dve", lambda e, i=i, q4=q4: e.tensor_copy(tf[:], QKT[:, i, q4 * 1024:(q4 + 1) * 1024]), reads=BQK, writes=[Bt2])
                                toks.append(dump(dbg[(2 + i) * P:(3 + i) * P, q4 * 1024:(q4 + 1) * 1024], tf[:], [Bt2]))
                        S.ops["sp"].append((tuple(toks), None, None))
                        S.emit()
                        return nc
                    S.emit()
                k.pstack = ph
        k.pstack = hst
        Wa = k.sb("Wa", [P, 8, D], BF16)
        Wm = k.sb("Wm", [P, 8, D], BF16)
        Wo = k.sb("Wo", [P, 8, D], BF16)
        BWa, BWm, BWo = Buf("Wa"), Buf("Wm"), Buf("Wo")
        with contextlib.ExitStack() as ph:
            k.pstack = ph
            W5 = k.sb("W5", [P, 8, 5, 256], BF16)
            BW5s = [Buf(f"W5_{j}") for j in range(5)]
            w5sems = [S.dma_sem(f"w5_{j}") for j in range(5)]
            Cst = k.sb("Cst", [P, 2, 258], F32)
            Cbf = k.sb("Cbf", [P, 2, 258], BF16)
            BC = Buf("Cst")
            BCbf = Buf("Cbf")
            pre = [[k.sb(f"pre{w}{dc}", [P, 516], BF16) for dc in range(2)] for w in range(2)]
            Bpre = [[Buf(f"pre{w}{dc}") for dc in range(2)] for w in range(2)]
            thr_ = Ring(k, "th", 2, [P, 512], F32)
            QmTr = Ring(k, "QmT", 2, [P, 2, 512], BF16)
            KmTr = Ring(k, "KmT", 2, [P, 2, 512], BF16)
            Ktokr = Ring(k, "Ktok", 2, [P, 4, 256], BF16)
            Vpr = Ring(k, "Vp", 2, [P, 258], BF16)
            Smr = Ring(k, "Sm", 2, [P, P], BF16)
            thozr = Ring(k, "thoz", 2, [P, 512], F32)
            dnr = Ring(k, "dn", 2, [P, 1], F32)
            hm2r = Ring(k, "hm2", 2, [P, 4, 256], F32)
            szr = Ring(k, "sz", 2, [P, 4, 256], F32)
            ssqmr = Ring(k, "ssqm", 2, [P, 4], F32)
            junk2 = k.sb("junk2", [P, 256], F32)
            Bjunk2 = Buf("junk2")
            t1r = Ring(k, "t1", 2, [P, 256], F32)
            ymgr = Ring(k, "ymg", 2, [P, 256], BF16)
            ymTsr = Ring(k, "ymTs", 2, [P, 2, 512], BF16, dma=True)
            bQK = Ring(k, "bQK", 2, [P, 512], F32, psum=True)
            bT = k.psum("bT", [P, 1024], BF16)
            BbT = Buf("bT", psum=True)
            bT2 = bT
            BbT2 = BbT
            diag = k.sb("diag", [P, 16, P], BF16)
            Bdiag = Buf("diag")
            caus_s = k.sb("caus_s", [P, P], F32)
            op("dve", lambda e: e.tensor_scalar(caus_s[:], caus[:], ML_KSCALE, None, ALU.mult), reads=[Bc], writes=[Bdiag])
            bV = k.psum("bV", [P, 512], F32)
            BbV = Buf("bV", psum=True)
            bOZ = k.psum("bOZ", [P, 512], F32)
            BbOZ = Buf("bOZ", psum=True)
            bOut = k.psum("bOut", [P, 512], F32)
            BbOut = Buf("bOut", psum=True)
            bCp = [k.psum(f"bC{i}", [P, 512], F32) for i in range(2)]
            BbCp = [Buf(f"bC{i}", psum=True) for i in range(2)]
            Bym = Buf("ymT_d")
            for mh in range(4):
                for j, col in enumerate((COL_QM, COL_KM, COL_VM, COL_ZM, COL_OM)):
                    op("pool", lambda e, j=j, col=col, mh=mh: e.dma_start(out=W5[:, :, j, :], in_=w_in_r[:, :, col + mh * 256:col + (mh + 1) * 256]), writes=[BW5s[j]], dma=w5sems[j])
                if mh == 0:
                    for W_, Wb_, src in ((Wa, BWa, w_att), (Wm, BWm, w_ml), (Wo, BWo, w_out)):
                        sm_ = S.dma_sem(f"w5_{k.uid()}")
                        op("pool", lambda e, W_=W_, src=src: e.dma_start(out=W_[:], in_=src.rearrange("(kc p) n -> p kc n", p=P)), writes=[Wb_], dma=sm_)
                if mh == 2:
                    op("dve", lambda e: e.tensor_scalar(Wa[:], Wa[:], 0.5, None, ALU.mult), reads=[BWa], writes=[BWa])
                op("dve", lambda e: e.memset(Cst[:], 0.0), writes=[BC])
                for w in range(2):
                    for dc in range(2):
                        op("dve", lambda e, w=w, dc=dc: e.memset(pre[w][dc][:, 0:3], 0.0), writes=[Bpre[w][dc]])
                for w in range(2):
                    for dc in range(2):
                        for j in range(4):
                            op("dve", lambda e, w=w, dc=dc, j=j, mh=mh: e.tensor_scalar(diag[:, (w * 2 + dc) * 4 + j, :], ident[:], convw_sb[:, w * 8 + mh * 2 + dc, j:j + 1], None, ALU.mult),
                               reads=[Bc], writes=[Bdiag])

                def qk_piece(g, w, dc, dstT, dstb, mh=mh):
                    bq, bqb, _ = bQK.next()
                    for kc in range(8):
                        op("pe", lambda e, bq=bq, kc=kc: e.matmul(bq[:], lhsT=W5[:, kc, w, dc * P:(dc + 1) * P], rhs=hT[:, kc, g * GS:(g + 1) * GS], start=(kc == 0), stop=(kc == 7)),
                           reads=[BW5s[w], hTb[g]], writes=[bqb])
                    pr = pre[w][dc]
                    prb = Bpre[w][dc]
                    cidx = w * 8 + mh * 2 + dc
                    didx = (w * 2 + dc) * 4
                    op("act", lambda e: e.activation(pr[:, 3:515], bq[:], AF.Copy), reads=[bqb], writes=[prb])

                    def part2():
                        bcv, bcvb, _ = bQK.next()
                        for j in range(4):
                            op("pe", lambda e, j=j: e.matmul(bcv[:], lhsT=diag[:, didx + j, :], rhs=pr[:, j:j + 512], start=(j == 0), stop=(j == 3)), reads=[Bdiag, prb], writes=[bcvb])
                        op("pool", lambda e: e.tensor_copy(pr[:, 0:3], pr[:, 512:515]), reads=[prb], writes=[prb])
                        op("act", lambda e: e.activation(dstT[:, dc, :], bcv[:], AF.Silu, bias=convb_sb[:, cidx:cidx + 1], scale=1.0), reads=[bcvb, Bc], writes=[dstb])
                    return part2

                pieces = [(w, dc) for w in range(2) for dc in range(2)]
                nxt = (QmTr.next(), KmTr.next())
                for (w, dc) in pieces:
                    dd = nxt[w]
                    qk_piece(0, w, dc, dd[0], dd[1])()

                def gend_dve(gs, c4, mh=mh):
                    hm2, hm2b, sz, szb, ssqm, ssqmb = gs["bufs"]
                    t1, t1b, _ = t1r.next()
                    ymg, ymgb, _ = ymgr.next()
                    op("dve", lambda e: e.scalar_tensor_tensor(out=t1[:], in0=hm2[:, c4, :], scalar=ssqm[:, c4:c4 + 1], in1=mlw_bc[:, mh * 256:(mh + 1) * 256], op0=ALU.mult, op1=ALU.mult),
                       reads=[hm2b, ssqmb, Bc], writes=[t1b])
                    op("pool", lambda e: e.tensor_tensor(ymg[:], t1[:], sz[:, c4, :], ALU.mult), reads=[t1b, szb], writes=[ymgb])
                    gs.setdefault("ymgs", {})[c4] = (ymg, ymgb)

                def gend_sqrt(gs):
                    hm2, hm2b, sz, szb, ssqm, ssqmb = gs["bufs"]
                    op("act", lambda e: e.activation(ssqm[:], ssqm[:], AF.Sqrt, bias=EPS, scale=1.0 / 256.0), reads=[ssqmb], writes=[ssqmb])
                    op("dve", lambda e: e.reciprocal(ssqm[:], ssqm[:]), reads=[ssqmb], writes=[ssqmb])

                def gend_pe(gs, c4, mh=mh):
                    ymg, ymgb = gs["ymgs"][c4]
                    g_ = gs["g"]
                    for dc in range(2):
                        op("pe", lambda e, dc=dc: e.transpose(bT2[:, (dc * 4 + c4) * P:(dc * 4 + c4 + 1) * P], ymg[:, dc * P:(dc + 1) * P], ident_bf[:]), reads=[ymgb, Bc], writes=[BbT2])
                    if c4 == 3:
                        ymTs, ymTsb, ymTss = ymTsr.next()
                        op("act", lambda e: e.activation(ymTs[:], bT2[:, 0:1024].rearrange("p (a b) -> p a b", b=512), AF.Copy), reads=[BbT2], writes=[ymTsb])
                        S.dma_batch("sp", ymTss, [((lambda e, dc=dc: e.dma_start(out=ymT_d[mh * 2 + dc, :, g_ * GS:(g_ + 1) * GS], in_=ymTs[:, dc, :])), [ymTsb], [Bym])
                                                  for dc in range(2)])

                prevg = None
                pend_tail = []
                for g in range(NG):
                    (QmT, QmTb, _), (KmT, KmTb, _) = nxt
                    if g + 1 < NG:
                        nxt = (QmTr.next(), KmTr.next())
                    Ktok, Ktokb, _ = Ktokr.next()
                    for c4 in range(4):
                        for dc in range(2):
                            op("pe", lambda e, KmT=KmT, c4=c4, dc=dc: e.transpose(bT[:, c4 * 256 + dc * P:c4 * 256 + (dc + 1) * P], KmT[:, dc, c4 * P:(c4 + 1) * P], ident_bf[:]),
                               reads=[KmTb, Bc], writes=[BbT])
                    op("act", lambda e, Ktok=Ktok: e.activation(Ktok[:], bT[:, 0:1024].rearrange("p (a b) -> p a b", b=256), AF.Copy), reads=[BbT], writes=[Ktokb])
                    hm2, hm2b, _ = hm2r.next()
                    sz, szb, _ = szr.next()
                    ssqm, ssqmb, _ = ssqmr.next()
                    curg = {"g": g, "bufs": (hm2, hm2b, sz, szb, ssqm, ssqmb)}
                    for c4 in range(4):
                        ci = g * 4 + c4
                        t0 = ci * P
                        col = ci * 4 + mh
                        part2 = None
                        if g + 1 < NG:
                            w, dc = pieces[c4]
                            dd = nxt[w]
                            part2 = qk_piece(g + 1, w, dc, dd[0], dd[1])
                        while pend_tail:
                            pend_tail.pop(0)()
                        if prevg is not None:
                            if c4 == 1:
                                gend_dve(prevg, 0)
                            if c4 >= 1:
                                gend_dve(prevg, c4)
                        for dc in range(2):
                            op("pe", lambda e, KmT=KmT, QmT=QmT, dc=dc, c4=c4: e.matmul(bV[:, 256:384], lhsT=KmT[:, dc, c4 * P:(c4 + 1) * P], rhs=QmT[:, dc, c4 * P:(c4 + 1) * P],
                                                                                     start=(dc == 0), stop=(dc == 1)),
                               reads=[KmTb, QmTb], writes=[BbV])
                        for kc in range(8):
                            op("pe", lambda e, kc=kc, t0=t0: e.matmul(bV[:, 0:256], lhsT=hT[:, kc, t0:t0 + P], rhs=W5[:, kc, 2, :], start=(kc == 0), stop=(kc == 7)),
                               reads=[hTb[g], BW5s[2]], writes=[BbV])
                        Sm, Smb, _ = Smr.next()
                        op("dve", lambda e, Sm=Sm: e.tensor_tensor(Sm[:], bV[:, 256:384], caus_s[:], ALU.mult), reads=[BbV, Bdiag], writes=[Smb])
                        Vp, Vpb, _ = Vpr.next()
                        op("act", lambda e, Vp=Vp, col=col: e.activation(Vp[:, 0:256], bV[:, 0:256], AF.Copy, scale=wtok[:, col:col + 1]), reads=[BbV, Bml], writes=[Vpb])
                        op("pool", lambda e, Vp=Vp, col=col: e.tensor_copy(Vp[:, 256:258], wtok[:, col:col + 1].to_broadcast([P, 2])), reads=[Bml], writes=[Vpb])
                        if ci > 0:
                            op("pool", lambda e, col=col: e.tensor_scalar(Cbf[:], Cst[:], decbc[:, col:col + 1], ML_KSCALE, ALU.mult, ALU.mult), reads=[BC, Bml], writes=[BCbf])
                        if part2 is not None:
                            part2()
                        for jj, wi in enumerate((4, 3)):
                            for kc in range(8):
                                op("pe", lambda e, kc=kc, t0=t0, jj=jj, wi=wi: e.matmul(bOZ[:, jj * 256:(jj + 1) * 256], lhsT=hT[:, kc, t0:t0 + P], rhs=W5[:, kc, wi, :], start=(kc == 0), stop=(kc == 7)),
                                   reads=[hTb[g], BW5s[wi]], writes=[BbOZ])
                        thoz, thozb, _ = thozr.next()
                        op("act", lambda e, thoz=thoz: e.activation(thoz[:, 0:256], bOZ[:, 0:256], AF.Tanh, scale=0.5), reads=[BbOZ], writes=[thozb])
                        op("act", lambda e, sz=sz, c4=c4: e.activation(sz[:, c4, :], bOZ[:, 256:512], AF.Silu), reads=[BbOZ], writes=[szb])
                        op("pool", lambda e, thoz=thoz: e.tensor_scalar(thoz[:, 0:256], thoz[:, 0:256], 0.5, 0.5, ALU.mult, ALU.add), reads=[thozb], writes=[thozb])
                        if ci > 0:
                            for dc in range(2):
                                op("pe", lambda e, QmT=QmT, dc=dc, c4=c4: e.matmul(bOut[:, 0:257], lhsT=QmT[:, dc, c4 * P:(c4 + 1) * P], rhs=Cbf[:, dc, 0:257], start=(dc == 0), stop=False),
                                   reads=[QmTb, BCbf], writes=[BbOut])
                        op("pe", lambda e, Sm=Sm, Vp=Vp, ci=ci: e.matmul(bOut[:, 0:257], lhsT=Sm[:], rhs=Vp[:, 0:257], start=(ci == 0), stop=True), reads=[Smb, Vpb], writes=[BbOut])
                        for dkc in range(2):
                            op("pe", lambda e, Ktok=Ktok, c4=c4, dkc=dkc, Vp=Vp: e.matmul(bCp[dkc][:, 0:257], lhsT=Ktok[:, c4, dkc * P:(dkc + 1) * P], rhs=Vp[:, 0:257], start=True, stop=True),
                               reads=[Ktokb, Vpb], writes=[BbCp[dkc]])
                            op("dve", lambda e, dkc=dkc, col=col: e.scalar_tensor_tensor(out=Cst[:, dkc, 0:257], in0=Cst[:, dkc, 0:257], scalar=decbc[:, col:col + 1], in1=bCp[dkc][:, 0:257],
                                                                                      op0=ALU.mult, op1=ALU.add),
                               reads=[BC, Bml, BbCp[dkc]], writes=[BC])
                        def tail(hm2=hm2, hm2b=hm2b, ssqm=ssqm, ssqmb=ssqmb, c4=c4, col=col, thoz=thoz, thozb=thozb):
                            dn, dnb, _ = dnr.next()
                            op("dve", lambda e: e.tensor_scalar(dn[:], bOut[:, 256:257], -1.0, cltok[:, col:col + 1], ALU.mult, ALU.max), reads=[BbOut, Bml], writes=[dnb])
                            op("dve", lambda e: e.tensor_tensor(dn[:], dn[:], bOut[:, 256:257], ALU.max), reads=[BbOut, dnb], writes=[dnb])
                            op("dve", lambda e: e.reciprocal(dn[:], dn[:]), reads=[dnb], writes=[dnb])
                            op("dve", lambda e: e.scalar_tensor_tensor(out=hm2[:, c4, :], in0=bOut[:, 0:256], scalar=dn[:, 0:1], in1=thoz[:, 0:256], op0=ALU.mult, op1=ALU.mult),
                               reads=[BbOut, dnb, thozb], writes=[hm2b])
                            op("act", lambda e: e.activation(junk2[:], hm2[:, c4, :], AF.Square, accum_out=ssqm[:, c4:c4 + 1]), reads=[hm2b], writes=[Bjunk2, ssqmb])
                        pend_tail.append(tail)
                        if prevg is not None:
                            if c4 == 0:
                                gend_sqrt(prevg)
                            else:
                                gend_pe(prevg, c4 - 1)
                                if c4 == 3:
                                    gend_pe(prevg, 3)
                    prevg = curg
                while pend_tail:
                    pend_tail.pop(0)()
                gend_sqrt(prevg)
                for c4 in range(4):
                    gend_dve(prevg, c4)
                    gend_pe(prevg, c4)
                if stop_after == "ph4" and mh == 0:
                    S.barrier()
                    toks = []
                    tb = k.sb("tmpd", [P, 1024], BF16)
                    tf = k.sb("tmpf", [P, 1024], F32)
                    Bt = Buf("tmpd")
                    Bt2 = Buf("tmpf")
                    ds2 = S.dma_sem("dbg2")
                    for kc in range(2):
                        for q4 in range(4):
                            op("sp", lambda e, kc=kc, q4=q4: e.dma_start(out=tb[:], in_=ymT_d[kc, :, q4 * 1024:(q4 + 1) * 1024]), reads=[Bym], writes=[Bt], dma=ds2)
                            op("dve", lambda e: e.tensor_copy(tf[:], tb[:]), reads=[Bt], writes=[Bt2])
                            toks.append(dump(dbg[kc * P:(kc + 1) * P, q4 * 1024:(q4 + 1) * 1024], tf[:], [Bt2]))
                    S.ops["sp"].append((tuple(toks), None, None))
                    S.emit()
                    return nc
            S.barrier()
            S.emit()
        with contextlib.ExitStack() as ph:
            k.pstack = ph
            yaTr = Ring(k, "yaTg", 2, [P, 8, GS], BF16, dma=True)
            ymTr = Ring(k, "ymTg", 2, [P, 8, GS], BF16, dma=True)
            sgar = Ring(k, "sga", 3, [P, GS], F32, dma=True)
            sgmr = Ring(k, "sgm", 3, [P, GS], F32, dma=True)
            y1r = Ring(k, "y1", 2, [P, GS], F32)
            y2r = Ring(k, "y2", 2, [P, GS], F32)
            yTr = Ring(k, "yT", 1, [P, 8, GS], BF16)
            xr5 = Ring(k, "x5", 2, [P, D], F32, dma=True)
            otr = Ring(k, "ot", 2, [P, D], F32, dma=True)
            bA = Ring(k, "bA", 3, [P, 512], F32, psum=True)
            bM = Ring(k, "bM", 3, [P, 512], F32, psum=True)
            bF = Ring(k, "bF", 2, [P, 512], F32, psum=True)
            def load_branch(g):
                yaTg, yaTgb, yas = yaTr.next()
                ymTg, ymTgb, yms = ymTr.next()
                op("sp", lambda e: e.dma_start(out=yaTg[:], in_=yaT_d[:, :, g * GS:(g + 1) * GS].rearrange("k p t -> p k t")), writes=[yaTgb], dma=yas)
                op("sp", lambda e: e.dma_start(out=ymTg[:], in_=ymT_d[:, :, g * GS:(g + 1) * GS].rearrange("k p t -> p k t")), writes=[ymTgb], dma=yms)
                return yaTg, yaTgb, ymTg, ymTgb

            nxt_br = load_branch(0)
            for g in range(NG):
                yaTg, yaTgb, ymTg, ymTgb = nxt_br
                if g + 1 < NG:
                    nxt_br = load_branch(g + 1)
                yT, yTb, _ = yTr.next()
                for cc in range(8):
                    sga, sgab, sgas = sgar.next()
                    sgm, sgmb, sgms = sgmr.next()
                    op("sp", lambda e, sga=sga, cc=cc, g=g: e.dma_start(out=sga[:], in_=sga_d[cc, :, g * GS:(g + 1) * GS]), writes=[sgab], dma=sgas)
                    op("sp", lambda e, sgm=sgm, cc=cc, g=g: e.dma_start(out=sgm[:], in_=sgm_d[cc, :, g * GS:(g + 1) * GS]), writes=[sgmb], dma=sgms)
                    ba, bab, _ = bA.next()
                    bm, bmb, _ = bM.next()
                    for kc in range(8):
                        op("pe", lambda e, ba=ba, kc=kc, cc=cc, yaTg=yaTg: e.matmul(ba[:], lhsT=Wa[:, kc, cc * P:(cc + 1) * P], rhs=yaTg[:, kc, :], start=(kc == 0), stop=(kc == 7)),
                           reads=[BWa, yaTgb], writes=[bab])
                    for kc in range(8):
                        op("pe", lambda e, bm=bm, kc=kc, cc=cc, ymTg=ymTg: e.matmul(bm[:], lhsT=Wm[:, kc, cc * P:(cc + 1) * P], rhs=ymTg[:, kc, :], start=(kc == 0), stop=(kc == 7)),
                           reads=[BWm, ymTgb], writes=[bmb])
                    y1, y1b, _ = y1r.next()
                    y2, y2b, _ = y2r.next()
                    op("dve", lambda e, y1=y1, sga=sga, ba=ba: e.scalar_tensor_tensor(out=y1[:], in0=sga[:], scalar=1.0, in1=ba[:], op0=ALU.add, op1=ALU.mult), reads=[sgab, bab], writes=[y1b])
                    op("dve", lambda e, y2=y2, sgm=sgm, bm=bm: e.scalar_tensor_tensor(out=y2[:], in0=sgm[:], scalar=1.0, in1=bm[:], op0=ALU.add, op1=ALU.mult), reads=[sgmb, bmb], writes=[y2b])
                    op("pool", lambda e, yT=yT, cc=cc, y1=y1, y2=y2: e.tensor_tensor(yT[:, cc, :], y1[:], y2[:], ALU.add), reads=[y1b, y2b], writes=[yTb])
                for tt in range(4):
                    ti = g * 4 + tt
                    xt, xb, xs = xr5.next()
                    ot, otb, ots = otr.next()
                    op("sp", lambda e, xt=xt, ti=ti: e.dma_start(out=xt[:], in_=x[ti * P:(ti + 1) * P, :]), writes=[xb], dma=xs)
                    for og in range(2):
                        bf_, bfb, _ = bF.next()
                        for cc in range(8):
                            op("pe", lambda e, bf_=bf_, cc=cc, tt=tt, og=og, yT=yT: e.matmul(bf_[:], lhsT=yT[:, cc, tt * P:(tt + 1) * P], rhs=Wo[:, cc, og * 512:(og + 1) * 512], start=(cc == 0), stop=(cc == 7)),
                               reads=[yTb, BWo], writes=[bfb])
                        op("dve", lambda e, ot=ot, bf_=bf_, og=og: e.tensor_tensor(ot[:, og * 512:(og + 1) * 512], bf_[:], gate_half[:, og * 512:(og + 1) * 512], ALU.mult), reads=[bfb, Bgate], writes=[otb])
                    op("pool", lambda e, ot=ot, xt=xt: e.tensor_tensor(ot[:], ot[:], xt[:], ALU.add), reads=[otb, xb], writes=[otb])
                    op("act", lambda e, ot=ot, ti=ti: e.dma_start(out=out[ti * P:(ti + 1) * P, :], in_=ot[:]), reads=[otb], dma=ots)
            S.barrier()
            S.emit()
    return nc


def _host_inputs(inputs):
    f = np.float32
    x = np.ascontiguousarray(inputs["x"], dtype=f)
    c = np.asarray(inputs["c"], dtype=f)
    rel_bias = np.asarray(inputs["rel_bias"], dtype=f)
    dist = np.arange(0, 1024)
    max_exact = 16
    nf = np.maximum(dist, 1).astype(np.float32)
    large = max_exact + (np.log(nf / max_exact) / np.log(128 / max_exact) * (32 - max_exact)).astype(np.int32)
    large = np.minimum(large, 31)
    bucket = np.where(dist < max_exact, dist, large)
    kk = np.arange(128)[:, None]
    jj = np.arange(1024)[None, :]
    dd = jj - 384 - kk
    valid = dd >= 0
    bidx = bucket[np.clip(dd, 0, 1023)]
    utab = np.empty((8, 128, 1024), dtype=f)
    for h in range(8):
        g = rel_bias[:, h][bidx]
        utab[h] = np.where(valid, g, f(NEGV))
    conv_w = np.asarray(inputs["conv_w"], dtype=f)[0]
    conv_b = np.asarray(inputs["conv_b"], dtype=f)[0]
    convw = np.ascontiguousarray(conv_w.T.reshape(16, 128, 4).transpose(1, 0, 2))
    convb = np.ascontiguousarray(conv_b.reshape(16, 128).T)
    ident = np.eye(128, dtype=f)
    caus = np.triu(np.ones((128, 128), dtype=f))
    ind = np.zeros((16, 16, 128), dtype=f)
    for b in range(16):
        ind[b, b, :] = 1.0
    common = {
        "w_ada": np.ascontiguousarray(inputs["w_ada"][0], dtype=f),
        "b_ada": np.ascontiguousarray(inputs["b_ada"][0:1], dtype=f),
        "norm_w": np.ascontiguousarray(inputs["norm_w"][0:1], dtype=f),
        "w_in": np.ascontiguousarray(inputs["w_in"][0], dtype=f),
        "qnw": np.ascontiguousarray(inputs["q_norm_w"][0:1], dtype=f),
        "knw": np.ascontiguousarray(inputs["k_norm_w"][0:1], dtype=f),
        "utab": utab,
        "bias31": np.ascontiguousarray(rel_bias[31:32, :]),
        "convw": convw,
        "convb": convb,
        "b_ig": np.ascontiguousarray(np.asarray(inputs["b_igate"], dtype=f)[0].reshape(4, 1)),
        "b_fg": np.ascontiguousarray(np.asarray(inputs["b_fgate"], dtype=f)[0].reshape(4, 1)),
        "mlnw": np.ascontiguousarray(inputs["ml_norm_w"][0:1], dtype=f),
        "w_att": np.ascontiguousarray(inputs["w_att_proj"][0], dtype=f),
        "w_ml": np.ascontiguousarray(inputs["w_ml_proj"][0], dtype=f),
        "w_out": np.ascontiguousarray(inputs["w_out"][0], dtype=f),
        "c_ident": ident,
        "c_caus": caus,
        "c_ind": ind.reshape(16, 16 * 128),
    }
    maps = []
    for b in range(x.shape[0]):
        m = dict(common)
        m["x"] = x[b]
        m["ccol"] = np.ascontiguousarray(c[b].reshape(8, 128).T)
        maps.append(m)
    return maps


def kernel(**inputs):
    maps = _host_inputs(inputs)
    nc = build()
    res = run_bass_kernel_spmd(nc, maps, core_ids=list(range(8)))
    return np.stack([np.asarray(r["out"], dtype=np.float32) for r in res.results], axis=0)
```

```python
import contextlib
import numpy as np
import ml_dtypes
import concourse.bass as bass
import concourse.mybir as mybir
from concourse.bass_utils import run_bass_kernel_spmd

F32 = mybir.dt.float32
BF16 = mybir.dt.bfloat16
AF = mybir.ActivationFunctionType
ALU = mybir.AluOpType
AX = mybir.AxisListType

T = 4096
D = 1024
P = 128
NKC = 8
NG = 8
GS = 512
EPS = 1e-6
NEGV = -30000.0
ATT_SCALE = 128.0 ** -0.5
ML_KSCALE = 256.0 ** -0.5
COL_QA, COL_KA, COL_VA, COL_ZA = 0, 1024, 2048, 3072
COL_QM, COL_KM, COL_VM, COL_ZM, COL_OM = 4096, 5120, 6144, 7168, 8192
COL_IG, COL_FG, COL_GA, COL_GM = 9216, 9220, 9224, 10248
D_IN = 11272


class Buf:
    __slots__ = ("w", "r", "name", "psum")

    def __init__(self, name="", psum=False):
        self.w = None
        self.r = {}
        self.name = name
        self.psum = psum


class Sched:
    ENGS = ("pe", "act", "dve", "pool", "sp")

    def __init__(self, nc, stack):
        self.nc = nc
        self.stack = stack
        self.ops = {e: [] for e in self.ENGS}
        self.cnt = {e: 0 for e in self.ENGS}
        self.waited = {e: {} for e in self.ENGS}
        self.sems = {}
        self.dcnt = {}
        for e in self.ENGS:
            self.sems[e] = stack.enter_context(nc.semaphore("s_" + e))

    def dma_sem(self, name):
        self.sems[name] = self.stack.enter_context(self.nc.semaphore("d_" + name))
        self.dcnt[name] = 0
        return name

    def dma_batch(self, eng, sem, items):
        n = len(items)
        toks = []
        for i, (fn, reads, writes) in enumerate(items):
            toks.append(self.op(eng, fn, reads=reads, writes=writes, dma=sem, dma_extra=n - 1 - i))
        return toks

    def op(self, eng, fn, reads=(), writes=(), dma=None, dma_extra=0):
        waits = {}
        wd = self.waited[eng]

        def need(dep):
            if dep is None:
                return
            sk, val = dep
            if sk == "pe" and eng == "pe":
                return
            if dma is not None and sk == dma and val > self.dcnt[dma]:
                return
            if wd.get(sk, 0) >= val:
                return
            if waits.get(sk, 0) < val:
                waits[sk] = val

        for b in reads:
            need(b.w)
            if b.psum:
                for sk, v in b.r.items():
                    if sk != eng:
                        need((sk, v))
        for b in writes:
            need(b.w)
            for sk, v in b.r.items():
                need((sk, v))
        for sk, v in waits.items():
            wd[sk] = v
        if dma is None:
            self.cnt[eng] += 1
            tok = (eng, self.cnt[eng])
            inc = (eng, 1)
        else:
            self.dcnt[dma] += 16
            tok = (dma, self.dcnt[dma] + 16 * dma_extra)
            inc = (dma, 16)
        for b in writes:
            b.w = tok
            b.r = {}
        for b in reads:
            if b.r.get(tok[0], 0) < tok[1]:
                b.r[tok[0]] = tok[1]
        self.ops[eng].append((tuple(waits.items()), fn, inc))
        return tok

    def barrier(self):
        toks = [(e, self.cnt[e]) for e in self.ENGS if self.cnt[e] > 0]
        toks += [(k, v) for k, v in self.dcnt.items() if v > 0]
        for e in self.ENGS:
            w = []
            for sk, v in toks:
                if sk == e:
                    continue
                if self.waited[e].get(sk, 0) < v:
                    self.waited[e][sk] = v
                    w.append((sk, v))
            self.ops[e].append((tuple(w), None, None))

    def emit(self):
        nc = self.nc
        sems = self.sems
        ops_now = self.ops
        self.ops = {e: [] for e in self.ENGS}

        def replay(ename, eng):
            for waits, fn, inc in ops_now[ename]:
                for sk, v in waits:
                    eng.wait_ge(sems[sk], v)
                if fn is None:
                    continue
                ins = fn(eng)
                ins.then_inc(sems[inc[0]], inc[1])

        with nc.Block() as block:
            @block.tensor
            def _(e):
                replay("pe", e)

            @block.scalar
            def _(e):
                replay("act", e)

            @block.vector
            def _(e):
                replay("dve", e)

            @block.gpsimd
            def _(e):
                replay("pool", e)

            @block.sync
            def _(e):
                replay("sp", e)


class Ring:
    def __init__(self, K, name, n, shape, dt, psum=False, dma=False):
        self.n = n
        self.t = []
        self.b = []
        self.s = []
        for i in range(n):
            if psum:
                self.t.append(K.psum(f"{name}{i}", shape, dt))
            else:
                self.t.append(K.sb(f"{name}{i}", shape, dt))
            self.b.append(Buf(f"{name}{i}", psum=psum))
            self.s.append(K.S.dma_sem(f"{name}{i}_{K.uid()}") if dma else None)
        self.i = 0

    def next(self):
        i = self.i % self.n
        self.i += 1
        return self.t[i], self.b[i], self.s[i]


class K:
    def __init__(self, nc, S, stack):
        self.nc = nc
        self.S = S
        self.stack = stack
        self.pstack = stack
        self._uid = 0

    def uid(self):
        self._uid += 1
        return self._uid

    def sb(self, name, shape, dt):
        return self.pstack.enter_context(self.nc.sbuf_tensor(f"{name}_{self.uid()}", shape, dt))

    def psum(self, name, shape, dt):
        return self.pstack.enter_context(self.nc.psum_tensor(f"{name}_{self.uid()}", shape, dt))


def build(stop_after=None, dbg_shape=None):
    nc = bass.Bass("TRN2", target_bir_lowering=False)

    def din(name, shape, dt=F32):
        return nc.dram_tensor(name, shape, dt, kind="ExternalInput").ap()

    x = din("x", [T, D])
    ccol = din("ccol", [P, 8])
    w_ada = din("w_ada", [D, 3 * D])
    b_ada = din("b_ada", [1, 3 * D])
    norm_w = din("norm_w", [1, D])
    w_in = din("w_in", [D, D_IN])
    qnw = din("qnw", [1, P])
    knw = din("knw", [1, P])
    utab = din("utab", [8, P, 1024])
    bias31 = din("bias31", [1, 8])
    convw = din("convw", [P, 16, 4])
    convb = din("convb", [P, 16])
    b_ig = din("b_ig", [4, 1])
    b_fg = din("b_fg", [4, 1])
    mlnw = din("mlnw", [1, D])
    w_att = din("w_att", [D, D])
    w_ml = din("w_ml", [D, D])
    w_out = din("w_out", [D, D])
    c_ident = din("c_ident", [P, P])
    c_caus = din("c_caus", [P, P])
    c_ind = din("c_ind", [16, 16 * P])
    out = nc.dram_tensor("out", [T, D], F32, kind="ExternalOutput").ap()
    dbg = None
    if dbg_shape is not None:
        dbg = nc.dram_tensor("dbg", list(dbg_shape), F32, kind="ExternalOutput").ap()
    yaT_d = nc.dram_tensor("yaT_d", [8, P, T], BF16, kind="Internal").ap()
    ymT_d = nc.dram_tensor("ymT_d", [8, P, T], BF16, kind="Internal").ap()
    sga_d = nc.dram_tensor("sga_d", [8, P, T], F32, kind="Internal").ap()
    sgm_d = nc.dram_tensor("sgm_d", [8, P, T], F32, kind="Internal").ap()

    w_in_r = w_in.rearrange("(kc p) n -> p kc n", p=P)

    with contextlib.ExitStack() as st:
        S = Sched(nc, st)
        k = K(nc, S, st)
        op = S.op
        ident = k.sb("ident", [P, P], F32)
        ident_bf = k.sb("ident_bf", [P, P], BF16)
        ones_bf = k.sb("ones_bf", [P, P], BF16)
        ones_f = k.sb("ones_f", [P, P], F32)
        caus = k.sb("caus", [P, P], F32)
        ind_bf = k.sb("ind_bf", [16, 16 * P], BF16)
        gate_half = k.sb("gate_half", [P, D], F32)
        mlw_bc = k.sb("mlw_bc", [P, D], F32)
        qw_bc = k.sb("qw_bc", [P, P], F32)
        kw_bc = k.sb("kw_bc", [P, P], F32)
        b31_bc = k.sb("b31_bc", [P, 8], F32)
        convw_sb = k.sb("convw_sb", [P, 16, 4], F32)
        convb_sb = k.sb("convb_sb", [P, 16], F32)
        wtok = k.sb("wtok", [P, 128], F32)
        cltok = k.sb("cltok", [P, 128], F32)
        decbc = k.sb("decbc", [P, 128], F32)
        Bc = Buf("consts")
        Bgate = Buf("gate_half")
        Bml = Buf("mlgates")
        hst = st.enter_context(contextlib.ExitStack())
        k.pstack = hst
        hT = k.sb("hT", [P, NKC, T], BF16)
        hTb = [Buf(f"hT{g}") for g in range(NG)]
        k.pstack = st
        dsem = S.dma_sem("const")
        dsem_out = S.dma_sem("dbgout")

        cl = [(ident[:], c_ident[:, :]), (caus[:], c_caus[:, :]),
              (mlw_bc[:], mlnw[0:1, :].partition_broadcast(P)), (qw_bc[:], qnw[0:1, :].partition_broadcast(P)),
              (kw_bc[:], knw[0:1, :].partition_broadcast(P)), (b31_bc[:], bias31[0:1, :].partition_broadcast(P)),
              (convw_sb[:], convw[:, :, :]), (convb_sb[:], convb[:, :])]
        S.dma_batch("sp", dsem, [((lambda e, d_=d_, s_=s_: e.dma_start(out=d_, in_=s_)), [], [Bc]) for d_, s_ in cl])
        op("dve", lambda e: e.tensor_copy(ident_bf[:], ident[:]), reads=[Bc], writes=[Bc])
        op("pool", lambda e: e.dma_start(out=ind_bf[:], in_=c_ind[:, :]), writes=[Bc], dma=S.dma_sem("cind"))
        op("dve", lambda e: e.memset(ones_bf[:], 1.0), writes=[Bc])
        op("dve", lambda e: e.memset(ones_f[:], 1.0), writes=[Bc])

        def finish(src_fn=None):
            toks = []
            if src_fn is not None:
                toks = src_fn()
            S.ops["sp"].append((tuple(toks), None, None))
            S.emit()

        def dump(dst, src, bufs):
            return op("sp", lambda e: e.dma_start(out=dst, in_=src), reads=bufs, dma=dsem_out)

        with contextlib.ExitStack() as ph:
            k.pstack = ph
            adarow = k.sb("adarow", [P, 3 * D], F32)
            Bada = Buf("adarow")
            ccol_sb = k.sb("ccol_sb", [P, 8], F32)
            cbc = k.sb("cbc", [P, 8, P], F32)
            nwbc = k.sb("nwbc", [P, D], F32)
            Abc = k.sb("Abc", [P, D], F32)
            junk = k.sb("junk", [P, D], F32)
            ssq = k.sb("ssq", [P, 32], F32)
            rs = k.sb("rs", [P, 32], F32)
            Bcc = Buf("ccol")
            Bcbc = Buf("cbc")
            Bnw = Buf("nwbc")
            BA = Buf("Abc")
            Bjunk = Buf("junk")
            wst = Ring(k, "wada", 4, [P, 8, 256], F32, dma=True)
            pa = Ring(k, "pa", 2, [P, 512], F32, psum=True)
            ptr = Ring(k, "ptr", 6, [P, 512], F32, psum=True)
            xr = Ring(k, "xt", 4, [P, D], F32, dma=True)
            xnr = Ring(k, "xn", 4, [P, D], F32)
            op("sp", lambda e: e.dma_start(out=ccol_sb[:], in_=ccol[:, :]), writes=[Bcc], dma=S.dma_sem("ccol"))
            op("sp", lambda e: e.dma_start(out=adarow[:], in_=b_ada[0:1, :].partition_broadcast(P)), writes=[Bada], dma=S.dma_sem("bada"))
            op("sp", lambda e: e.dma_start(out=nwbc[:], in_=norm_w[0:1, :].partition_broadcast(P)), writes=[Bnw], dma=S.dma_sem("nwbc"))
            op("dve", lambda e: e.tensor_copy(cbc[:], ccol_sb[:].unsqueeze(2).to_broadcast([P, 8, P])), reads=[Bcc], writes=[Bcbc])
            w_ada_r = w_ada.rearrange("(kc p) n -> p kc n", p=P)
            xpre = []
            for tt in range(4):
                xt, xb, xs = xr.next()
                op("sp", lambda e, xt=xt, tt=tt: e.dma_start(out=xt[:], in_=x[tt * P:(tt + 1) * P, :]), writes=[xb], dma=xs)
                xpre.append((xt, xb, xs))
            for cg in range(12):
                wt, wb, ws = wst.next()
                op("sp" if cg % 2 == 0 else "act", lambda e, wt=wt, cg=cg: e.dma_start(out=wt[:], in_=w_ada_r[:, :, cg * 256:(cg + 1) * 256]), writes=[wb], dma=ws)
                pt_, pb, _ = pa.next()
                for kc in range(8):
                    op("pe", lambda e, pt_=pt_, wt=wt, kc=kc: e.matmul(pt_[:, 0:256], lhsT=cbc[:, kc, :], rhs=wt[:, kc, :], start=(kc == 0), stop=(kc == 7)),
                       reads=[Bcbc, wb], writes=[pb])
                op("dve", lambda e, pt_=pt_, cg=cg: e.tensor_tensor(adarow[:, cg * 256:(cg + 1) * 256], pt_[:, 0:256], adarow[:, cg * 256:(cg + 1) * 256], ALU.add),
                   reads=[pb, Bada], writes=[Bada])
            op("dve", lambda e: e.scalar_tensor_tensor(out=Abc[:], in0=adarow[:, D:2 * D], scalar=1.0, in1=nwbc[:], op0=ALU.add, op1=ALU.mult),
               reads=[Bada, Bnw], writes=[BA])
            op("dve", lambda e: e.tensor_scalar(gate_half[:], adarow[:, 2 * D:3 * D], 0.5, None, ALU.mult), reads=[Bada], writes=[Bgate])
            Bsst = [Buf(f"ssq{t}") for t in range(32)]
            lagq = []

            def stage2(tt, banks):
                g = tt // 4
                for half, (pt_, pb) in enumerate(banks):
                    op("act", lambda e, pt_=pt_, half=half, tt=tt: e.activation(hT[:, half * 4:half * 4 + 4, tt * P:(tt + 1) * P],
                                                                               pt_[:, 0:512].rearrange("p (a b) -> p a b", b=P), AF.Copy),
                       reads=[pb], writes=[hTb[g]])

            for tt in range(32):
                if tt < 4:
                    xt, xb, xs = xpre[tt]
                else:
                    xt, xb, xs = xr.next()
                    op("sp", lambda e, xt=xt, tt=tt: e.dma_start(out=xt[:], in_=x[tt * P:(tt + 1) * P, :]), writes=[xb], dma=xs)
                op("act", lambda e, xt=xt, tt=tt: e.activation(junk[:], xt[:], AF.Square, accum_out=ssq[:, tt:tt + 1]), reads=[xb], writes=[Bjunk, Bsst[tt]])
                op("act", lambda e, tt=tt: e.activation(rs[:, tt:tt + 1], ssq[:, tt:tt + 1], AF.Sqrt, bias=EPS, scale=1.0 / D), reads=[Bsst[tt]], writes=[Bsst[tt]])
                op("dve", lambda e, tt=tt: e.reciprocal(rs[:, tt:tt + 1], rs[:, tt:tt + 1]), reads=[Bsst[tt]], writes=[Bsst[tt]])
                xn, xnb, _ = xnr.next()
                op("dve", lambda e, xn=xn, xt=xt, tt=tt: e.scalar_tensor_tensor(out=xn[:], in0=xt[:], scalar=rs[:, tt:tt + 1], in1=Abc[:], op0=ALU.mult, op1=ALU.mult),
                   reads=[xb, Bsst[tt], BA], writes=[xnb])
                op("pool", lambda e, xn=xn: e.tensor_tensor(xn[:], xn[:], adarow[:, 0:D], ALU.add), reads=[xnb, Bada], writes=[xnb])
                banks = []
                for half in range(2):
                    pt_, pb, _ = ptr.next()
                    for j in range(4):
                        kc = half * 4 + j
                        op("pe", lambda e, pt_=pt_, xn=xn, kc=kc, j=j: e.transpose(pt_[:, j * P:(j + 1) * P], xn[:, kc * P:(kc + 1) * P], ident[:]),
                           reads=[xnb, Bc], writes=[pb])
                    banks.append((pt_, pb))
                lagq.append((tt, banks))
                if len(lagq) > 2:
                    stage2(*lagq.pop(0))
            while lagq:
                stage2(*lagq.pop(0))
            S.barrier()
            if stop_after == "ph1":
                tmpf = k.sb("tmpf", [P, T], F32)
                Bt = Buf("tmpf")
                toks = []
                for kc in range(8):
                    op("dve", lambda e, kc=kc: e.tensor_copy(tmpf[:], hT[:, kc, :]), reads=hTb, writes=[Bt])
                    toks.append(dump(dbg[kc * P:(kc + 1) * P, :], tmpf[:], [Bt]))
                toks.append(dump(dbg[8 * P:9 * P, 0:D], gate_half[:], [Bgate]))
                S.ops["sp"].append((tuple(toks), None, None))
                S.emit()
                return nc
            S.emit()
        with contextlib.ExitStack() as ph:
            k.pstack = ph
            Wg = k.sb("Wg", [P, 8, 8], BF16)
            BWg = Buf("Wg")
            wsem = S.dma_sem("wg")
            op("pool", lambda e: e.dma_start(out=Wg[:], in_=w_in_r[:, :, COL_IG:COL_IG + 8]), writes=[BWg], dma=wsem)
            Wga = k.sb("Wga", [P, 8, D], BF16)
            Wgm = k.sb("Wgm", [P, 8, D], BF16)
            BWga = Buf("Wga")
            BWgm = Buf("Wgm")
            wsem2 = S.dma_sem("wga")
            wsem3 = S.dma_sem("wgm")
            op("pool", lambda e: e.dma_start(out=Wga[:], in_=w_in_r[:, :, COL_GA:COL_GA + D]), writes=[BWga], dma=wsem2)
            op("pool", lambda e: e.dma_start(out=Wgm[:], in_=w_in_r[:, :, COL_GM:COL_GM + D]), writes=[BWgm], dma=wsem3)
            bi_sb = k.sb("bi_sb", [4, 1], F32)
            bf_sb = k.sb("bf_sb", [4, 1], F32)
            negbf = k.sb("negbf", [4, 1], F32)
            Bb = Buf("gbias")
            S.dma_batch("sp", S.dma_sem("gb"), [((lambda e: e.dma_start(out=bi_sb[:], in_=b_ig[:, :])), [], [Bb]),
                                                ((lambda e: e.dma_start(out=bf_sb[:], in_=b_fg[:, :])), [], [Bb])])
            op("dve", lambda e: e.tensor_scalar(negbf[:], bf_sb[:], -1.0, None, ALU.mult), reads=[Bb], writes=[Bb])
            bufA = k.sb("bufA", [4, T], F32)
            bufB = k.sb("bufB", [4, T], F32)
            bufC = k.sb("bufC", [4, T], F32)
            BA_, BB_, BC_ = Buf("bufA"), Buf("bufB"), Buf("bufC")
            cm = k.sb("cm", [4, 32], F32)
            Mx = k.sb("Mx", [4, 32], F32)
            Mp = k.sb("Mp", [4, 32], F32)
            dec = k.sb("dec", [4, 32], F32)
            X4 = k.sb("X4", [4, 32, 4], F32)
            Bsm = Buf("gsmall")
            pg = Ring(k, "pg", 2, [P, 512], F32, psum=True)
            ptok = k.psum("ptok", [P, 512], F32)
            Bptok = Buf("ptok", psum=True)
            pdec = k.psum("pdec", [P, 512], F32)
            Bpdec = Buf("pdec", psum=True)
            for g in range(NG):
                pf, pfb, _ = pg.next()
                for kc in range(8):
                    op("pe", lambda e, pf=pf, kc=kc, g=g: e.matmul(pf[0:4, :], lhsT=Wg[:, kc, 4:8], rhs=hT[:, kc, g * GS:(g + 1) * GS], start=(kc == 0), stop=(kc == 7)),
                       reads=[BWg, hTb[g]], writes=[pfb])
                op("act", lambda e, pf=pf, g=g: e.activation(bufA[:, g * GS:(g + 1) * GS], pf[0:4, :], AF.Exp, bias=negbf[:], scale=-1.0), reads=[pfb, Bb], writes=[BA_])
                pi, pib, _ = pg.next()
                for kc in range(8):
                    op("pe", lambda e, pi=pi, kc=kc, g=g: e.matmul(pi[0:4, :], lhsT=Wg[:, kc, 0:4], rhs=hT[:, kc, g * GS:(g + 1) * GS], start=(kc == 0), stop=(kc == 7)),
                       reads=[BWg, hTb[g]], writes=[pib])
                op("act", lambda e, pi=pi, g=g: e.activation(bufC[:, g * GS:(g + 1) * GS], pi[0:4, :], AF.Identity, bias=bi_sb[:]), reads=[pib, Bb], writes=[BC_])
            op("act", lambda e: e.activation(bufA[:], bufA[:], AF.Ln, bias=1.0, scale=1.0), reads=[BA_], writes=[BA_])
            op("dve", lambda e: e.tensor_tensor_scan(bufB[:], bufA[:], bufA[:], 0.0, ALU.add, ALU.max), reads=[BA_], writes=[BB_])
            op("dve", lambda e: e.tensor_tensor(bufC[:], bufC[:], bufB[:], ALU.add), reads=[BC_, BB_], writes=[BC_])
            op("dve", lambda e: e.tensor_reduce(out=cm[:], in_=bufC[:].rearrange("p (c l) -> p c l", l=P), axis=AX.X, op=ALU.max), reads=[BC_], writes=[Bsm])
            op("dve", lambda e: e.tensor_tensor_scan(Mx[:], cm[:], cm[:], 0.0, ALU.max, ALU.max), reads=[Bsm], writes=[Bsm])
            op("dve", lambda e: e.memset(Mp[:, 0:1], 0.0), reads=[Bsm], writes=[Bsm])
            op("dve", lambda e: e.tensor_copy(Mp[:, 1:32], Mx[:, 0:31]), reads=[Bsm], writes=[Bsm])
            op("dve", lambda e: e.tensor_tensor(dec[:], Mp[:], Mx[:], ALU.subtract), reads=[Bsm], writes=[Bsm])
            op("act", lambda e: e.activation(dec[:], dec[:], AF.Exp), reads=[Bsm], writes=[Bsm])
            v3 = lambda t_: t_[:].rearrange("p (c l) -> p c l", l=P)
            Mb = lambda: Mx[:].unsqueeze(2).to_broadcast([4, 32, P])
            op("dve", lambda e: e.tensor_tensor(v3(bufA), v3(bufC), Mb(), ALU.subtract), reads=[BC_, Bsm, BA_], writes=[BA_])
            op("act", lambda e: e.activation(bufA[:], bufA[:], AF.Exp), reads=[BA_], writes=[BA_])
            op("dve", lambda e: e.tensor_tensor(v3(bufB), v3(bufB), Mb(), ALU.subtract), reads=[BB_, Bsm], writes=[BB_])
            op("act", lambda e: e.activation(bufB[:], bufB[:], AF.Exp), reads=[BB_], writes=[BB_])
            for c in range(32):
                op("pe", lambda e, c=c: e.transpose(ptok[:, c * 4:(c + 1) * 4], bufA[0:4, c * P:(c + 1) * P], ident[0:4, 0:4]), reads=[BA_, Bc], writes=[Bptok])
                op("pe", lambda e, c=c: e.transpose(ptok[:, 128 + c * 4:128 + (c + 1) * 4], bufB[0:4, c * P:(c + 1) * P], ident[0:4, 0:4]), reads=[BB_, Bc], writes=[Bptok])
            op("dve", lambda e: e.tensor_copy(wtok[:], ptok[:, 0:128]), reads=[Bptok], writes=[Bml])
            op("dve", lambda e: e.tensor_copy(cltok[:], ptok[:, 128:256]), reads=[Bptok], writes=[Bml])
            op("dve", lambda e: e.tensor_tensor(X4[:], dec[:].unsqueeze(2).to_broadcast([4, 32, 4]), ident[0:4, 0:4].unsqueeze(1).to_broadcast([4, 32, 4]), ALU.mult),
               reads=[Bsm, Bc], writes=[Bsm])
            op("pe", lambda e: e.matmul(pdec[:, 0:128], lhsT=ones_f[0:4, :], rhs=X4[:].rearrange("p c h -> p (c h)"), start=True, stop=True), reads=[Bsm, Bc], writes=[Bpdec])
            op("dve", lambda e: e.tensor_copy(decbc[:], pdec[:, 0:128]), reads=[Bpdec], writes=[Bml])
            pgg = Ring(k, "pgg", 4, [P, 512], F32, psum=True)
            sgr = Ring(k, "sg", 4, [P, 512], F32, dma=True)
            Bsga = Buf("sga_d")
            for g in range(NG):
                for cc in range(8):
                    for W_, Wb_, dst in ((Wga, BWga, sga_d), (Wgm, BWgm, sgm_d)):
                        pb_, pbb, _ = pgg.next()
                        for kc in range(8):
                            op("pe", lambda e, pb_=pb_, W_=W_, kc=kc, cc=cc, g=g: e.matmul(pb_[:], lhsT=W_[:, kc, cc * P:(cc + 1) * P], rhs=hT[:, kc, g * GS:(g + 1) * GS],
                                                                                        start=(kc == 0), stop=(kc == 7)),
                               reads=[Wb_, hTb[g]], writes=[pbb])
                        sg, sgb, sgs = sgr.next()
                        op("act", lambda e, sg=sg, pb_=pb_: e.activation(sg[:], pb_[:], AF.Tanh, scale=0.5), reads=[pbb], writes=[sgb])
                        op("sp", lambda e, sg=sg, dst=dst, cc=cc, g=g: e.dma_start(out=dst[cc, :, g * GS:(g + 1) * GS], in_=sg[:]), reads=[sgb], dma=sgs)
            S.barrier()
            if stop_after == "ph2":
                toks = []
                toks.append(dump(dbg[0:P, 0:128], wtok[:], [Bml]))
                toks.append(dump(dbg[0:P, 128:256], cltok[:], [Bml]))
                toks.append(dump(dbg[0:P, 256:384], decbc[:], [Bml]))
                tmp = k.sb("tmpd", [P, 512], F32)
                Bt = Buf("tmpd")
                ds2 = S.dma_sem("dbg2")
                for i, (src, cc, g) in enumerate(((sga_d, 3, 5), (sgm_d, 6, 2))):
                    op("sp", lambda e, src=src, cc=cc, g=g: e.dma_start(out=tmp[:], in_=src[cc, :, g * GS:(g + 1) * GS]), writes=[Bt], dma=ds2)
                    toks.append(dump(dbg[P * (i + 1):P * (i + 2), 0:512], tmp[:], [Bt]))
                S.ops["sp"].append((tuple(toks), None, None))
                S.emit()
                return nc
            S.emit()
        with contextlib.ExitStack() as ph:
            k.pstack = ph
            QKT = k.sb("QKT", [P, 4, T], BF16)
            Vall = k.sb("Vall", [P, 32, 256], BF16)
            kmS = k.sb("kmS", [P, 64], F32)
            tmpk = k.sb("tmpk", [P, 2, 16], F32)
            kmT = k.sb("kmT", [P, 2, 16], BF16)
            kmL = k.sb("kmL", [P, 2, 16], BF16)
            W3 = k.sb("W3", [P, 8, 3, 256], BF16)
            BW3 = Buf("W3")
            w3sem = S.dma_sem("w3")
            BQK = [Buf(f"QK{g}") for g in range(NG)]
            BV = [Buf(f"V{g}") for g in range(NG)]
            Bkm = Buf("km")
            Wzr = Ring(k, "Wz", 2, [P, 8, P], BF16, dma=True)
            Utr = Ring(k, "Ut", 2, [P, 1024], F32, dma=True)
            Bya = Buf("yaT_d")
            for hp in range(4):
                if hp == 0:
                    S.dma_batch("pool", w3sem, [((lambda e, j=j, col=col: e.dma_start(out=W3[:, :, j, :], in_=w_in_r[:, :, col:col + 256])), [], [BW3])
                                                for j, col in enumerate((COL_QA, COL_KA, COL_VA))])
                with contextlib.ExitStack() as sa:
                    k.pstack = sa
                    bX = Ring(k, "bX", 2, [P, 512], F32, psum=True)
                    bY = Ring(k, "bY", 2, [P, 512], F32, psum=True)
                    bZ = Ring(k, "bZ", 2, [P, 1024], BF16, psum=True)
                    pkm = k.psum("pkm", [P, 512], F32)
                    Bpkm = Buf("pkm", psum=True)
                    sqr = Ring(k, "sq", 2, [P, 512], F32)
                    s4r = Ring(k, "ssq4", 2, [P, 4], F32)
                    qknr = Ring(k, "qkn", 3, [P, 512], BF16)
                    prevT = None

                    def emitT(tt, qkn, qknb):
                        g = tt // 4
                        bz, bzb, _ = bZ.next()
                        for j in range(4):
                            op("pe", lambda e, bz=bz, qkn=qkn, j=j: e.transpose(bz[:, j * P:(j + 1) * P], qkn[:, j * P:(j + 1) * P], ident_bf[:]), reads=[qknb, Bc], writes=[bzb])
                        op("act", lambda e, bz=bz, tt=tt: e.activation(QKT[:, 0:4, tt * P:(tt + 1) * P], bz[:, 0:512].rearrange("p (a b) -> p a b", b=P), AF.Copy),
                           reads=[bzb], writes=[BQK[g]])
                        for hh in range(2):
                            op("pe", lambda e, qkn=qkn, hh=hh, tt=tt: e.matmul(pkm[:, hh * 32 + tt:hh * 32 + tt + 1], lhsT=qkn[:, (2 + hh) * P:(3 + hh) * P], rhs=ones_bf[:, 0:1],
                                                                            start=True, stop=True),
                               reads=[qknb, Bc], writes=[Bpkm])

                    for tt in range(32):
                        g = tt // 4
                        bx, bxb, _ = bX.next()
                        for j in range(2):
                            for kc in range(8):
                                op("pe", lambda e, bx=bx, j=j, kc=kc, tt=tt: e.matmul(bx[:, j * 256:(j + 1) * 256], lhsT=hT[:, kc, tt * P:(tt + 1) * P], rhs=W3[:, kc, j, :],
                                                                                    start=(kc == 0), stop=(kc == 7)),
                                   reads=[hTb[g], BW3], writes=[bxb])
                        by, byb, _ = bY.next()
                        for kc in range(8):
                            op("pe", lambda e, by=by, kc=kc, tt=tt: e.matmul(by[:, 0:256], lhsT=hT[:, kc, tt * P:(tt + 1) * P], rhs=W3[:, kc, 2, :], start=(kc == 0), stop=(kc == 7)),
                               reads=[hTb[g], BW3], writes=[byb])
                        sq, sqb, _ = sqr.next()
                        s4, s4b, _ = s4r.next()
                        op("act", lambda e, sq=sq, bx=bx: e.activation(sq[:], bx[:], AF.Square), reads=[bxb], writes=[sqb])
                        op("dve", lambda e, sq=sq, s4=s4: e.tensor_reduce(out=s4[:], in_=sq[:].rearrange("p (a b) -> p a b", b=P), axis=AX.X, op=ALU.add), reads=[sqb], writes=[s4b])
                        op("act", lambda e, s4=s4: e.activation(s4[:], s4[:], AF.Sqrt, bias=EPS, scale=1.0 / P), reads=[s4b], writes=[s4b])
                        op("dve", lambda e, s4=s4: e.reciprocal(s4[:], s4[:]), reads=[s4b], writes=[s4b])
                        qkn, qknb, _ = qknr.next()
                        for j in range(4):
                            wbc = qw_bc if j < 2 else kw_bc
                            op("dve", lambda e, qkn=qkn, bx=bx, s4=s4, j=j, wbc=wbc: e.scalar_tensor_tensor(out=qkn[:, j * P:(j + 1) * P], in0=bx[:, j * P:(j + 1) * P], scalar=s4[:, j:j + 1],
                                                                                                          in1=wbc[:], op0=ALU.mult, op1=ALU.mult),
                               reads=[bxb, s4b, Bc], writes=[qknb])
                        op("act", lambda e, by=by, tt=tt: e.activation(Vall[:, tt, :], by[:, 0:256], AF.Copy), reads=[byb], writes=[BV[g]])
                        if prevT is not None:
                            emitT(*prevT)
                        prevT = (tt, qkn, qknb)
                    emitT(*prevT)
                    op("dve", lambda e: e.tensor_copy(kmS[:], pkm[:, 0:64]), reads=[Bpkm], writes=[Bkm])
                    kv = lambda i: kmS[:].rearrange("p (h b two) -> p h b two", h=2, two=2)[:, :, :, i]
                    op("dve", lambda e: e.tensor_tensor(tmpk[:], kv(0), kv(1), ALU.add), reads=[Bkm], writes=[Bkm])
                    op("dve", lambda e: e.tensor_scalar(tmpk[:], tmpk[:], 1.0 / 256.0, None, ALU.mult), reads=[Bkm], writes=[Bkm])
                    op("dve", lambda e: e.tensor_copy(kmT[:], tmpk[:]), reads=[Bkm], writes=[Bkm])
                    op("dve", lambda e: e.tensor_tensor(kmL[:], tmpk[:], kmT[:], ALU.subtract), reads=[Bkm], writes=[Bkm])
                    S.barrier()
                    S.emit()
                with contextlib.ExitStack() as sbk:
                    k.pstack = sbk
                    bS = Ring(k, "bS", 3, [P, 512], F32, psum=True)
                    bO = Ring(k, "bO", 2, [P, 512], F32, psum=True)
                    bSumr = Ring(k, "bSum", 2, [P, 512], F32, psum=True)
                    pnm = k.psum("pnm", [P, 1024], BF16)
                    Bpnm = Buf("pnm", psum=True)
                    tzr = Ring(k, "tz", 2, [P, 512], F32)
                    zsr = Ring(k, "zs", 3, [P, 512], F32)
                    gsbr = Ring(k, "gsb", 2, [P, 16], F32)
                    gallr = Ring(k, "gall", 2, [P, 64], F32)
                    t8r = Ring(k, "top8", 2, [P, 8], F32)
                    nmr = Ring(k, "nm", 8, [P, 16], BF16)
                    nmTr = Ring(k, "nmT", 3, [16, 512], BF16)
                    PTr = Ring(k, "PT", 4, [P, 512], BF16)
                    tmpdr = Ring(k, "tmpd", 2, [P, 512], F32)
                    recr = Ring(k, "rec", 2, [P, 512], F32)
                    osbr = Ring(k, "osb", 2, [P, 512], F32)
                    yagr = Ring(k, "yag", 2, [P, 512], BF16, dma=True)
                    heads = []
                    for hh in range(2):
                        hd = 2 * hp + hh
                        Wz, Wzb, Wzs = Wzr.next()
                        op("pool", lambda e, Wz=Wz, hd=hd: e.dma_start(out=Wz[:], in_=w_in_r[:, :, COL_ZA + hd * P:COL_ZA + (hd + 1) * P]), writes=[Wzb], dma=Wzs)
                        Ut, Utb, Uts = Utr.next()
                        op("sp", lambda e, Ut=Ut, hd=hd: e.dma_start(out=Ut[:], in_=utab[hd, :, :]), writes=[Utb], dma=Uts)
                        heads.append((hd, Wz, Wzb, Ut, Utb))
                    if hp + 1 < 4:
                        S.dma_batch("pool", w3sem, [((lambda e, j=j, col=col, hp=hp: e.dma_start(out=W3[:, :, j, :], in_=w_in_r[:, :, col + (hp + 1) * 256:col + (hp + 2) * 256])), [], [BW3])
                                                    for j, col in enumerate((COL_QA, COL_KA, COL_VA))])
                    LOOK = 2
                    for hh in range(2):
                        hd, Wz, Wzb, Ut, Utb = heads[hh]
                        gst = {}

                        def proA(g, hh=hh, Wz=Wz, Wzb=Wzb, gst=gst):
                            st_ = {}
                            bz_, bzb_, _ = bS.next()
                            for kc in range(8):
                                op("pe", lambda e, bz_=bz_, kc=kc, Wz=Wz, g=g: e.matmul(bz_[:], lhsT=Wz[:, kc, :], rhs=hT[:, kc, g * GS:(g + 1) * GS], start=(kc == 0), stop=(kc == 7)),
                                   reads=[Wzb, hTb[g]], writes=[bzb_])
                            tz, tzb, _ = tzr.next()
                            zs, zsb, _ = zsr.next()
                            op("act", lambda e, tz=tz, bz_=bz_: e.activation(tz[:], bz_[:], AF.Tanh, scale=0.5), reads=[bzb_], writes=[tzb])
                            op("dve", lambda e, zs=zs, tz=tz, bz_=bz_: e.scalar_tensor_tensor(out=zs[:], in0=tz[:], scalar=1.0, in1=bz_[:], op0=ALU.add, op1=ALU.mult),
                               reads=[tzb, bzb_], writes=[zsb])
                            st_["zs"] = (zs, zsb)
                            st_["nms"] = []
                            if g >= 2:
                                pg_, pgb_, _ = bS.next()
                                for qi in range(4):
                                    tq = 4 * g + qi
                                    op("pe", lambda e, pg_=pg_, qi=qi, tq=tq, hh=hh: e.matmul(pg_[:, qi * 16:(qi + 1) * 16], lhsT=QKT[:, hh, tq * P:(tq + 1) * P], rhs=kmT[:, hh, :], start=True, stop=False),
                                       reads=[BQK[g], Bkm], writes=[pgb_])
                                    op("pe", lambda e, pg_=pg_, qi=qi, tq=tq, hh=hh: e.matmul(pg_[:, qi * 16:(qi + 1) * 16], lhsT=QKT[:, hh, tq * P:(tq + 1) * P], rhs=kmL[:, hh, :], start=False, stop=True),
                                       reads=[BQK[g], Bkm], writes=[pgb_])
                                gall, gallb, _ = gallr.next()
                                op("dve", lambda e, gall=gall, pg_=pg_: e.tensor_copy(gall[:], pg_[:, 0:64]), reads=[pgb_], writes=[gallb])
                                for qi in range(4):
                                    qblk = 2 * g + qi // 2
                                    gsb, gsbb, _ = gsbr.next()
                                    t8, t8b, _ = t8r.next()
                                    nm, nmb, _ = nmr.next()
                                    op("dve", lambda e, gsb=gsb: e.memset(gsb[:], -1.0e30), writes=[gsbb])
                                    op("dve", lambda e, gsb=gsb, qi=qi, qblk=qblk, gall=gall: e.tensor_copy(gsb[:, 0:qblk], gall[:, qi * 16:qi * 16 + qblk]), reads=[gallb], writes=[gsbb])
                                    op("dve", lambda e, gsb=gsb, t8=t8: e.max(t8[:], gsb[:]), reads=[gsbb], writes=[t8b])
                                    op("dve", lambda e, nm=nm: e.memset(nm[:], 0.0), writes=[nmb])
                                    op("dve", lambda e, nm=nm, gsb=gsb, t8=t8, qblk=qblk: e.tensor_scalar(nm[:, 0:qblk], gsb[:, 0:qblk], t8[:, 2:3], NEGV, ALU.is_lt, ALU.mult),
                                       reads=[gsbb, t8b], writes=[nmb])
                                    st_["nms"].append((nm, nmb))
                            gst[g] = st_

                        def proB(g, gst=gst):
                            st_ = gst[g]
                            st_["nmT"] = (None, None)
                            if g >= 2:
                                for qi, (nm, nmb) in enumerate(st_["nms"]):
                                    op("pe", lambda e, nm=nm, qi=qi: e.transpose(pnm[0:16, qi * P:(qi + 1) * P], nm[:], ident_bf[:]), reads=[nmb, Bc], writes=[Bpnm])
                                nmT, nmTb, _ = nmTr.next()
                                op("act", lambda e, nmT=nmT: e.activation(nmT[:], pnm[0:16, 0:512], AF.Copy), reads=[Bpnm], writes=[nmTb])
                                st_["nmT"] = (nmT, nmTb)

                        def epilogue(g, bo, bob, bsum, bsumb, hd=hd, gst=gst):
                            zs, zsb = gst[g]["zs"]
                            rec, recb, _ = recr.next()
                            osb, osbb, _ = osbr.next()
                            yag, yagb, yags = yagr.next()
                            op("act", lambda e, osb=osb, bo=bo: e.activation(osb[:], bo[:], AF.Copy), reads=[bob], writes=[osbb])
                            op("dve", lambda e, rec=rec, bsum=bsum: e.reciprocal(rec[:], bsum[:]), reads=[bsumb], writes=[recb])
                            op("pool", lambda e, osb=osb, rec=rec: e.tensor_tensor(osb[:], osb[:], rec[:], ALU.mult), reads=[osbb, recb], writes=[osbb])
                            op("pool", lambda e, yag=yag, osb=osb, zs=zs: e.tensor_tensor(yag[:], osb[:], zs[:], ALU.mult), reads=[osbb, zsb], writes=[yagb])
                            op("sp", lambda e, yag=yag, g=g, hd=hd: e.dma_start(out=yaT_d[hd, :, g * GS:(g + 1) * GS], in_=yag[:]), reads=[yagb], writes=[Bya], dma=yags)

                        pending = []

                        def pop_pv(hh=hh):
                            g, kt, NKT, c0, Nq, pt, ptb, bo, bob, bsum, bsumb = pending.pop(0)
                            op("pe", lambda e, bo=bo, c0=c0, kt=kt, hh=hh, pt=pt, Nq=Nq, NKT=NKT: e.matmul(bo[:, c0:GS], lhsT=Vall[:, kt, hh * P:(hh + 1) * P], rhs=pt[:, 0:Nq], start=(kt == 0), stop=(kt == NKT - 1)),
                               reads=[BV[kt // 4], ptb], writes=[bob])
                            op("pe", lambda e, bsum=bsum, c0=c0, kt=kt, pt=pt, Nq=Nq, NKT=NKT: e.matmul(bsum[:, c0:GS], lhsT=ones_bf[:], rhs=pt[:, 0:Nq], start=(kt == 0), stop=(kt == NKT - 1)),
                               reads=[Bc, ptb], writes=[bsumb])
                            if kt == NKT - 1:
                                epilogue(g, bo, bob, bsum, bsumb)

                        proA(0)
                        proB(0)
                        proA(1)
                        for g in range(NG):
                            NKT = 4 * (g + 1)
                            bo, bob, _ = bO.next()
                            bsum, bsumb, _ = bSumr.next()
                            for kt in range(NKT):
                                if kt == max(2, NKT - 2) and g + 1 < NG:
                                    proB(g + 1)
                                nmT, nmTb = gst[g]["nmT"]
                                j = kt - 4 * g
                                c0 = P * j if j >= 1 else 0
                                Nq = GS - c0
                                masked = (g >= 2) and (kt // 2 <= 2 * g)
                                bs, bsb, _ = bS.next()
                                op("pe", lambda e, bs=bs, kt=kt, g=g, c0=c0, Nq=Nq, masked=masked, hh=hh: e.matmul(bs[:, 0:Nq], lhsT=QKT[:, 2 + hh, kt * P:(kt + 1) * P],
                                                                                                         rhs=QKT[:, hh, g * GS + c0:(g + 1) * GS], start=True, stop=(not masked)),
                                   reads=[BQK[kt // 4], BQK[g]], writes=[bsb])
                                if masked:
                                    bk = kt // 2
                                    op("pe", lambda e, bs=bs, bk=bk, nmT=nmT, c0=c0, Nq=Nq: e.matmul(bs[:, 0:Nq], lhsT=ind_bf[0:16, bk * P:(bk + 1) * P], rhs=nmT[0:16, c0:GS],
                                                                                                  start=False, stop=True),
                                       reads=[Bc, nmTb], writes=[bsb])
                                pt, ptb, _ = PTr.next()
                                if j >= -1:
                                    td, tdb, _ = tmpdr.next()
                                    u0 = 384 - P * j + c0
                                    op("dve", lambda e, td=td, bs=bs, u0=u0, Nq=Nq, Ut=Ut: e.scalar_tensor_tensor(out=td[:, 0:Nq], in0=bs[:, 0:Nq], scalar=ATT_SCALE, in1=Ut[:, u0:u0 + Nq],
                                                                                                        op0=ALU.mult, op1=ALU.add),
                                       reads=[bsb, Utb], writes=[tdb])
                                    op("act", lambda e, pt=pt, td=td, Nq=Nq: e.activation(pt[:, 0:Nq], td[:, 0:Nq], AF.Exp), reads=[tdb], writes=[ptb])
                                else:
                                    op("act", lambda e, pt=pt, bs=bs, hd=hd: e.activation(pt[:], bs[:], AF.Exp, bias=b31_bc[:, hd:hd + 1], scale=ATT_SCALE), reads=[bsb, Bc], writes=[ptb])
                                pending.append((g, kt, NKT, c0, Nq, pt, ptb, bo, bob, bsum, bsumb))
                                if len(pending) > LOOK:
                                    pop_pv()
                            if g + 2 < NG:
                                proA(g + 2)
                        while pending:
                            pop_pv()
                    S.barrier()
                    if stop_after == "ph3" and hp == 0:
                        toks = []
                        tb = k.sb("tmpd", [P, 1024], BF16)
                        tf = k.sb("tmpf", [P, 1024], F32)
                        Bt = Buf("tmpd")
                        Bt2 = Buf("tmpf")
                        ds2 = S.dma_sem("dbg2")
                        for hd in range(2):
                            for q4 in range(4):
                                op("sp", lambda e, hd=hd, q4=q4: e.dma_start(out=tb[:], in_=yaT_d[hd, :, q4 * 1024:(q4 + 1) * 1024]), reads=[Bya], writes=[Bt], dma=ds2)
                                op("dve", lambda e: e.tensor_copy(tf[:], tb[:]), reads=[Bt], writes=[Bt2])
                                toks.append(dump(dbg[hd * P:(hd + 1) * P, q4 * 1024:(q4 + 1) * 1024], tf[:], [Bt2]))
                        for i in range(4):
                            for q4 in range(4):
                                op("dve", lambda e, i=i, q4=q4: e.tensor_copy(tf[:], QKT[:, i, q4 * 1024:(q4 + 1) * 1024]), reads=BQK, writes=[Bt2])
                                toks.append(dump(dbg[(2 + i) * P:(3 + i) * P, q4 * 1024:(q4 + 1) * 1024], tf[:], [Bt2]))
                        S.ops["sp"].append((tuple(toks), None, None))
                        S.emit()
                        return nc
                    S.emit()
                k.pstack = ph
        k.pstack = hst
        Wa = k.sb("Wa", [P, 8, D], BF16)
        Wm = k.sb("Wm", [P, 8, D], BF16)
        Wo = k.sb("Wo", [P, 8, D], BF16)
        BWa, BWm, BWo = Buf("Wa"), Buf("Wm"), Buf("Wo")
        with contextlib.ExitStack() as ph:
            k.pstack = ph
            W5 = k.sb("W5", [P, 8, 5, 256], BF16)
            BW5s = [Buf(f"W5_{j}") for j in range(5)]
            w5sems = [S.dma_sem(f"w5_{j}") for j in range(5)]
            Cst = k.sb("Cst", [P, 2, 258], F32)
            Cbf = k.sb("Cbf", [P, 2, 258], BF16)
            BC = Buf("Cst")
            BCbf = Buf("Cbf")
            pre = [[k.sb(f"pre{w}{dc}", [P, 516], BF16) for dc in range(2)] for w in range(2)]
            Bpre = [[Buf(f"pre{w}{dc}") for dc in range(2)] for w in range(2)]
            thr_ = Ring(k, "th", 2, [P, 512], F32)
            QmTr = Ring(k, "QmT", 2, [P, 2, 512], BF16)
            KmTr = Ring(k, "KmT", 2, [P, 2, 512], BF16)
            Ktokr = Ring(k, "Ktok", 2, [P, 4, 256], BF16)
            Vpr = Ring(k, "Vp", 2, [P, 258], BF16)
            Smr = Ring(k, "Sm", 2, [P, P], BF16)
            thozr = Ring(k, "thoz", 2, [P, 512], F32)
            dnr = Ring(k, "dn", 2, [P, 1], F32)
            hm2r = Ring(k, "hm2", 2, [P, 4, 256], F32)
            szr = Ring(k, "sz", 2, [P, 4, 256], F32)
            ssqmr = Ring(k, "ssqm", 2, [P, 4], F32)
            junk2 = k.sb("junk2", [P, 256], F32)
            Bjunk2 = Buf("junk2")
            t1r = Ring(k, "t1", 2, [P, 256], F32)
            ymgr = Ring(k, "ymg", 2, [P, 256], BF16)
            ymTsr = Ring(k, "ymTs", 2, [P, 2, 512], BF16, dma=True)
            bQK = Ring(k, "bQK", 2, [P, 512], F32, psum=True)
            bT = k.psum("bT", [P, 1024], BF16)
            BbT = Buf("bT", psum=True)
            bT2 = bT
            BbT2 = BbT
            diag = k.sb("diag", [P, 16, P], BF16)
            Bdiag = Buf("diag")
            caus_s = k.sb("caus_s", [P, P], F32)
            op("dve", lambda e: e.tensor_scalar(caus_s[:], caus[:], ML_KSCALE, None, ALU.mult), reads=[Bc], writes=[Bdiag])
            bV = k.psum("bV", [P, 512], F32)
            BbV = Buf("bV", psum=True)
            bOZ = k.psum("bOZ", [P, 512], F32)
            BbOZ = Buf("bOZ", psum=True)
            bOut = k.psum("bOut", [P, 512], F32)
            BbOut = Buf("bOut", psum=True)
            bCp = [k.psum(f"bC{i}", [P, 512], F32) for i in range(2)]
            BbCp = [Buf(f"bC{i}", psum=True) for i in range(2)]
            Bym = Buf("ymT_d")
            for mh in range(4):
                for j, col in enumerate((COL_QM, COL_KM, COL_VM, COL_ZM, COL_OM)):
                    if j < 2 and mh > 0:
                        continue
                    op("pool", lambda e, j=j, col=col, mh=mh: e.dma_start(out=W5[:, :, j, :], in_=w_in_r[:, :, col + mh * 256:col + (mh + 1) * 256]), writes=[BW5s[j]], dma=w5sems[j])
                if mh == 0:
                    for W_, Wb_, src in ((Wa, BWa, w_att), (Wm, BWm, w_ml), (Wo, BWo, w_out)):
                        sm_ = S.dma_sem(f"w5_{k.uid()}")
                        op("pool", lambda e, W_=W_, src=src: e.dma_start(out=W_[:], in_=src.rearrange("(kc p) n -> p kc n", p=P)), writes=[Wb_], dma=sm_)
                if mh == 2:
                    op("dve", lambda e: e.tensor_scalar(Wa[:], Wa[:], 0.5, None, ALU.mult), reads=[BWa], writes=[BWa])
                op("dve", lambda e: e.memset(Cst[:], 0.0), writes=[BC])
                for w in range(2):
                    for dc in range(2):
                        op("dve", lambda e, w=w, dc=dc: e.memset(pre[w][dc][:, 0:3], 0.0), writes=[Bpre[w][dc]])
                for w in range(2):
                    for dc in range(2):
                        for j in range(4):
                            op("dve", lambda e, w=w, dc=dc, j=j, mh=mh: e.tensor_scalar(diag[:, (w * 2 + dc) * 4 + j, :], ident[:], convw_sb[:, w * 8 + mh * 2 + dc, j:j + 1], None, ALU.mult),
                               reads=[Bc], writes=[Bdiag])

                def qk_piece(g, w, dc, dstT, dstb, mh=mh):
                    bq, bqb, _ = bQK.next()
                    for kc in range(8):
                        op("pe", lambda e, bq=bq, kc=kc: e.matmul(bq[:], lhsT=W5[:, kc, w, dc * P:(dc + 1) * P], rhs=hT[:, kc, g * GS:(g + 1) * GS], start=(kc == 0), stop=(kc == 7)),
                           reads=[BW5s[w], hTb[g]], writes=[bqb])
                    pr = pre[w][dc]
                    prb = Bpre[w][dc]
                    cidx = w * 8 + mh * 2 + dc
                    didx = (w * 2 + dc) * 4
                    op("act", lambda e: e.activation(pr[:, 3:515], bq[:], AF.Copy), reads=[bqb], writes=[prb])

                    def part2():
                        bcv, bcvb, _ = bQK.next()
                        for j in range(4):
                            op("pe", lambda e, j=j: e.matmul(bcv[:], lhsT=diag[:, didx + j, :], rhs=pr[:, j:j + 512], start=(j == 0), stop=(j == 3)), reads=[Bdiag, prb], writes=[bcvb])
                        op("pool", lambda e: e.tensor_copy(pr[:, 0:3], pr[:, 512:515]), reads=[prb], writes=[prb])
                        op("act", lambda e: e.activation(dstT[:, dc, :], bcv[:], AF.Silu, bias=convb_sb[:, cidx:cidx + 1], scale=1.0), reads=[bcvb, Bc], writes=[dstb])
                    return part2

                pieces = [(w, dc) for w in range(2) for dc in range(2)]
                nxt = (QmTr.next(), KmTr.next())
                for (w, dc) in pieces:
                    dd = nxt[w]
                    qk_piece(0, w, dc, dd[0], dd[1])()

                def gend_dve(gs, c4, mh=mh):
                    hm2, hm2b, sz, szb, ssqm, ssqmb = gs["bufs"]
                    t1, t1b, _ = t1r.next()
                    ymg, ymgb, _ = ymgr.next()
                    op("dve", lambda e: e.scalar_tensor_tensor(out=t1[:], in0=hm2[:, c4, :], scalar=ssqm[:, c4:c4 + 1], in1=mlw_bc[:, mh * 256:(mh + 1) * 256], op0=ALU.mult, op1=ALU.mult),
                       reads=[hm2b, ssqmb, Bc], writes=[t1b])
                    op("pool", lambda e: e.tensor_tensor(ymg[:], t1[:], sz[:, c4, :], ALU.mult), reads=[t1b, szb], writes=[ymgb])
                    gs.setdefault("ymgs", {})[c4] = (ymg, ymgb)

                def gend_sqrt(gs):
                    hm2, hm2b, sz, szb, ssqm, ssqmb = gs["bufs"]
                    op("act", lambda e: e.activation(ssqm[:], ssqm[:], AF.Sqrt, bias=EPS, scale=1.0 / 256.0), reads=[ssqmb], writes=[ssqmb])
                    op("dve", lambda e: e.reciprocal(ssqm[:], ssqm[:]), reads=[ssqmb], writes=[ssqmb])

                def gend_pe(gs, c4, mh=mh):
                    ymg, ymgb = gs["ymgs"][c4]
                    g_ = gs["g"]
                    for dc in range(2):
                        op("pe", lambda e, dc=dc: e.transpose(bT2[:, (dc * 4 + c4) * P:(dc * 4 + c4 + 1) * P], ymg[:, dc * P:(dc + 1) * P], ident_bf[:]), reads=[ymgb, Bc], writes=[BbT2])
                    if c4 == 3:
                        ymTs, ymTsb, ymTss = ymTsr.next()
                        op("act", lambda e: e.activation(ymTs[:], bT2[:, 0:1024].rearrange("p (a b) -> p a b", b=512), AF.Copy), reads=[BbT2], writes=[ymTsb])
                        S.dma_batch("sp", ymTss, [((lambda e, dc=dc: e.dma_start(out=ymT_d[mh * 2 + dc, :, g_ * GS:(g_ + 1) * GS], in_=ymTs[:, dc, :])), [ymTsb], [Bym])
                                                  for dc in range(2)])

                prevg = None
                pend_tail = []
                for g in range(NG):
                    (QmT, QmTb, _), (KmT, KmTb, _) = nxt
                    if g + 1 < NG:
                        nxt = (QmTr.next(), KmTr.next())
                    Ktok, Ktokb, _ = Ktokr.next()
                    for c4 in range(4):
                        for dc in range(2):
                            op("pe", lambda e, KmT=KmT, c4=c4, dc=dc: e.transpose(bT[:, c4 * 256 + dc * P:c4 * 256 + (dc + 1) * P], KmT[:, dc, c4 * P:(c4 + 1) * P], ident_bf[:]),
                               reads=[KmTb, Bc], writes=[BbT])
                    op("act", lambda e, Ktok=Ktok: e.activation(Ktok[:], bT[:, 0:1024].rearrange("p (a b) -> p a b", b=256), AF.Copy), reads=[BbT], writes=[Ktokb])
                    hm2, hm2b, _ = hm2r.next()
                    sz, szb, _ = szr.next()
                    ssqm, ssqmb, _ = ssqmr.next()
                    curg = {"g": g, "bufs": (hm2, hm2b, sz, szb, ssqm, ssqmb)}
                    for c4 in range(4):
                        ci = g * 4 + c4
                        t0 = ci * P
                        col = ci * 4 + mh
                        if g == NG - 1 and c4 == 0 and mh + 1 < 4:
                            for j, wcol in enumerate((COL_QM, COL_KM)):
                                op("pool", lambda e, j=j, wcol=wcol, mh=mh: e.dma_start(out=W5[:, :, j, :], in_=w_in_r[:, :, wcol + (mh + 1) * 256:wcol + (mh + 2) * 256]),
                                   writes=[BW5s[j]], dma=w5sems[j])
                        part2 = None
                        if g + 1 < NG:
                            w, dc = pieces[c4]
                            dd = nxt[w]
                            part2 = qk_piece(g + 1, w, dc, dd[0], dd[1])
                        while pend_tail:
                            pend_tail.pop(0)()
                        if prevg is not None:
                            if c4 == 1:
                                gend_dve(prevg, 0)
                            if c4 >= 1:
                                gend_dve(prevg, c4)
                        for dc in range(2):
                            op("pe", lambda e, KmT=KmT, QmT=QmT, dc=dc, c4=c4: e.matmul(bV[:, 256:384], lhsT=KmT[:, dc, c4 * P:(c4 + 1) * P], rhs=QmT[:, dc, c4 * P:(c4 + 1) * P],
                                                                                     start=(dc == 0), stop=(dc == 1)),
                               reads=[KmTb, QmTb], writes=[BbV])
                        for kc in range(8):
                            op("pe", lambda e, kc=kc, t0=t0: e.matmul(bV[:, 0:256], lhsT=hT[:, kc, t0:t0 + P], rhs=W5[:, kc, 2, :], start=(kc == 0), stop=(kc == 7)),
                               reads=[hTb[g], BW5s[2]], writes=[BbV])
                        Sm, Smb, _ = Smr.next()
                        op("dve", lambda e, Sm=Sm: e.tensor_tensor(Sm[:], bV[:, 256:384], caus_s[:], ALU.mult), reads=[BbV, Bdiag], writes=[Smb])
                        Vp, Vpb, _ = Vpr.next()
                        op("act", lambda e, Vp=Vp, col=col: e.activation(Vp[:, 0:256], bV[:, 0:256], AF.Copy, scale=wtok[:, col:col + 1]), reads=[BbV, Bml], writes=[Vpb])
                        op("pool", lambda e, Vp=Vp, col=col: e.tensor_copy(Vp[:, 256:258], wtok[:, col:col + 1].to_broadcast([P, 2])), reads=[Bml], writes=[Vpb])
                        if ci > 0:
                            op("pool", lambda e, col=col: e.tensor_scalar(Cbf[:], Cst[:], decbc[:, col:col + 1], ML_KSCALE, ALU.mult, ALU.mult), reads=[BC, Bml], writes=[BCbf])
                        if part2 is not None:
                            part2()
                        for jj, wi in enumerate((4, 3)):
                            for kc in range(8):
                                op("pe", lambda e, kc=kc, t0=t0, jj=jj, wi=wi: e.matmul(bOZ[:, jj * 256:(jj + 1) * 256], lhsT=hT[:, kc, t0:t0 + P], rhs=W5[:, kc, wi, :], start=(kc == 0), stop=(kc == 7)),
                                   reads=[hTb[g], BW5s[wi]], writes=[BbOZ])
                        thoz, thozb, _ = thozr.next()
                        op("act", lambda e, thoz=thoz: e.activation(thoz[:, 0:256], bOZ[:, 0:256], AF.Tanh, scale=0.5), reads=[BbOZ], writes=[thozb])
                        op("act", lambda e, sz=sz, c4=c4: e.activation(sz[:, c4, :], bOZ[:, 256:512], AF.Silu), reads=[BbOZ], writes=[szb])
                        op("pool", lambda e, thoz=thoz: e.tensor_scalar(thoz[:, 0:256], thoz[:, 0:256], 0.5, 0.5, ALU.mult, ALU.add), reads=[thozb], writes=[thozb])
                        if ci > 0:
                            for dc in range(2):
                                op("pe", lambda e, QmT=QmT, dc=dc, c4=c4: e.matmul(bOut[:, 0:257], lhsT=QmT[:, dc, c4 * P:(c4 + 1) * P], rhs=Cbf[:, dc, 0:257], start=(dc == 0), stop=False),
                                   reads=[QmTb, BCbf], writes=[BbOut])
                        op("pe", lambda e, Sm=Sm, Vp=Vp, ci=ci: e.matmul(bOut[:, 0:257], lhsT=Sm[:], rhs=Vp[:, 0:257], start=(ci == 0), stop=True), reads=[Smb, Vpb], writes=[BbOut])
                        for dkc in range(2):
                            op("pe", lambda e, Ktok=Ktok, c4=c4, dkc=dkc, Vp=Vp: e.matmul(bCp[dkc][:, 0:257], lhsT=Ktok[:, c4, dkc * P:(dkc + 1) * P], rhs=Vp[:, 0:257], start=True, stop=True),
                               reads=[Ktokb, Vpb], writes=[BbCp[dkc]])
                            op("dve", lambda e, dkc=dkc, col=col: e.scalar_tensor_tensor(out=Cst[:, dkc, 0:257], in0=Cst[:, dkc, 0:257], scalar=decbc[:, col:col + 1], in1=bCp[dkc][:, 0:257],
                                                                                      op0=ALU.mult, op1=ALU.add),
                               reads=[BC, Bml, BbCp[dkc]], writes=[BC])
                        def tail(hm2=hm2, hm2b=hm2b, ssqm=ssqm, ssqmb=ssqmb, c4=c4, col=col, thoz=thoz, thozb=thozb):
                            dn, dnb, _ = dnr.next()
                            op("dve", lambda e: e.tensor_scalar(dn[:], bOut[:, 256:257], -1.0, cltok[:, col:col + 1], ALU.mult, ALU.max), reads=[BbOut, Bml], writes=[dnb])
                            op("dve", lambda e: e.tensor_tensor(dn[:], dn[:], bOut[:, 256:257], ALU.max), reads=[BbOut, dnb], writes=[dnb])
                            op("dve", lambda e: e.reciprocal(dn[:], dn[:]), reads=[dnb], writes=[dnb])
                            op("dve", lambda e: e.scalar_tensor_tensor(out=hm2[:, c4, :], in0=bOut[:, 0:256], scalar=dn[:, 0:1], in1=thoz[:, 0:256], op0=ALU.mult, op1=ALU.mult),
                               reads=[BbOut, dnb, thozb], writes=[hm2b])
                            op("act", lambda e: e.activation(junk2[:], hm2[:, c4, :], AF.Square, accum_out=ssqm[:, c4:c4 + 1]), reads=[hm2b], writes=[Bjunk2, ssqmb])
                        pend_tail.append(tail)
                        if prevg is not None:
                            if c4 == 0:
                                gend_sqrt(prevg)
                            else:
                                gend_pe(prevg, c4 - 1)
                                if c4 == 3:
                                    gend_pe(prevg, 3)
                    prevg = curg
                while pend_tail:
                    pend_tail.pop(0)()
                gend_sqrt(prevg)
                for c4 in range(4):
                    gend_dve(prevg, c4)
                    gend_pe(prevg, c4)
                if stop_after == "ph4" and mh == 0:
                    S.barrier()
                    toks = []
                    tb = k.sb("tmpd", [P, 1024], BF16)
                    tf = k.sb("tmpf", [P, 1024], F32)
                    Bt = Buf("tmpd")
                    Bt2 = Buf("tmpf")
                    ds2 = S.dma_sem("dbg2")
                    for kc in range(2):
                        for q4 in range(4):
                            op("sp", lambda e, kc=kc, q4=q4: e.dma_start(out=tb[:], in_=ymT_d[kc, :, q4 * 1024:(q4 + 1) * 1024]), reads=[Bym], writes=[Bt], dma=ds2)
                            op("dve", lambda e: e.tensor_copy(tf[:], tb[:]), reads=[Bt], writes=[Bt2])
                            toks.append(dump(dbg[kc * P:(kc + 1) * P, q4 * 1024:(q4 + 1) * 1024], tf[:], [Bt2]))
                    S.ops["sp"].append((tuple(toks), None, None))
                    S.emit()
                    return nc
            S.barrier()
            S.emit()
        with contextlib.ExitStack() as ph:
            k.pstack = ph
            yaTr = Ring(k, "yaTg", 2, [P, 8, GS], BF16, dma=True)
            ymTr = Ring(k, "ymTg", 2, [P, 8, GS], BF16, dma=True)
            sgar = Ring(k, "sga", 3, [P, GS], F32, dma=True)
            sgmr = Ring(k, "sgm", 3, [P, GS], F32, dma=True)
            y1r = Ring(k, "y1", 2, [P, GS], F32)
            y2r = Ring(k, "y2", 2, [P, GS], F32)
            yTr = Ring(k, "yT", 1, [P, 8, GS], BF16)
            xr5 = Ring(k, "x5", 2, [P, D], F32, dma=True)
            otr = Ring(k, "ot", 2, [P, D], F32, dma=True)
            bA = Ring(k, "bA", 3, [P, 512], F32, psum=True)
            bM = Ring(k, "bM", 3, [P, 512], F32, psum=True)
            bF = Ring(k, "bF", 2, [P, 512], F32, psum=True)
            def load_branch(g):
                yaTg, yaTgb, yas = yaTr.next()
                ymTg, ymTgb, yms = ymTr.next()
                op("sp", lambda e: e.dma_start(out=yaTg[:], in_=yaT_d[:, :, g * GS:(g + 1) * GS].rearrange("k p t -> p k t")), writes=[yaTgb], dma=yas)
                op("sp", lambda e: e.dma_start(out=ymTg[:], in_=ymT_d[:, :, g * GS:(g + 1) * GS].rearrange("k p t -> p k t")), writes=[ymTgb], dma=yms)
                return yaTg, yaTgb, ymTg, ymTgb

            nxt_br = load_branch(0)
            for g in range(NG):
                yaTg, yaTgb, ymTg, ymTgb = nxt_br
                if g + 1 < NG:
                    nxt_br = load_branch(g + 1)
                yT, yTb, _ = yTr.next()
                for cc in range(8):
                    sga, sgab, sgas = sgar.next()
                    sgm, sgmb, sgms = sgmr.next()
                    op("sp", lambda e, sga=sga, cc=cc, g=g: e.dma_start(out=sga[:], in_=sga_d[cc, :, g * GS:(g + 1) * GS]), writes=[sgab], dma=sgas)
                    op("sp", lambda e, sgm=sgm, cc=cc, g=g: e.dma_start(out=sgm[:], in_=sgm_d[cc, :, g * GS:(g + 1) * GS]), writes=[sgmb], dma=sgms)
                    ba, bab, _ = bA.next()
                    bm, bmb, _ = bM.next()
                    for kc in range(8):
                        op("pe", lambda e, ba=ba, kc=kc, cc=cc, yaTg=yaTg: e.matmul(ba[:], lhsT=Wa[:, kc, cc * P:(cc + 1) * P], rhs=yaTg[:, kc, :], start=(kc == 0), stop=(kc == 7)),
                           reads=[BWa, yaTgb], writes=[bab])
                    for kc in range(8):
                        op("pe", lambda e, bm=bm, kc=kc, cc=cc, ymTg=ymTg: e.matmul(bm[:], lhsT=Wm[:, kc, cc * P:(cc + 1) * P], rhs=ymTg[:, kc, :], start=(kc == 0), stop=(kc == 7)),
                           reads=[BWm, ymTgb], writes=[bmb])
                    y1, y1b, _ = y1r.next()
                    y2, y2b, _ = y2r.next()
                    op("dve", lambda e, y1=y1, sga=sga, ba=ba: e.scalar_tensor_tensor(out=y1[:], in0=sga[:], scalar=1.0, in1=ba[:], op0=ALU.add, op1=ALU.mult), reads=[sgab, bab], writes=[y1b])
                    op("dve", lambda e, y2=y2, sgm=sgm, bm=bm: e.scalar_tensor_tensor(out=y2[:], in0=sgm[:], scalar=1.0, in1=bm[:], op0=ALU.add, op1=ALU.mult), reads=[sgmb, bmb], writes=[y2b])
                    op("pool", lambda e, yT=yT, cc=cc, y1=y1, y2=y2: e.tensor_tensor(yT[:, cc, :], y1[:], y2[:], ALU.add), reads=[y1b, y2b], writes=[yTb])
                for tt in range(4):
                    ti = g * 4 + tt
                    xt, xb, xs = xr5.next()
                    ot, otb, ots = otr.next()
                    op("sp", lambda e, xt=xt, ti=ti: e.dma_start(out=xt[:], in_=x[ti * P:(ti + 1) * P, :]), writes=[xb], dma=xs)
                    for og in range(2):
                        bf_, bfb, _ = bF.next()
                        for cc in range(8):
                            op("pe", lambda e, bf_=bf_, cc=cc, tt=tt, og=og, yT=yT: e.matmul(bf_[:], lhsT=yT[:, cc, tt * P:(tt + 1) * P], rhs=Wo[:, cc, og * 512:(og + 1) * 512], start=(cc == 0), stop=(cc == 7)),
                               reads=[yTb, BWo], writes=[bfb])
                        op("dve", lambda e, ot=ot, bf_=bf_, og=og: e.tensor_tensor(ot[:, og * 512:(og + 1) * 512], bf_[:], gate_half[:, og * 512:(og + 1) * 512], ALU.mult), reads=[bfb, Bgate], writes=[otb])
                    op("pool", lambda e, ot=ot, xt=xt: e.tensor_tensor(ot[:], ot[:], xt[:], ALU.add), reads=[otb, xb], writes=[otb])
                    op("act", lambda e, ot=ot, ti=ti: e.dma_start(out=out[ti * P:(ti + 1) * P, :], in_=ot[:]), reads=[otb], dma=ots)
            S.barrier()
            S.emit()
    return nc


def _host_inputs(inputs):
    f = np.float32
    x = np.ascontiguousarray(inputs["x"], dtype=f)
    c = np.asarray(inputs["c"], dtype=f)
    rel_bias = np.asarray(inputs["rel_bias"], dtype=f)
    dist = np.arange(0, 1024)
    max_exact = 16
    nf = np.maximum(dist, 1).astype(np.float32)
    large = max_exact + (np.log(nf / max_exact) / np.log(128 / max_exact) * (32 - max_exact)).astype(np.int32)
    large = np.minimum(large, 31)
    bucket = np.where(dist < max_exact, dist, large)
    kk = np.arange(128)[:, None]
    jj = np.arange(1024)[None, :]
    dd = jj - 384 - kk
    valid = dd >= 0
    bidx = bucket[np.clip(dd, 0, 1023)]
    utab = np.empty((8, 128, 1024), dtype=f)
    for h in range(8):
        g = rel_bias[:, h][bidx]
        utab[h] = np.where(valid, g, f(NEGV))
    conv_w = np.asarray(inputs["conv_w"], dtype=f)[0]
    conv_b = np.asarray(inputs["conv_b"], dtype=f)[0]
    convw = np.ascontiguousarray(conv_w.T.reshape(16, 128, 4).transpose(1, 0, 2))
    convb = np.ascontiguousarray(conv_b.reshape(16, 128).T)
    ident = np.eye(128, dtype=f)
    caus = np.triu(np.ones((128, 128), dtype=f))
    ind = np.zeros((16, 16, 128), dtype=f)
    for b in range(16):
        ind[b, b, :] = 1.0
    common = {
        "w_ada": np.ascontiguousarray(inputs["w_ada"][0], dtype=f),
        "b_ada": np.ascontiguousarray(inputs["b_ada"][0:1], dtype=f),
        "norm_w": np.ascontiguousarray(inputs["norm_w"][0:1], dtype=f),
        "w_in": np.ascontiguousarray(inputs["w_in"][0], dtype=f),
        "qnw": np.ascontiguousarray(inputs["q_norm_w"][0:1], dtype=f),
        "knw": np.ascontiguousarray(inputs["k_norm_w"][0:1], dtype=f),
        "utab": utab,
        "bias31": np.ascontiguousarray(rel_bias[31:32, :]),
        "convw": convw,
        "convb": convb,
        "b_ig": np.ascontiguousarray(np.asarray(inputs["b_igate"], dtype=f)[0].reshape(4, 1)),
        "b_fg": np.ascontiguousarray(np.asarray(inputs["b_fgate"], dtype=f)[0].reshape(4, 1)),
        "mlnw": np.ascontiguousarray(inputs["ml_norm_w"][0:1], dtype=f),
        "w_att": np.ascontiguousarray(inputs["w_att_proj"][0], dtype=f),
        "w_ml": np.ascontiguousarray(inputs["w_ml_proj"][0], dtype=f),
        "w_out": np.ascontiguousarray(inputs["w_out"][0], dtype=f),
        "c_ident": ident,
        "c_caus": caus,
        "c_ind": ind.reshape(16, 16 * 128),
    }
    maps = []
    for b in range(x.shape[0]):
        m = dict(common)
        m["x"] = x[b]
        m["ccol"] = np.ascontiguousarray(c[b].reshape(8, 128).T)
        maps.append(m)
    return maps


def kernel(**inputs):
    maps = _host_inputs(inputs)
    nc = build()
    res = run_bass_kernel_spmd(nc, maps, core_ids=list(range(8)))
    return np.stack([np.asarray(r["out"], dtype=np.float32) for r in res.results], axis=0)
```

```python
import contextlib
import numpy as np
import ml_dtypes
import concourse.bass as bass
import concourse.mybir as mybir
from concourse.bass_utils import run_bass_kernel_spmd

F32 = mybir.dt.float32
BF16 = mybir.dt.bfloat16
AF = mybir.ActivationFunctionType
ALU = mybir.AluOpType
AX = mybir.AxisListType

T = 4096
D = 1024
P = 128
NKC = 8
NG = 8
GS = 512
EPS = 1e-6
NEGV = -30000.0
ATT_SCALE = 128.0 ** -0.5
ML_KSCALE = 256.0 ** -0.5
COL_QA, COL_KA, COL_VA, COL_ZA = 0, 1024, 2048, 3072
COL_QM, COL_KM, COL_VM, COL_ZM, COL_OM = 4096, 5120, 6144, 7168, 8192
COL_IG, COL_FG, COL_GA, COL_GM = 9216, 9220, 9224, 10248
D_IN = 11272


class Buf:
    __slots__ = ("w", "r", "name", "psum")

    def __init__(self, name="", psum=False):
        self.w = None
        self.r = {}
        self.name = name
        self.psum = psum


class Sched:
    ENGS = ("pe", "act", "dve", "pool", "sp")

    def __init__(self, nc, stack):
        self.nc = nc
        self.stack = stack
        self.ops = {e: [] for e in self.ENGS}
        self.cnt = {e: 0 for e in self.ENGS}
        self.waited = {e: {} for e in self.ENGS}
        self.sems = {}
        self.dcnt = {}
        for e in self.ENGS:
            self.sems[e] = stack.enter_context(nc.semaphore("s_" + e))

    def dma_sem(self, name):
        self.sems[name] = self.stack.enter_context(self.nc.semaphore("d_" + name))
        self.dcnt[name] = 0
        return name

    def dma_batch(self, eng, sem, items):
        n = len(items)
        toks = []
        for i, (fn, reads, writes) in enumerate(items):
            toks.append(self.op(eng, fn, reads=reads, writes=writes, dma=sem, dma_extra=n - 1 - i))
        return toks

    def op(self, eng, fn, reads=(), writes=(), dma=None, dma_extra=0):
        waits = {}
        wd = self.waited[eng]

        def need(dep):
            if dep is None:
                return
            sk, val = dep
            if sk == "pe" and eng == "pe":
                return
            if dma is not None and sk == dma and val > self.dcnt[dma]:
                return
            if wd.get(sk, 0) >= val:
                return
            if waits.get(sk, 0) < val:
                waits[sk] = val

        for b in reads:
            need(b.w)
            if b.psum:
                for sk, v in b.r.items():
                    if sk != eng:
                        need((sk, v))
        for b in writes:
            need(b.w)
            for sk, v in b.r.items():
                need((sk, v))
        for sk, v in waits.items():
            wd[sk] = v
        if dma is None:
            self.cnt[eng] += 1
            tok = (eng, self.cnt[eng])
            inc = (eng, 1)
        else:
            self.dcnt[dma] += 16
            tok = (dma, self.dcnt[dma] + 16 * dma_extra)
            inc = (dma, 16)
        for b in writes:
            b.w = tok
            b.r = {}
        for b in reads:
            if b.r.get(tok[0], 0) < tok[1]:
                b.r[tok[0]] = tok[1]
        self.ops[eng].append((tuple(waits.items()), fn, inc))
        return tok

    def barrier(self):
        toks = [(e, self.cnt[e]) for e in self.ENGS if self.cnt[e] > 0]
        toks += [(k, v) for k, v in self.dcnt.items() if v > 0]
        for e in self.ENGS:
            w = []
            for sk, v in toks:
                if sk == e:
                    continue
                if self.waited[e].get(sk, 0) < v:
                    self.waited[e][sk] = v
                    w.append((sk, v))
            self.ops[e].append((tuple(w), None, None))

    def emit(self):
        nc = self.nc
        sems = self.sems
        ops_now = self.ops
        self.ops = {e: [] for e in self.ENGS}

        def replay(ename, eng):
            for waits, fn, inc in ops_now[ename]:
                for sk, v in waits:
                    eng.wait_ge(sems[sk], v)
                if fn is None:
                    continue
                ins = fn(eng)
                ins.then_inc(sems[inc[0]], inc[1])

        with nc.Block() as block:
            @block.tensor
            def _(e):
                replay("pe", e)

            @block.scalar
            def _(e):
                replay("act", e)

            @block.vector
            def _(e):
                replay("dve", e)

            @block.gpsimd
            def _(e):
                replay("pool", e)

            @block.sync
            def _(e):
                replay("sp", e)


class Ring:
    def __init__(self, K, name, n, shape, dt, psum=False, dma=False):
        self.n = n
        self.t = []
        self.b = []
        self.s = []
        for i in range(n):
            if psum:
                self.t.append(K.psum(f"{name}{i}", shape, dt))
            else:
                self.t.append(K.sb(f"{name}{i}", shape, dt))
            self.b.append(Buf(f"{name}{i}", psum=psum))
            self.s.append(K.S.dma_sem(f"{name}{i}_{K.uid()}") if dma else None)
        self.i = 0

    def next(self):
        i = self.i % self.n
        self.i += 1
        return self.t[i], self.b[i], self.s[i]


class K:
    def __init__(self, nc, S, stack):
        self.nc = nc
        self.S = S
        self.stack = stack
        self.pstack = stack
        self._uid = 0

    def uid(self):
        self._uid += 1
        return self._uid

    def sb(self, name, shape, dt):
        return self.pstack.enter_context(self.nc.sbuf_tensor(f"{name}_{self.uid()}", shape, dt))

    def psum(self, name, shape, dt):
        return self.pstack.enter_context(self.nc.psum_tensor(f"{name}_{self.uid()}", shape, dt))


def build(stop_after=None, dbg_shape=None):
    nc = bass.Bass("TRN2", target_bir_lowering=False)

    def din(name, shape, dt=F32):
        return nc.dram_tensor(name, shape, dt, kind="ExternalInput").ap()

    x = din("x", [T, D])
    ccol = din("ccol", [P, 8])
    w_ada = din("w_ada", [D, 3 * D])
    b_ada = din("b_ada", [1, 3 * D])
    norm_w = din("norm_w", [1, D])
    w_in = din("w_in", [D, D_IN])
    qnw = din("qnw", [1, P])
    knw = din("knw", [1, P])
    utab = din("utab", [8, P, 1024])
    bias31 = din("bias31", [1, 8])
    convw = din("convw", [P, 16, 4])
    convb = din("convb", [P, 16])
    b_ig = din("b_ig", [4, 1])
    b_fg = din("b_fg", [4, 1])
    mlnw = din("mlnw", [1, D])
    w_att = din("w_att", [D, D])
    w_ml = din("w_ml", [D, D])
    w_out = din("w_out", [D, D])
    c_ident = din("c_ident", [P, P])
    c_caus = din("c_caus", [P, P])
    c_ind = din("c_ind", [16, 16 * P])
    out = nc.dram_tensor("out", [T, D], F32, kind="ExternalOutput").ap()
    dbg = None
    if dbg_shape is not None:
        dbg = nc.dram_tensor("dbg", list(dbg_shape), F32, kind="ExternalOutput").ap()
    yaT_d = nc.dram_tensor("yaT_d", [8, P, T], BF16, kind="Internal").ap()
    ymT_d = nc.dram_tensor("ymT_d", [8, P, T], BF16, kind="Internal").ap()
    sga_d = nc.dram_tensor("sga_d", [8, P, T], F32, kind="Internal").ap()
    sgm_d = nc.dram_tensor("sgm_d", [8, P, T], F32, kind="Internal").ap()

    w_in_r = w_in.rearrange("(kc p) n -> p kc n", p=P)

    with contextlib.ExitStack() as st:
        S = Sched(nc, st)
        k = K(nc, S, st)
        op = S.op
        ident = k.sb("ident", [P, P], F32)
        ident_bf = k.sb("ident_bf", [P, P], BF16)
        ones_bf = k.sb("ones_bf", [P, P], BF16)
        ones_f = k.sb("ones_f", [P, P], F32)
        caus = k.sb("caus", [P, P], F32)
        ind_bf = k.sb("ind_bf", [16, 16 * P], BF16)
        gate_half = k.sb("gate_half", [P, D], F32)
        mlw_bc = k.sb("mlw_bc", [P, D], F32)
        qw_bc = k.sb("qw_bc", [P, P], F32)
        kw_bc = k.sb("kw_bc", [P, P], F32)
        b31_bc = k.sb("b31_bc", [P, 8], F32)
        convw_sb = k.sb("convw_sb", [P, 16, 4], F32)
        convb_sb = k.sb("convb_sb", [P, 16], F32)
        wtok = k.sb("wtok", [P, 128], F32)
        cltok = k.sb("cltok", [P, 128], F32)
        decbc = k.sb("decbc", [P, 128], F32)
        Bc = Buf("consts")
        Bgate = Buf("gate_half")
        Bml = Buf("mlgates")
        hst = st.enter_context(contextlib.ExitStack())
        k.pstack = hst
        hT = k.sb("hT", [P, NKC, T], BF16)
        hTb = [Buf(f"hT{g}") for g in range(NG)]
        k.pstack = st
        dsem = S.dma_sem("const")
        dsem_out = S.dma_sem("dbgout")

        cl = [(ident[:], c_ident[:, :]), (caus[:], c_caus[:, :]),
              (mlw_bc[:], mlnw[0:1, :].partition_broadcast(P)), (qw_bc[:], qnw[0:1, :].partition_broadcast(P)),
              (kw_bc[:], knw[0:1, :].partition_broadcast(P)), (b31_bc[:], bias31[0:1, :].partition_broadcast(P)),
              (convw_sb[:], convw[:, :, :]), (convb_sb[:], convb[:, :])]
        S.dma_batch("sp", dsem, [((lambda e, d_=d_, s_=s_: e.dma_start(out=d_, in_=s_)), [], [Bc]) for d_, s_ in cl])
        op("dve", lambda e: e.tensor_copy(ident_bf[:], ident[:]), reads=[Bc], writes=[Bc])
        op("pool", lambda e: e.dma_start(out=ind_bf[:], in_=c_ind[:, :]), writes=[Bc], dma=S.dma_sem("cind"))
        op("dve", lambda e: e.memset(ones_bf[:], 1.0), writes=[Bc])
        op("dve", lambda e: e.memset(ones_f[:], 1.0), writes=[Bc])

        def finish(src_fn=None):
            toks = []
            if src_fn is not None:
                toks = src_fn()
            S.ops["sp"].append((tuple(toks), None, None))
            S.emit()

        def dump(dst, src, bufs):
            return op("sp", lambda e: e.dma_start(out=dst, in_=src), reads=bufs, dma=dsem_out)

        with contextlib.ExitStack() as ph:
            k.pstack = ph
            adarow = k.sb("adarow", [P, 3 * D], F32)
            Bada = Buf("adarow")
            ccol_sb = k.sb("ccol_sb", [P, 8], F32)
            cbc = k.sb("cbc", [P, 8, P], F32)
            nwbc = k.sb("nwbc", [P, D], F32)
            Abc = k.sb("Abc", [P, D], F32)
            junk = k.sb("junk", [P, D], F32)
            ssq = k.sb("ssq", [P, 32], F32)
            rs = k.sb("rs", [P, 32], F32)
            Bcc = Buf("ccol")
            Bcbc = Buf("cbc")
            Bnw = Buf("nwbc")
            BA = Buf("Abc")
            Bjunk = Buf("junk")
            wst = Ring(k, "wada", 4, [P, 8, 256], F32, dma=True)
            pa = Ring(k, "pa", 2, [P, 512], F32, psum=True)
            ptr = Ring(k, "ptr", 6, [P, 512], F32, psum=True)
            xr = Ring(k, "xt", 4, [P, D], F32, dma=True)
            xnr = Ring(k, "xn", 4, [P, D], F32)
            op("sp", lambda e: e.dma_start(out=ccol_sb[:], in_=ccol[:, :]), writes=[Bcc], dma=S.dma_sem("ccol"))
            op("sp", lambda e: e.dma_start(out=adarow[:], in_=b_ada[0:1, :].partition_broadcast(P)), writes=[Bada], dma=S.dma_sem("bada"))
            op("sp", lambda e: e.dma_start(out=nwbc[:], in_=norm_w[0:1, :].partition_broadcast(P)), writes=[Bnw], dma=S.dma_sem("nwbc"))
            op("dve", lambda e: e.tensor_copy(cbc[:], ccol_sb[:].unsqueeze(2).to_broadcast([P, 8, P])), reads=[Bcc], writes=[Bcbc])
            w_ada_r = w_ada.rearrange("(kc p) n -> p kc n", p=P)
            xpre = []
            for tt in range(4):
                xt, xb, xs = xr.next()
                op("sp", lambda e, xt=xt, tt=tt: e.dma_start(out=xt[:], in_=x[tt * P:(tt + 1) * P, :]), writes=[xb], dma=xs)
                xpre.append((xt, xb, xs))
            for cg in range(12):
                wt, wb, ws = wst.next()
                op("sp" if cg % 2 == 0 else "act", lambda e, wt=wt, cg=cg: e.dma_start(out=wt[:], in_=w_ada_r[:, :, cg * 256:(cg + 1) * 256]), writes=[wb], dma=ws)
                pt_, pb, _ = pa.next()
                for kc in range(8):
                    op("pe", lambda e, pt_=pt_, wt=wt, kc=kc: e.matmul(pt_[:, 0:256], lhsT=cbc[:, kc, :], rhs=wt[:, kc, :], start=(kc == 0), stop=(kc == 7)),
                       reads=[Bcbc, wb], writes=[pb])
                op("dve", lambda e, pt_=pt_, cg=cg: e.tensor_tensor(adarow[:, cg * 256:(cg + 1) * 256], pt_[:, 0:256], adarow[:, cg * 256:(cg + 1) * 256], ALU.add),
                   reads=[pb, Bada], writes=[Bada])
            op("dve", lambda e: e.scalar_tensor_tensor(out=Abc[:], in0=adarow[:, D:2 * D], scalar=1.0, in1=nwbc[:], op0=ALU.add, op1=ALU.mult),
               reads=[Bada, Bnw], writes=[BA])
            op("dve", lambda e: e.tensor_scalar(gate_half[:], adarow[:, 2 * D:3 * D], 0.5, None, ALU.mult), reads=[Bada], writes=[Bgate])
            Bsst = [Buf(f"ssq{t}") for t in range(32)]
            lagq = []

            def stage2(tt, banks):
                g = tt // 4
                for half, (pt_, pb) in enumerate(banks):
                    op("act", lambda e, pt_=pt_, half=half, tt=tt: e.activation(hT[:, half * 4:half * 4 + 4, tt * P:(tt + 1) * P],
                                                                               pt_[:, 0:512].rearrange("p (a b) -> p a b", b=P), AF.Copy),
                       reads=[pb], writes=[hTb[g]])

            for tt in range(32):
                if tt < 4:
                    xt, xb, xs = xpre[tt]
                else:
                    xt, xb, xs = xr.next()
                    op("sp", lambda e, xt=xt, tt=tt: e.dma_start(out=xt[:], in_=x[tt * P:(tt + 1) * P, :]), writes=[xb], dma=xs)
                op("act", lambda e, xt=xt, tt=tt: e.activation(junk[:], xt[:], AF.Square, accum_out=ssq[:, tt:tt + 1]), reads=[xb], writes=[Bjunk, Bsst[tt]])
                op("act", lambda e, tt=tt: e.activation(rs[:, tt:tt + 1], ssq[:, tt:tt + 1], AF.Sqrt, bias=EPS, scale=1.0 / D), reads=[Bsst[tt]], writes=[Bsst[tt]])
                op("dve", lambda e, tt=tt: e.reciprocal(rs[:, tt:tt + 1], rs[:, tt:tt + 1]), reads=[Bsst[tt]], writes=[Bsst[tt]])
                xn, xnb, _ = xnr.next()
                op("dve", lambda e, xn=xn, xt=xt, tt=tt: e.scalar_tensor_tensor(out=xn[:], in0=xt[:], scalar=rs[:, tt:tt + 1], in1=Abc[:], op0=ALU.mult, op1=ALU.mult),
                   reads=[xb, Bsst[tt], BA], writes=[xnb])
                op("pool", lambda e, xn=xn: e.tensor_tensor(xn[:], xn[:], adarow[:, 0:D], ALU.add), reads=[xnb, Bada], writes=[xnb])
                banks = []
                for half in range(2):
                    pt_, pb, _ = ptr.next()
                    for j in range(4):
                        kc = half * 4 + j
                        op("pe", lambda e, pt_=pt_, xn=xn, kc=kc, j=j: e.transpose(pt_[:, j * P:(j + 1) * P], xn[:, kc * P:(kc + 1) * P], ident[:]),
                           reads=[xnb, Bc], writes=[pb])
                    banks.append((pt_, pb))
                lagq.append((tt, banks))
                if len(lagq) > 2:
                    stage2(*lagq.pop(0))
            while lagq:
                stage2(*lagq.pop(0))
            S.barrier()
            if stop_after == "ph1":
                tmpf = k.sb("tmpf", [P, T], F32)
                Bt = Buf("tmpf")
                toks = []
                for kc in range(8):
                    op("dve", lambda e, kc=kc: e.tensor_copy(tmpf[:], hT[:, kc, :]), reads=hTb, writes=[Bt])
                    toks.append(dump(dbg[kc * P:(kc + 1) * P, :], tmpf[:], [Bt]))
                toks.append(dump(dbg[8 * P:9 * P, 0:D], gate_half[:], [Bgate]))
                S.ops["sp"].append((tuple(toks), None, None))
                S.emit()
                return nc
            S.emit()
        with contextlib.ExitStack() as ph:
            k.pstack = ph
            Wg = k.sb("Wg", [P, 8, 8], BF16)
            BWg = Buf("Wg")
            wsem = S.dma_sem("wg")
            op("pool", lambda e: e.dma_start(out=Wg[:], in_=w_in_r[:, :, COL_IG:COL_IG + 8]), writes=[BWg], dma=wsem)
            Wga = k.sb("Wga", [P, 8, D], BF16)
            Wgm = k.sb("Wgm", [P, 8, D], BF16)
            BWga = Buf("Wga")
            BWgm = Buf("Wgm")
            wsem2 = S.dma_sem("wga")
            wsem3 = S.dma_sem("wgm")
            op("pool", lambda e: e.dma_start(out=Wga[:], in_=w_in_r[:, :, COL_GA:COL_GA + D]), writes=[BWga], dma=wsem2)
            op("pool", lambda e: e.dma_start(out=Wgm[:], in_=w_in_r[:, :, COL_GM:COL_GM + D]), writes=[BWgm], dma=wsem3)
            bi_sb = k.sb("bi_sb", [4, 1], F32)
            bf_sb = k.sb("bf_sb", [4, 1], F32)
            negbf = k.sb("negbf", [4, 1], F32)
            Bb = Buf("gbias")
            S.dma_batch("sp", S.dma_sem("gb"), [((lambda e: e.dma_start(out=bi_sb[:], in_=b_ig[:, :])), [], [Bb]),
                                                ((lambda e: e.dma_start(out=bf_sb[:], in_=b_fg[:, :])), [], [Bb])])
            op("dve", lambda e: e.tensor_scalar(negbf[:], bf_sb[:], -1.0, None, ALU.mult), reads=[Bb], writes=[Bb])
            bufA = k.sb("bufA", [4, T], F32)
            bufB = k.sb("bufB", [4, T], F32)
            bufC = k.sb("bufC", [4, T], F32)
            BA_, BB_, BC_ = Buf("bufA"), Buf("bufB"), Buf("bufC")
            cm = k.sb("cm", [4, 32], F32)
            Mx = k.sb("Mx", [4, 32], F32)
            Mp = k.sb("Mp", [4, 32], F32)
            dec = k.sb("dec", [4, 32], F32)
            X4 = k.sb("X4", [4, 32, 4], F32)
            Bsm = Buf("gsmall")
            pg = Ring(k, "pg", 2, [P, 512], F32, psum=True)
            ptok = k.psum("ptok", [P, 512], F32)
            Bptok = Buf("ptok", psum=True)
            pdec = k.psum("pdec", [P, 512], F32)
            Bpdec = Buf("pdec", psum=True)
            for g in range(NG):
                pf, pfb, _ = pg.next()
                for kc in range(8):
                    op("pe", lambda e, pf=pf, kc=kc, g=g: e.matmul(pf[0:4, :], lhsT=Wg[:, kc, 4:8], rhs=hT[:, kc, g * GS:(g + 1) * GS], start=(kc == 0), stop=(kc == 7)),
                       reads=[BWg, hTb[g]], writes=[pfb])
                op("act", lambda e, pf=pf, g=g: e.activation(bufA[:, g * GS:(g + 1) * GS], pf[0:4, :], AF.Exp, bias=negbf[:], scale=-1.0), reads=[pfb, Bb], writes=[BA_])
                pi, pib, _ = pg.next()
                for kc in range(8):
                    op("pe", lambda e, pi=pi, kc=kc, g=g: e.matmul(pi[0:4, :], lhsT=Wg[:, kc, 0:4], rhs=hT[:, kc, g * GS:(g + 1) * GS], start=(kc == 0), stop=(kc == 7)),
                       reads=[BWg, hTb[g]], writes=[pib])
                op("act", lambda e, pi=pi, g=g: e.activation(bufC[:, g * GS:(g + 1) * GS], pi[0:4, :], AF.Identity, bias=bi_sb[:]), reads=[pib, Bb], writes=[BC_])
            op("act", lambda e: e.activation(bufA[:], bufA[:], AF.Ln, bias=1.0, scale=1.0), reads=[BA_], writes=[BA_])
            op("dve", lambda e: e.tensor_tensor_scan(bufB[:], bufA[:], bufA[:], 0.0, ALU.add, ALU.max), reads=[BA_], writes=[BB_])
            op("dve", lambda e: e.tensor_tensor(bufC[:], bufC[:], bufB[:], ALU.add), reads=[BC_, BB_], writes=[BC_])
            op("dve", lambda e: e.tensor_reduce(out=cm[:], in_=bufC[:].rearrange("p (c l) -> p c l", l=P), axis=AX.X, op=ALU.max), reads=[BC_], writes=[Bsm])
            op("dve", lambda e: e.tensor_tensor_scan(Mx[:], cm[:], cm[:], 0.0, ALU.max, ALU.max), reads=[Bsm], writes=[Bsm])
            op("dve", lambda e: e.memset(Mp[:, 0:1], 0.0), reads=[Bsm], writes=[Bsm])
            op("dve", lambda e: e.tensor_copy(Mp[:, 1:32], Mx[:, 0:31]), reads=[Bsm], writes=[Bsm])
            op("dve", lambda e: e.tensor_tensor(dec[:], Mp[:], Mx[:], ALU.subtract), reads=[Bsm], writes=[Bsm])
            op("act", lambda e: e.activation(dec[:], dec[:], AF.Exp), reads=[Bsm], writes=[Bsm])
            v3 = lambda t_: t_[:].rearrange("p (c l) -> p c l", l=P)
            Mb = lambda: Mx[:].unsqueeze(2).to_broadcast([4, 32, P])
            op("dve", lambda e: e.tensor_tensor(v3(bufA), v3(bufC), Mb(), ALU.subtract), reads=[BC_, Bsm, BA_], writes=[BA_])
            op("act", lambda e: e.activation(bufA[:], bufA[:], AF.Exp), reads=[BA_], writes=[BA_])
            op("dve", lambda e: e.tensor_tensor(v3(bufB), v3(bufB), Mb(), ALU.subtract), reads=[BB_, Bsm], writes=[BB_])
            op("act", lambda e: e.activation(bufB[:], bufB[:], AF.Exp), reads=[BB_], writes=[BB_])
            for c in range(32):
                op("pe", lambda e, c=c: e.transpose(ptok[:, c * 4:(c + 1) * 4], bufA[0:4, c * P:(c + 1) * P], ident[0:4, 0:4]), reads=[BA_, Bc], writes=[Bptok])
                op("pe", lambda e, c=c: e.transpose(ptok[:, 128 + c * 4:128 + (c + 1) * 4], bufB[0:4, c * P:(c + 1) * P], ident[0:4, 0:4]), reads=[BB_, Bc], writes=[Bptok])
            op("dve", lambda e: e.tensor_copy(wtok[:], ptok[:, 0:128]), reads=[Bptok], writes=[Bml])
            op("dve", lambda e: e.tensor_copy(cltok[:], ptok[:, 128:256]), reads=[Bptok], writes=[Bml])
            op("dve", lambda e: e.tensor_tensor(X4[:], dec[:].unsqueeze(2).to_broadcast([4, 32, 4]), ident[0:4, 0:4].unsqueeze(1).to_broadcast([4, 32, 4]), ALU.mult),
               reads=[Bsm, Bc], writes=[Bsm])
            op("pe", lambda e: e.matmul(pdec[:, 0:128], lhsT=ones_f[0:4, :], rhs=X4[:].rearrange("p c h -> p (c h)"), start=True, stop=True), reads=[Bsm, Bc], writes=[Bpdec])
            op("dve", lambda e: e.tensor_copy(decbc[:], pdec[:, 0:128]), reads=[Bpdec], writes=[Bml])
            pgg = Ring(k, "pgg", 4, [P, 512], F32, psum=True)
            sgr = Ring(k, "sg", 4, [P, 512], F32, dma=True)
            Bsga = Buf("sga_d")
            for g in range(NG):
                for cc in range(8):
                    for W_, Wb_, dst in ((Wga, BWga, sga_d), (Wgm, BWgm, sgm_d)):
                        pb_, pbb, _ = pgg.next()
                        for kc in range(8):
                            op("pe", lambda e, pb_=pb_, W_=W_, kc=kc, cc=cc, g=g: e.matmul(pb_[:], lhsT=W_[:, kc, cc * P:(cc + 1) * P], rhs=hT[:, kc, g * GS:(g + 1) * GS],
                                                                                        start=(kc == 0), stop=(kc == 7)),
                               reads=[Wb_, hTb[g]], writes=[pbb])
                        sg, sgb, sgs = sgr.next()
                        op("act", lambda e, sg=sg, pb_=pb_: e.activation(sg[:], pb_[:], AF.Tanh, scale=0.5), reads=[pbb], writes=[sgb])
                        op("sp", lambda e, sg=sg, dst=dst, cc=cc, g=g: e.dma_start(out=dst[cc, :, g * GS:(g + 1) * GS], in_=sg[:]), reads=[sgb], dma=sgs)
            S.barrier()
            if stop_after == "ph2":
                toks = []
                toks.append(dump(dbg[0:P, 0:128], wtok[:], [Bml]))
                toks.append(dump(dbg[0:P, 128:256], cltok[:], [Bml]))
                toks.append(dump(dbg[0:P, 256:384], decbc[:], [Bml]))
                tmp = k.sb("tmpd", [P, 512], F32)
                Bt = Buf("tmpd")
                ds2 = S.dma_sem("dbg2")
                for i, (src, cc, g) in enumerate(((sga_d, 3, 5), (sgm_d, 6, 2))):
                    op("sp", lambda e, src=src, cc=cc, g=g: e.dma_start(out=tmp[:], in_=src[cc, :, g * GS:(g + 1) * GS]), writes=[Bt], dma=ds2)
                    toks.append(dump(dbg[P * (i + 1):P * (i + 2), 0:512], tmp[:], [Bt]))
                S.ops["sp"].append((tuple(toks), None, None))
                S.emit()
                return nc
            S.emit()
        with contextlib.ExitStack() as ph:
            k.pstack = ph
            QKT = k.sb("QKT", [P, 4, T], BF16)
            Vall = k.sb("Vall", [P, 32, 256], BF16)
            kmS = k.sb("kmS", [P, 64], F32)
            tmpk = k.sb("tmpk", [P, 2, 16], F32)
            kmT = k.sb("kmT", [P, 2, 16], BF16)
            kmL = k.sb("kmL", [P, 2, 16], BF16)
            W3 = k.sb("W3", [P, 8, 3, 256], BF16)
            BW3 = Buf("W3")
            w3sem = S.dma_sem("w3")
            BQK = [Buf(f"QK{g}") for g in range(NG)]
            BV = [Buf(f"V{g}") for g in range(NG)]
            Bkm = Buf("km")
            Wzr = Ring(k, "Wz", 2, [P, 8, P], BF16, dma=True)
            Utr = Ring(k, "Ut", 2, [P, 1024], F32, dma=True)
            Bya = Buf("yaT_d")
            for hp in range(4):
                if hp == 0:
                    S.dma_batch("pool", w3sem, [((lambda e, j=j, col=col: e.dma_start(out=W3[:, :, j, :], in_=w_in_r[:, :, col:col + 256])), [], [BW3])
                                                for j, col in enumerate((COL_QA, COL_KA, COL_VA))])
                with contextlib.ExitStack() as sa:
                    k.pstack = sa
                    bX = Ring(k, "bX", 2, [P, 512], F32, psum=True)
                    bY = Ring(k, "bY", 2, [P, 512], F32, psum=True)
                    bZ = Ring(k, "bZ", 2, [P, 1024], BF16, psum=True)
                    pkm = k.psum("pkm", [P, 512], F32)
                    Bpkm = Buf("pkm", psum=True)
                    sqr = Ring(k, "sq", 2, [P, 512], F32)
                    s4r = Ring(k, "ssq4", 2, [P, 4], F32)
                    qknr = Ring(k, "qkn", 3, [P, 512], BF16)
                    prevT = None

                    def emitT(tt, qkn, qknb):
                        g = tt // 4
                        bz, bzb, _ = bZ.next()
                        for j in range(4):
                            op("pe", lambda e, bz=bz, qkn=qkn, j=j: e.transpose(bz[:, j * P:(j + 1) * P], qkn[:, j * P:(j + 1) * P], ident_bf[:]), reads=[qknb, Bc], writes=[bzb])
                        op("act", lambda e, bz=bz, tt=tt: e.activation(QKT[:, 0:4, tt * P:(tt + 1) * P], bz[:, 0:512].rearrange("p (a b) -> p a b", b=P), AF.Copy),
                           reads=[bzb], writes=[BQK[g]])
                        for hh in range(2):
                            op("pe", lambda e, qkn=qkn, hh=hh, tt=tt: e.matmul(pkm[:, hh * 32 + tt:hh * 32 + tt + 1], lhsT=qkn[:, (2 + hh) * P:(3 + hh) * P], rhs=ones_bf[:, 0:1],
                                                                            start=True, stop=True),
                               reads=[qknb, Bc], writes=[Bpkm])

                    for tt in range(32):
                        g = tt // 4
                        bx, bxb, _ = bX.next()
                        for j in range(2):
                            for kc in range(8):
                                op("pe", lambda e, bx=bx, j=j, kc=kc, tt=tt: e.matmul(bx[:, j * 256:(j + 1) * 256], lhsT=hT[:, kc, tt * P:(tt + 1) * P], rhs=W3[:, kc, j, :],
                                                                                    start=(kc == 0), stop=(kc == 7)),
                                   reads=[hTb[g], BW3], writes=[bxb])
                        by, byb, _ = bY.next()
                        for kc in range(8):
                            op("pe", lambda e, by=by, kc=kc, tt=tt: e.matmul(by[:, 0:256], lhsT=hT[:, kc, tt * P:(tt + 1) * P], rhs=W3[:, kc, 2, :], start=(kc == 0), stop=(kc == 7)),
                               reads=[hTb[g], BW3], writes=[byb])
                        sq, sqb, _ = sqr.next()
                        s4, s4b, _ = s4r.next()
                        op("act", lambda e, sq=sq, bx=bx: e.activation(sq[:], bx[:], AF.Square), reads=[bxb], writes=[sqb])
                        op("dve", lambda e, sq=sq, s4=s4: e.tensor_reduce(out=s4[:], in_=sq[:].rearrange("p (a b) -> p a b", b=P), axis=AX.X, op=ALU.add), reads=[sqb], writes=[s4b])
                        op("act", lambda e, s4=s4: e.activation(s4[:], s4[:], AF.Sqrt, bias=EPS, scale=1.0 / P), reads=[s4b], writes=[s4b])
                        op("dve", lambda e, s4=s4: e.reciprocal(s4[:], s4[:]), reads=[s4b], writes=[s4b])
                        qkn, qknb, _ = qknr.next()
                        for j in range(4):
                            wbc = qw_bc if j < 2 else kw_bc
                            op("dve", lambda e, qkn=qkn, bx=bx, s4=s4, j=j, wbc=wbc: e.scalar_tensor_tensor(out=qkn[:, j * P:(j + 1) * P], in0=bx[:, j * P:(j + 1) * P], scalar=s4[:, j:j + 1],
                                                                                                          in1=wbc[:], op0=ALU.mult, op1=ALU.mult),
                               reads=[bxb, s4b, Bc], writes=[qknb])
                        op("act", lambda e, by=by, tt=tt: e.activation(Vall[:, tt, :], by[:, 0:256], AF.Copy), reads=[byb], writes=[BV[g]])
                        if prevT is not None:
                            emitT(*prevT)
                        prevT = (tt, qkn, qknb)
                    emitT(*prevT)
                    op("dve", lambda e: e.tensor_copy(kmS[:], pkm[:, 0:64]), reads=[Bpkm], writes=[Bkm])
                    kv = lambda i: kmS[:].rearrange("p (h b two) -> p h b two", h=2, two=2)[:, :, :, i]
                    op("dve", lambda e: e.tensor_tensor(tmpk[:], kv(0), kv(1), ALU.add), reads=[Bkm], writes=[Bkm])
                    op("dve", lambda e: e.tensor_scalar(tmpk[:], tmpk[:], 1.0 / 256.0, None, ALU.mult), reads=[Bkm], writes=[Bkm])
                    op("dve", lambda e: e.tensor_copy(kmT[:], tmpk[:]), reads=[Bkm], writes=[Bkm])
                    op("dve", lambda e: e.tensor_tensor(kmL[:], tmpk[:], kmT[:], ALU.subtract), reads=[Bkm], writes=[Bkm])
                    S.barrier()
                    S.emit()
                with contextlib.ExitStack() as sbk:
                    k.pstack = sbk
                    bS = Ring(k, "bS", 3, [P, 512], F32, psum=True)
                    bO = Ring(k, "bO", 2, [P, 512], F32, psum=True)
                    bSumr = Ring(k, "bSum", 2, [P, 512], F32, psum=True)
                    pnm = k.psum("pnm", [P, 1024], BF16)
                    Bpnm = Buf("pnm", psum=True)
                    tzr = Ring(k, "tz", 2, [P, 512], F32)
                    zsr = Ring(k, "zs", 3, [P, 512], F32)
                    gsbr = Ring(k, "gsb", 2, [P, 16], F32)
                    gallr = Ring(k, "gall", 2, [P, 64], F32)
                    t8r = Ring(k, "top8", 2, [P, 8], F32)
                    nmr = Ring(k, "nm", 8, [P, 16], BF16)
                    nmTr = Ring(k, "nmT", 3, [16, 512], BF16)
                    PTr = Ring(k, "PT", 4, [P, 512], BF16)
                    tmpdr = Ring(k, "tmpd", 2, [P, 512], F32)
                    recr = Ring(k, "rec", 2, [P, 512], F32)
                    osbr = Ring(k, "osb", 2, [P, 512], F32)
                    yagr = Ring(k, "yag", 2, [P, 512], BF16, dma=True)
                    heads = []
                    for hh in range(2):
                        hd = 2 * hp + hh
                        Wz, Wzb, Wzs = Wzr.next()
                        op("pool", lambda e, Wz=Wz, hd=hd: e.dma_start(out=Wz[:], in_=w_in_r[:, :, COL_ZA + hd * P:COL_ZA + (hd + 1) * P]), writes=[Wzb], dma=Wzs)
                        Ut, Utb, Uts = Utr.next()
                        op("sp", lambda e, Ut=Ut, hd=hd: e.dma_start(out=Ut[:], in_=utab[hd, :, :]), writes=[Utb], dma=Uts)
                        heads.append((hd, Wz, Wzb, Ut, Utb))
                    if hp + 1 < 4:
                        S.dma_batch("pool", w3sem, [((lambda e, j=j, col=col, hp=hp: e.dma_start(out=W3[:, :, j, :], in_=w_in_r[:, :, col + (hp + 1) * 256:col + (hp + 2) * 256])), [], [BW3])
                                                    for j, col in enumerate((COL_QA, COL_KA, COL_VA))])
                    LOOK = 2
                    for hh in range(2):
                        hd, Wz, Wzb, Ut, Utb = heads[hh]
                        gst = {}

                        def proA(g, hh=hh, Wz=Wz, Wzb=Wzb, gst=gst):
                            st_ = {}
                            bz_, bzb_, _ = bS.next()
                            for kc in range(8):
                                op("pe", lambda e, bz_=bz_, kc=kc, Wz=Wz, g=g: e.matmul(bz_[:], lhsT=Wz[:, kc, :], rhs=hT[:, kc, g * GS:(g + 1) * GS], start=(kc == 0), stop=(kc == 7)),
                                   reads=[Wzb, hTb[g]], writes=[bzb_])
                            tz, tzb, _ = tzr.next()
                            zs, zsb, _ = zsr.next()
                            op("act", lambda e, tz=tz, bz_=bz_: e.activation(tz[:], bz_[:], AF.Tanh, scale=0.5), reads=[bzb_], writes=[tzb])
                            op("dve", lambda e, zs=zs, tz=tz, bz_=bz_: e.scalar_tensor_tensor(out=zs[:], in0=tz[:], scalar=1.0, in1=bz_[:], op0=ALU.add, op1=ALU.mult),
                               reads=[tzb, bzb_], writes=[zsb])
                            st_["zs"] = (zs, zsb)
                            st_["nms"] = []
                            if g >= 2:
                                pg_, pgb_, _ = bS.next()
                                for qi in range(4):
                                    tq = 4 * g + qi
                                    op("pe", lambda e, pg_=pg_, qi=qi, tq=tq, hh=hh: e.matmul(pg_[:, qi * 16:(qi + 1) * 16], lhsT=QKT[:, hh, tq * P:(tq + 1) * P], rhs=kmT[:, hh, :], start=True, stop=False),
                                       reads=[BQK[g], Bkm], writes=[pgb_])
                                    op("pe", lambda e, pg_=pg_, qi=qi, tq=tq, hh=hh: e.matmul(pg_[:, qi * 16:(qi + 1) * 16], lhsT=QKT[:, hh, tq * P:(tq + 1) * P], rhs=kmL[:, hh, :], start=False, stop=True),
                                       reads=[BQK[g], Bkm], writes=[pgb_])
                                gall, gallb, _ = gallr.next()
                                op("dve", lambda e, gall=gall, pg_=pg_: e.tensor_copy(gall[:], pg_[:, 0:64]), reads=[pgb_], writes=[gallb])
                                for qi in range(4):
                                    qblk = 2 * g + qi // 2
                                    gsb, gsbb, _ = gsbr.next()
                                    t8, t8b, _ = t8r.next()
                                    nm, nmb, _ = nmr.next()
                                    op("dve", lambda e, gsb=gsb: e.memset(gsb[:], -1.0e30), writes=[gsbb])
                                    op("dve", lambda e, gsb=gsb, qi=qi, qblk=qblk, gall=gall: e.tensor_copy(gsb[:, 0:qblk], gall[:, qi * 16:qi * 16 + qblk]), reads=[gallb], writes=[gsbb])
                                    op("dve", lambda e, gsb=gsb, t8=t8: e.max(t8[:], gsb[:]), reads=[gsbb], writes=[t8b])
                                    op("dve", lambda e, nm=nm: e.memset(nm[:], 0.0), writes=[nmb])
                                    op("dve", lambda e, nm=nm, gsb=gsb, t8=t8, qblk=qblk: e.tensor_scalar(nm[:, 0:qblk], gsb[:, 0:qblk], t8[:, 2:3], NEGV, ALU.is_lt, ALU.mult),
                                       reads=[gsbb, t8b], writes=[nmb])
                                    st_["nms"].append((nm, nmb))
                            gst[g] = st_

                        def proB(g, gst=gst):
                            st_ = gst[g]
                            st_["nmT"] = (None, None)
                            if g >= 2:
                                for qi, (nm, nmb) in enumerate(st_["nms"]):
                                    op("pe", lambda e, nm=nm, qi=qi: e.transpose(pnm[0:16, qi * P:(qi + 1) * P], nm[:], ident_bf[:]), reads=[nmb, Bc], writes=[Bpnm])
                                nmT, nmTb, _ = nmTr.next()
                                op("act", lambda e, nmT=nmT: e.activation(nmT[:], pnm[0:16, 0:512], AF.Copy), reads=[Bpnm], writes=[nmTb])
                                st_["nmT"] = (nmT, nmTb)

                        def epilogue(g, bo, bob, bsum, bsumb, hd=hd, gst=gst):
                            zs, zsb = gst[g]["zs"]
                            rec, recb, _ = recr.next()
                            osb, osbb, _ = osbr.next()
                            yag, yagb, yags = yagr.next()
                            op("act", lambda e, osb=osb, bo=bo: e.activation(osb[:], bo[:], AF.Copy), reads=[bob], writes=[osbb])
                            op("dve", lambda e, rec=rec, bsum=bsum: e.reciprocal(rec[:], bsum[:]), reads=[bsumb], writes=[recb])
                            op("pool", lambda e, osb=osb, rec=rec: e.tensor_tensor(osb[:], osb[:], rec[:], ALU.mult), reads=[osbb, recb], writes=[osbb])
                            op("pool", lambda e, yag=yag, osb=osb, zs=zs: e.tensor_tensor(yag[:], osb[:], zs[:], ALU.mult), reads=[osbb, zsb], writes=[yagb])
                            op("sp", lambda e, yag=yag, g=g, hd=hd: e.dma_start(out=yaT_d[hd, :, g * GS:(g + 1) * GS], in_=yag[:]), reads=[yagb], writes=[Bya], dma=yags)

                        pending = []

                        def pop_pv(hh=hh):
                            g, kt, NKT, c0, Nq, pt, ptb, bo, bob, bsum, bsumb = pending.pop(0)
                            op("pe", lambda e, bo=bo, c0=c0, kt=kt, hh=hh, pt=pt, Nq=Nq, NKT=NKT: e.matmul(bo[:, c0:GS], lhsT=Vall[:, kt, hh * P:(hh + 1) * P], rhs=pt[:, 0:Nq], start=(kt == 0), stop=(kt == NKT - 1)),
                               reads=[BV[kt // 4], ptb], writes=[bob])
                            op("pe", lambda e, bsum=bsum, c0=c0, kt=kt, pt=pt, Nq=Nq, NKT=NKT: e.matmul(bsum[:, c0:GS], lhsT=ones_bf[:], rhs=pt[:, 0:Nq], start=(kt == 0), stop=(kt == NKT - 1)),
                               reads=[Bc, ptb], writes=[bsumb])
                            if kt == NKT - 1:
                                epilogue(g, bo, bob, bsum, bsumb)

                        proA(0)
                        proB(0)
                        proA(1)
                        for g in range(NG):
                            NKT = 4 * (g + 1)
                            bo, bob, _ = bO.next()
                            bsum, bsumb, _ = bSumr.next()
                            for kt in range(NKT):
                                if kt == max(2, NKT - 2) and g + 1 < NG:
                                    proB(g + 1)
                                nmT, nmTb = gst[g]["nmT"]
                                j = kt - 4 * g
                                c0 = P * j if j >= 1 else 0
                                Nq = GS - c0
                                masked = (g >= 2) and (kt // 2 <= 2 * g)
                                bs, bsb, _ = bS.next()
                                op("pe", lambda e, bs=bs, kt=kt, g=g, c0=c0, Nq=Nq, masked=masked, hh=hh: e.matmul(bs[:, 0:Nq], lhsT=QKT[:, 2 + hh, kt * P:(kt + 1) * P],
                                                                                                         rhs=QKT[:, hh, g * GS + c0:(g + 1) * GS], start=True, stop=(not masked)),
                                   reads=[BQK[kt // 4], BQK[g]], writes=[bsb])
                                if masked:
                                    bk = kt // 2
                                    op("pe", lambda e, bs=bs, bk=bk, nmT=nmT, c0=c0, Nq=Nq: e.matmul(bs[:, 0:Nq], lhsT=ind_bf[0:16, bk * P:(bk + 1) * P], rhs=nmT[0:16, c0:GS],
                                                                                                  start=False, stop=True),
                                       reads=[Bc, nmTb], writes=[bsb])
                                pt, ptb, _ = PTr.next()
                                if j >= -1:
                                    td, tdb, _ = tmpdr.next()
                                    u0 = 384 - P * j + c0
                                    op("dve", lambda e, td=td, bs=bs, u0=u0, Nq=Nq, Ut=Ut: e.scalar_tensor_tensor(out=td[:, 0:Nq], in0=bs[:, 0:Nq], scalar=ATT_SCALE, in1=Ut[:, u0:u0 + Nq],
                                                                                                        op0=ALU.mult, op1=ALU.add),
                                       reads=[bsb, Utb], writes=[tdb])
                                    op("act", lambda e, pt=pt, td=td, Nq=Nq: e.activation(pt[:, 0:Nq], td[:, 0:Nq], AF.Exp), reads=[tdb], writes=[ptb])
                                else:
                                    op("act", lambda e, pt=pt, bs=bs, hd=hd: e.activation(pt[:], bs[:], AF.Exp, bias=b31_bc[:, hd:hd + 1], scale=ATT_SCALE), reads=[bsb, Bc], writes=[ptb])
                                pending.append((g, kt, NKT, c0, Nq, pt, ptb, bo, bob, bsum, bsumb))
                                if len(pending) > LOOK:
                                    pop_pv()
                            if g + 2 < NG:
                                proA(g + 2)
                        while pending:
                            pop_pv()
                    S.barrier()
                    if stop_after == "ph3" and hp == 0:
                        toks = []
                        tb = k.sb("tmpd", [P, 1024], BF16)
                        tf = k.sb("tmpf", [P, 1024], F32)
                        Bt = Buf("tmpd")
                        Bt2 = Buf("tmpf")
                        ds2 = S.dma_sem("dbg2")
                        for hd in range(2):
                            for q4 in range(4):
                                op("sp", lambda e, hd=hd, q4=q4: e.dma_start(out=tb[:], in_=yaT_d[hd, :, q4 * 1024:(q4 + 1) * 1024]), reads=[Bya], writes=[Bt], dma=ds2)
                                op("dve", lambda e: e.tensor_copy(tf[:], tb[:]), reads=[Bt], writes=[Bt2])
                                toks.append(dump(dbg[hd * P:(hd + 1) * P, q4 * 1024:(q4 + 1) * 1024], tf[:], [Bt2]))
                        for i in range(4):
                            for q4 in range(4):
                                op("dve", lambda e, i=i, q4=q4: e.tensor_copy(tf[:], QKT[:, i, q4 * 1024:(q4 + 1) * 1024]), reads=BQK, writes=[Bt2])
                                toks.append(dump(dbg[(2 + i) * P:(3 + i) * P, q4 * 1024:(q4 + 1) * 1024], tf[:], [Bt2]))
                        S.ops["sp"].append((tuple(toks), None, None))
                        S.emit()
                        return nc
                    S.emit()
                k.pstack = ph
        k.pstack = hst
        Wa = k.sb("Wa", [P, 8, D], BF16)
        Wm = k.sb("Wm", [P, 8, D], BF16)
        Wo = k.sb("Wo", [P, 8, D], BF16)
        BWa, BWm, BWo = Buf("Wa"), Buf("Wm"), Buf("Wo")
        with contextlib.ExitStack() as ph:
            k.pstack = ph
            W5 = k.sb("W5", [P, 8, 5, 256], BF16)
            BW5s = [Buf(f"W5_{j}") for j in range(5)]
            w5sems = [S.dma_sem(f"w5_{j}") for j in range(5)]
            Cst = k.sb("Cst", [P, 2, 258], F32)
            Cbf = k.sb("Cbf", [P, 2, 258], BF16)
            BC = Buf("Cst")
            BCbf = Buf("Cbf")
            pre = [[k.sb(f"pre{w}{dc}", [P, 516], BF16) for dc in range(2)] for w in range(2)]
            Bpre = [[Buf(f"pre{w}{dc}") for dc in range(2)] for w in range(2)]
            thr_ = Ring(k, "th", 2, [P, 512], F32)
            QmTr = Ring(k, "QmT", 2, [P, 2, 512], BF16)
            KmTr = Ring(k, "KmT", 2, [P, 2, 512], BF16)
            Ktokr = Ring(k, "Ktok", 2, [P, 4, 256], BF16)
            Vpr = Ring(k, "Vp", 2, [P, 258], BF16)
            Smr = Ring(k, "Sm", 2, [P, P], BF16)
            thozr = Ring(k, "thoz", 2, [P, 512], F32)
            dnr = Ring(k, "dn", 2, [P, 1], F32)
            hm2r = Ring(k, "hm2", 2, [P, 4, 256], F32)
            szr = Ring(k, "sz", 2, [P, 4, 256], F32)
            ssqmr = Ring(k, "ssqm", 2, [P, 4], F32)
            junk2 = k.sb("junk2", [P, 256], F32)
            Bjunk2 = Buf("junk2")
            t1r = Ring(k, "t1", 2, [P, 256], F32)
            ymgr = Ring(k, "ymg", 2, [P, 256], BF16)
            ymTsr = Ring(k, "ymTs", 2, [P, 2, 512], BF16, dma=True)
            bQK = Ring(k, "bQK", 2, [P, 512], F32, psum=True)
            bT = k.psum("bT", [P, 1024], BF16)
            BbT = Buf("bT", psum=True)
            bT2 = bT
            BbT2 = BbT
            diag = k.sb("diag", [P, 16, P], BF16)
            Bdiag = Buf("diag")
            caus_s = k.sb("caus_s", [P, P], F32)
            op("dve", lambda e: e.tensor_scalar(caus_s[:], caus[:], ML_KSCALE, None, ALU.mult), reads=[Bc], writes=[Bdiag])
            bV = k.psum("bV", [P, 512], F32)
            BbV = Buf("bV", psum=True)
            bOZ = k.psum("bOZ", [P, 512], F32)
            BbOZ = Buf("bOZ", psum=True)
            bOut = k.psum("bOut", [P, 512], F32)
            BbOut = Buf("bOut", psum=True)
            bCp = [k.psum(f"bC{i}", [P, 512], F32) for i in range(2)]
            BbCp = [Buf(f"bC{i}", psum=True) for i in range(2)]
            Bym = Buf("ymT_d")
            for mh in range(4):
                for j, col in enumerate((COL_QM, COL_KM, COL_VM, COL_ZM, COL_OM)):
                    op("pool", lambda e, j=j, col=col, mh=mh: e.dma_start(out=W5[:, :, j, :], in_=w_in_r[:, :, col + mh * 256:col + (mh + 1) * 256]), writes=[BW5s[j]], dma=w5sems[j])
                if mh == 0:
                    for W_, Wb_, src in ((Wa, BWa, w_att), (Wm, BWm, w_ml), (Wo, BWo, w_out)):
                        sm_ = S.dma_sem(f"w5_{k.uid()}")
                        op("pool", lambda e, W_=W_, src=src: e.dma_start(out=W_[:], in_=src.rearrange("(kc p) n -> p kc n", p=P)), writes=[Wb_], dma=sm_)
                if mh == 2:
                    op("dve", lambda e: e.tensor_scalar(Wa[:], Wa[:], 0.5, None, ALU.mult), reads=[BWa], writes=[BWa])
                op("dve", lambda e: e.memset(Cst[:], 0.0), writes=[BC])
                for w in range(2):
                    for dc in range(2):
                        op("dve", lambda e, w=w, dc=dc: e.memset(pre[w][dc][:, 0:3], 0.0), writes=[Bpre[w][dc]])
                for w in range(2):
                    for dc in range(2):
                        for j in range(4):
                            op("dve", lambda e, w=w, dc=dc, j=j, mh=mh: e.tensor_scalar(diag[:, (w * 2 + dc) * 4 + j, :], ident[:], convw_sb[:, w * 8 + mh * 2 + dc, j:j + 1], None, ALU.mult),
                               reads=[Bc], writes=[Bdiag])

                def qk_piece(g, w, dc, dstT, dstb, mh=mh):
                    bq, bqb, _ = bQK.next()
                    for kc in range(8):
                        op("pe", lambda e, bq=bq, kc=kc: e.matmul(bq[:], lhsT=W5[:, kc, w, dc * P:(dc + 1) * P], rhs=hT[:, kc, g * GS:(g + 1) * GS], start=(kc == 0), stop=(kc == 7)),
                           reads=[BW5s[w], hTb[g]], writes=[bqb])
                    pr = pre[w][dc]
                    prb = Bpre[w][dc]
                    cidx = w * 8 + mh * 2 + dc
                    didx = (w * 2 + dc) * 4
                    op("act", lambda e: e.activation(pr[:, 3:515], bq[:], AF.Copy), reads=[bqb], writes=[prb])

                    def part2():
                        bcv, bcvb, _ = bQK.next()
                        for j in range(4):
                            op("pe", lambda e, j=j: e.matmul(bcv[:], lhsT=diag[:, didx + j, :], rhs=pr[:, j:j + 512], start=(j == 0), stop=(j == 3)), reads=[Bdiag, prb], writes=[bcvb])
                        op("pool", lambda e: e.tensor_copy(pr[:, 0:3], pr[:, 512:515]), reads=[prb], writes=[prb])
                        op("act", lambda e: e.activation(dstT[:, dc, :], bcv[:], AF.Silu, bias=convb_sb[:, cidx:cidx + 1], scale=1.0), reads=[bcvb, Bc], writes=[dstb])
                    return part2

                pieces = [(w, dc) for w in range(2) for dc in range(2)]
                nxt = (QmTr.next(), KmTr.next())
                for (w, dc) in pieces:
                    dd = nxt[w]
                    qk_piece(0, w, dc, dd[0], dd[1])()

                def gend_dve(gs, c4, mh=mh):
                    hm2, hm2b, sz, szb, ssqm, ssqmb = gs["bufs"]
                    t1, t1b, _ = t1r.next()
                    ymg, ymgb, _ = ymgr.next()
                    op("dve", lambda e: e.scalar_tensor_tensor(out=t1[:], in0=hm2[:, c4, :], scalar=ssqm[:, c4:c4 + 1], in1=mlw_bc[:, mh * 256:(mh + 1) * 256], op0=ALU.mult, op1=ALU.mult),
                       reads=[hm2b, ssqmb, Bc], writes=[t1b])
                    op("pool", lambda e: e.tensor_tensor(ymg[:], t1[:], sz[:, c4, :], ALU.mult), reads=[t1b, szb], writes=[ymgb])
                    gs.setdefault("ymgs", {})[c4] = (ymg, ymgb)

                def gend_sqrt(gs):
                    hm2, hm2b, sz, szb, ssqm, ssqmb = gs["bufs"]
                    op("act", lambda e: e.activation(ssqm[:], ssqm[:], AF.Sqrt, bias=EPS, scale=1.0 / 256.0), reads=[ssqmb], writes=[ssqmb])
                    op("dve", lambda e: e.reciprocal(ssqm[:], ssqm[:]), reads=[ssqmb], writes=[ssqmb])

                def gend_pe(gs, c4, mh=mh):
                    ymg, ymgb = gs["ymgs"][c4]
                    g_ = gs["g"]
                    for dc in range(2):
                        op("pe", lambda e, dc=dc: e.transpose(bT2[:, (dc * 4 + c4) * P:(dc * 4 + c4 + 1) * P], ymg[:, dc * P:(dc + 1) * P], ident_bf[:]), reads=[ymgb, Bc], writes=[BbT2])
                    if c4 == 3:
                        ymTs, ymTsb, ymTss = ymTsr.next()
                        op("act", lambda e: e.activation(ymTs[:], bT2[:, 0:1024].rearrange("p (a b) -> p a b", b=512), AF.Copy), reads=[BbT2], writes=[ymTsb])
                        S.dma_batch("sp", ymTss, [((lambda e, dc=dc: e.dma_start(out=ymT_d[mh * 2 + dc, :, g_ * GS:(g_ + 1) * GS], in_=ymTs[:, dc, :])), [ymTsb], [Bym])
                                                  for dc in range(2)])

                prevg = None
                pend_tail = []
                for g in range(NG):
                    (QmT, QmTb, _), (KmT, KmTb, _) = nxt
                    if g + 1 < NG:
                        nxt = (QmTr.next(), KmTr.next())
                    Ktok, Ktokb, _ = Ktokr.next()
                    for c4 in range(4):
                        for dc in range(2):
                            op("pe", lambda e, KmT=KmT, c4=c4, dc=dc: e.transpose(bT[:, c4 * 256 + dc * P:c4 * 256 + (dc + 1) * P], KmT[:, dc, c4 * P:(c4 + 1) * P], ident_bf[:]),
                               reads=[KmTb, Bc], writes=[BbT])
                    op("act", lambda e, Ktok=Ktok: e.activation(Ktok[:], bT[:, 0:1024].rearrange("p (a b) -> p a b", b=256), AF.Copy), reads=[BbT], writes=[Ktokb])
                    hm2, hm2b, _ = hm2r.next()
                    sz, szb, _ = szr.next()
                    ssqm, ssqmb, _ = ssqmr.next()
                    curg = {"g": g, "bufs": (hm2, hm2b, sz, szb, ssqm, ssqmb)}
                    for c4 in range(4):
                        ci = g * 4 + c4
                        t0 = ci * P
                        col = ci * 4 + mh
                        part2 = None
                        if g + 1 < NG:
                            w, dc = pieces[c4]
                            dd = nxt[w]
                            part2 = qk_piece(g + 1, w, dc, dd[0], dd[1])
                        while pend_tail:
                            pend_tail.pop(0)()
                        if prevg is not None:
                            if c4 == 1:
                                gend_dve(prevg, 0)
                            if c4 >= 1:
                                gend_dve(prevg, c4)
                        for dc in range(2):
                            op("pe", lambda e, KmT=KmT, QmT=QmT, dc=dc, c4=c4: e.matmul(bV[:, 256:384], lhsT=KmT[:, dc, c4 * P:(c4 + 1) * P], rhs=QmT[:, dc, c4 * P:(c4 + 1) * P],
                                                                                     start=(dc == 0), stop=(dc == 1)),
                               reads=[KmTb, QmTb], writes=[BbV])
                        for kc in range(8):
                            op("pe", lambda e, kc=kc, t0=t0: e.matmul(bV[:, 0:256], lhsT=hT[:, kc, t0:t0 + P], rhs=W5[:, kc, 2, :], start=(kc == 0), stop=(kc == 7)),
                               reads=[hTb[g], BW5s[2]], writes=[BbV])
                        Sm, Smb, _ = Smr.next()
                        op("dve", lambda e, Sm=Sm: e.tensor_tensor(Sm[:], bV[:, 256:384], caus_s[:], ALU.mult), reads=[BbV, Bdiag], writes=[Smb])
                        Vp, Vpb, _ = Vpr.next()
                        op("dve", lambda e, Vp=Vp, col=col: e.tensor_scalar(Vp[:, 0:256], bV[:, 0:256], wtok[:, col:col + 1], None, ALU.mult), reads=[BbV, Bml], writes=[Vpb])
                        op("pool", lambda e, Vp=Vp, col=col: e.tensor_copy(Vp[:, 256:258], wtok[:, col:col + 1].to_broadcast([P, 2])), reads=[Bml], writes=[Vpb])
                        if ci > 0:
                            op("pool", lambda e, col=col: e.tensor_scalar(Cbf[:], Cst[:], decbc[:, col:col + 1], ML_KSCALE, ALU.mult, ALU.mult), reads=[BC, Bml], writes=[BCbf])
                        if part2 is not None:
                            part2()
                        for jj, wi in enumerate((4, 3)):
                            for kc in range(8):
                                op("pe", lambda e, kc=kc, t0=t0, jj=jj, wi=wi: e.matmul(bOZ[:, jj * 256:(jj + 1) * 256], lhsT=hT[:, kc, t0:t0 + P], rhs=W5[:, kc, wi, :], start=(kc == 0), stop=(kc == 7)),
                                   reads=[hTb[g], BW5s[wi]], writes=[BbOZ])
                        thoz, thozb, _ = thozr.next()
                        op("act", lambda e, thoz=thoz: e.activation(thoz[:, 0:256], bOZ[:, 0:256], AF.Tanh, scale=0.5), reads=[BbOZ], writes=[thozb])
                        op("act", lambda e, sz=sz, c4=c4: e.activation(sz[:, c4, :], bOZ[:, 256:512], AF.Silu), reads=[BbOZ], writes=[szb])
                        op("pool", lambda e, thoz=thoz: e.tensor_scalar(thoz[:, 0:256], thoz[:, 0:256], 0.5, 0.5, ALU.mult, ALU.add), reads=[thozb], writes=[thozb])
                        if ci > 0:
                            for dc in range(2):
                                op("pe", lambda e, QmT=QmT, dc=dc, c4=c4: e.matmul(bOut[:, 0:257], lhsT=QmT[:, dc, c4 * P:(c4 + 1) * P], rhs=Cbf[:, dc, 0:257], start=(dc == 0), stop=False),
                                   reads=[QmTb, BCbf], writes=[BbOut])
                        op("pe", lambda e, Sm=Sm, Vp=Vp, ci=ci: e.matmul(bOut[:, 0:257], lhsT=Sm[:], rhs=Vp[:, 0:257], start=(ci == 0), stop=True), reads=[Smb, Vpb], writes=[BbOut])
                        for dkc in range(2):
                            op("pe", lambda e, Ktok=Ktok, c4=c4, dkc=dkc, Vp=Vp: e.matmul(bCp[dkc][:, 0:257], lhsT=Ktok[:, c4, dkc * P:(dkc + 1) * P], rhs=Vp[:, 0:257], start=True, stop=True),
                               reads=[Ktokb, Vpb], writes=[BbCp[dkc]])
                            op("dve", lambda e, dkc=dkc, col=col: e.scalar_tensor_tensor(out=Cst[:, dkc, 0:257], in0=Cst[:, dkc, 0:257], scalar=decbc[:, col:col + 1], in1=bCp[dkc][:, 0:257],
                                                                                      op0=ALU.mult, op1=ALU.add),
                               reads=[BC, Bml, BbCp[dkc]], writes=[BC])
                        def tail(hm2=hm2, hm2b=hm2b, ssqm=ssqm, ssqmb=ssqmb, c4=c4, col=col, thoz=thoz, thozb=thozb):
                            dn, dnb, _ = dnr.next()
                            op("dve", lambda e: e.tensor_scalar(dn[:], bOut[:, 256:257], -1.0, cltok[:, col:col + 1], ALU.mult, ALU.max), reads=[BbOut, Bml], writes=[dnb])
                            op("dve", lambda e: e.tensor_tensor(dn[:], dn[:], bOut[:, 256:257], ALU.max), reads=[BbOut, dnb], writes=[dnb])
                            op("dve", lambda e: e.reciprocal(dn[:], dn[:]), reads=[dnb], writes=[dnb])
                            op("dve", lambda e: e.scalar_tensor_tensor(out=hm2[:, c4, :], in0=bOut[:, 0:256], scalar=dn[:, 0:1], in1=thoz[:, 0:256], op0=ALU.mult, op1=ALU.mult),
                               reads=[BbOut, dnb, thozb], writes=[hm2b])
                            op("act", lambda e: e.activation(junk2[:], hm2[:, c4, :], AF.Square, accum_out=ssqm[:, c4:c4 + 1]), reads=[hm2b], writes=[Bjunk2, ssqmb])
                        pend_tail.append(tail)
                        if prevg is not None:
                            if c4 == 0:
                                gend_sqrt(prevg)
                            else:
                                gend_pe(prevg, c4 - 1)
                                if c4 == 3:
                                    gend_pe(prevg, 3)
                    prevg = curg
                while pend_tail:
                    pend_tail.pop(0)()
                gend_sqrt(prevg)
                for c4 in range(4):
                    gend_dve(prevg, c4)
                    gend_pe(prevg, c4)
                if stop_after == "ph4" and mh == 0:
                    S.barrier()
                    toks = []
                    tb = k.sb("tmpd", [P, 1024], BF16)
                    tf = k.sb("tmpf", [P, 1024], F32)
                    Bt = Buf("tmpd")
                    Bt2 = Buf("tmpf")
                    ds2 = S.dma_sem("dbg2")
                    for kc in range(2):
                        for q4 in range(4):
                            op("sp", lambda e, kc=kc, q4=q4: e.dma_start(out=tb[:], in_=ymT_d[kc, :, q4 * 1024:(q4 + 1) * 1024]), reads=[Bym], writes=[Bt], dma=ds2)
                            op("dve", lambda e: e.tensor_copy(tf[:], tb[:]), reads=[Bt], writes=[Bt2])
                            toks.append(dump(dbg[kc * P:(kc + 1) * P, q4 * 1024:(q4 + 1) * 1024], tf[:], [Bt2]))
                    S.ops["sp"].append((tuple(toks), None, None))
                    S.emit()
                    return nc
            S.barrier()
            S.emit()
        with contextlib.ExitStack() as ph:
            k.pstack = ph
            yaTr = Ring(k, "yaTg", 2, [P, 8, GS], BF16, dma=True)
            ymTr = Ring(k, "ymTg", 2, [P, 8, GS], BF16, dma=True)
            sgar = Ring(k, "sga", 3, [P, GS], F32, dma=True)
            sgmr = Ring(k, "sgm", 3, [P, GS], F32, dma=True)
            y1r = Ring(k, "y1", 2, [P, GS], F32)
            y2r = Ring(k, "y2", 2, [P, GS], F32)
            yTr = Ring(k, "yT", 1, [P, 8, GS], BF16)
            xr5 = Ring(k, "x5", 2, [P, D], F32, dma=True)
            otr = Ring(k, "ot", 2, [P, D], F32, dma=True)
            bA = Ring(k, "bA", 3, [P, 512], F32, psum=True)
            bM = Ring(k, "bM", 3, [P, 512], F32, psum=True)
            bF = Ring(k, "bF", 2, [P, 512], F32, psum=True)
            def load_branch(g):
                yaTg, yaTgb, yas = yaTr.next()
                ymTg, ymTgb, yms = ymTr.next()
                op("sp", lambda e: e.dma_start(out=yaTg[:], in_=yaT_d[:, :, g * GS:(g + 1) * GS].rearrange("k p t -> p k t")), writes=[yaTgb], dma=yas)
                op("sp", lambda e: e.dma_start(out=ymTg[:], in_=ymT_d[:, :, g * GS:(g + 1) * GS].rearrange("k p t -> p k t")), writes=[ymTgb], dma=yms)
                return yaTg, yaTgb, ymTg, ymTgb

            nxt_br = load_branch(0)
            for g in range(NG):
                yaTg, yaTgb, ymTg, ymTgb = nxt_br
                if g + 1 < NG:
                    nxt_br = load_branch(g + 1)
                yT, yTb, _ = yTr.next()
                for cc in range(8):
                    sga, sgab, sgas = sgar.next()
                    sgm, sgmb, sgms = sgmr.next()
                    op("sp", lambda e, sga=sga, cc=cc, g=g: e.dma_start(out=sga[:], in_=sga_d[cc, :, g * GS:(g + 1) * GS]), writes=[sgab], dma=sgas)
                    op("sp", lambda e, sgm=sgm, cc=cc, g=g: e.dma_start(out=sgm[:], in_=sgm_d[cc, :, g * GS:(g + 1) * GS]), writes=[sgmb], dma=sgms)
                    ba, bab, _ = bA.next()
                    bm, bmb, _ = bM.next()
                    for kc in range(8):
                        op("pe", lambda e, ba=ba, kc=kc, cc=cc, yaTg=yaTg: e.matmul(ba[:], lhsT=Wa[:, kc, cc * P:(cc + 1) * P], rhs=yaTg[:, kc, :], start=(kc == 0), stop=(kc == 7)),
                           reads=[BWa, yaTgb], writes=[bab])
                    for kc in range(8):
                        op("pe", lambda e, bm=bm, kc=kc, cc=cc, ymTg=ymTg: e.matmul(bm[:], lhsT=Wm[:, kc, cc * P:(cc + 1) * P], rhs=ymTg[:, kc, :], start=(kc == 0), stop=(kc == 7)),
                           reads=[BWm, ymTgb], writes=[bmb])
                    y1, y1b, _ = y1r.next()
                    y2, y2b, _ = y2r.next()
                    op("dve", lambda e, y1=y1, sga=sga, ba=ba: e.scalar_tensor_tensor(out=y1[:], in0=sga[:], scalar=1.0, in1=ba[:], op0=ALU.add, op1=ALU.mult), reads=[sgab, bab], writes=[y1b])
                    op("dve", lambda e, y2=y2, sgm=sgm, bm=bm: e.scalar_tensor_tensor(out=y2[:], in0=sgm[:], scalar=1.0, in1=bm[:], op0=ALU.add, op1=ALU.mult), reads=[sgmb, bmb], writes=[y2b])
                    op("pool", lambda e, yT=yT, cc=cc, y1=y1, y2=y2: e.tensor_tensor(yT[:, cc, :], y1[:], y2[:], ALU.add), reads=[y1b, y2b], writes=[yTb])
                for tt in range(4):
                    ti = g * 4 + tt
                    xt, xb, xs = xr5.next()
                    ot, otb, ots = otr.next()
                    op("sp", lambda e, xt=xt, ti=ti: e.dma_start(out=xt[:], in_=x[ti * P:(ti + 1) * P, :]), writes=[xb], dma=xs)
                    for og in range(2):
                        bf_, bfb, _ = bF.next()
                        for cc in range(8):
                            op("pe", lambda e, bf_=bf_, cc=cc, tt=tt, og=og, yT=yT: e.matmul(bf_[:], lhsT=yT[:, cc, tt * P:(tt + 1) * P], rhs=Wo[:, cc, og * 512:(og + 1) * 512], start=(cc == 0), stop=(cc == 7)),
                               reads=[yTb, BWo], writes=[bfb])
                        op("dve", lambda e, ot=ot, bf_=bf_, og=og: e.tensor_tensor(ot[:, og * 512:(og + 1) * 512], bf_[:], gate_half[:, og * 512:(og + 1) * 512], ALU.mult), reads=[bfb, Bgate], writes=[otb])
                    op("pool", lambda e, ot=ot, xt=xt: e.tensor_tensor(ot[:], ot[:], xt[:], ALU.add), reads=[otb, xb], writes=[otb])
                    op("act", lambda e, ot=ot, ti=ti: e.dma_start(out=out[ti * P:(ti + 1) * P, :], in_=ot[:]), reads=[otb], dma=ots)
            S.barrier()
            S.emit()
    return nc


def _host_inputs(inputs):
    f = np.float32
    x = np.ascontiguousarray(inputs["x"], dtype=f)
    c = np.asarray(inputs["c"], dtype=f)
    rel_bias = np.asarray(inputs["rel_bias"], dtype=f)
    dist = np.arange(0, 1024)
    max_exact = 16
    nf = np.maximum(dist, 1).astype(np.float32)
    large = max_exact + (np.log(nf / max_exact) / np.log(128 / max_exact) * (32 - max_exact)).astype(np.int32)
    large = np.minimum(large, 31)
    bucket = np.where(dist < max_exact, dist, large)
    kk = np.arange(128)[:, None]
    jj = np.arange(1024)[None, :]
    dd = jj - 384 - kk
    valid = dd >= 0
    bidx = bucket[np.clip(dd, 0, 1023)]
    utab = np.empty((8, 128, 1024), dtype=f)
    for h in range(8):
        g = rel_bias[:, h][bidx]
        utab[h] = np.where(valid, g, f(NEGV))
    conv_w = np.asarray(inputs["conv_w"], dtype=f)[0]
    conv_b = np.asarray(inputs["conv_b"], dtype=f)[0]
    convw = np.ascontiguousarray(conv_w.T.reshape(16, 128, 4).transpose(1, 0, 2))
    convb = np.ascontiguousarray(conv_b.reshape(16, 128).T)
    ident = np.eye(128, dtype=f)
    caus = np.triu(np.ones((128, 128), dtype=f))
    ind = np.zeros((16, 16, 128), dtype=f)
    for b in range(16):
        ind[b, b, :] = 1.0
    common = {
        "w_ada": np.ascontiguousarray(inputs["w_ada"][0], dtype=f),
        "b_ada": np.ascontiguousarray(inputs["b_ada"][0:1], dtype=f),
        "norm_w": np.ascontiguousarray(inputs["norm_w"][0:1], dtype=f),
        "w_in": np.ascontiguousarray(inputs["w_in"][0], dtype=f),
        "qnw": np.ascontiguousarray(inputs["q_norm_w"][0:1], dtype=f),
        "knw": np.ascontiguousarray(inputs["k_norm_w"][0:1], dtype=f),
        "utab": utab,
        "bias31": np.ascontiguousarray(rel_bias[31:32, :]),
        "convw": convw,
        "convb": convb,
        "b_ig": np.ascontiguousarray(np.asarray(inputs["b_igate"], dtype=f)[0].reshape(4, 1)),
        "b_fg": np.ascontiguousarray(np.asarray(inputs["b_fgate"], dtype=f)[0].reshape(4, 1)),
        "mlnw": np.ascontiguousarray(inputs["ml_norm_w"][0:1], dtype=f),
        "w_att": np.ascontiguousarray(inputs["w_att_proj"][0], dtype=f),
        "w_ml": np.ascontiguousarray(inputs["w_ml_proj"][0], dtype=f),
        "w_out": np.ascontiguousarray(inputs["w_out"][0], dtype=f),
        "c_ident": ident,
        "c_caus": caus,
        "c_ind": ind.reshape(16, 16 * 128),
    }
    maps = []
    for b in range(x.shape[0]):
        m = dict(common)
        m["x"] = x[b]
        m["ccol"] = np.ascontiguousarray(c[b].reshape(8, 128).T)
        maps.append(m)
    return maps


def kernel(**inputs):
    maps = _host_inputs(inputs)
    nc = build()
    res = run_bass_kernel_spmd(nc, maps, core_ids=list(range(8)))
    return np.stack([np.asarray(r["out"], dtype=np.float32) for r in res.results], axis=0)
```

```python
import contextlib
import numpy as np
import ml_dtypes
import concourse.bass as bass
import concourse.mybir as mybir
from concourse.bass_utils import run_bass_kernel_spmd

F32 = mybir.dt.float32
BF16 = mybir.dt.bfloat16
AF = mybir.ActivationFunctionType
ALU = mybir.AluOpType
AX = mybir.AxisListType

T = 4096
D = 1024
P = 128
NKC = 8
NG = 8
GS = 512
EPS = 1e-6
NEGV = -30000.0
ATT_SCALE = 128.0 ** -0.5
ML_KSCALE = 256.0 ** -0.5
COL_QA, COL_KA, COL_VA, COL_ZA = 0, 1024, 2048, 3072
COL_QM, COL_KM, COL_VM, COL_ZM, COL_OM = 4096, 5120, 6144, 7168, 8192
COL_IG, COL_FG, COL_GA, COL_GM = 9216, 9220, 9224, 10248
D_IN = 11272


class Buf:
    __slots__ = ("w", "r", "name", "psum")

    def __init__(self, name="", psum=False):
        self.w = None
        self.r = {}
        self.name = name
        self.psum = psum


class Sched:
    ENGS = ("pe", "act", "dve", "pool", "sp")

    def __init__(self, nc, stack):
        self.nc = nc
        self.stack = stack
        self.ops = {e: [] for e in self.ENGS}
        self.cnt = {e: 0 for e in self.ENGS}
        self.waited = {e: {} for e in self.ENGS}
        self.sems = {}
        self.dcnt = {}
        for e in self.ENGS:
            self.sems[e] = stack.enter_context(nc.semaphore("s_" + e))

    def dma_sem(self, name):
        self.sems[name] = self.stack.enter_context(self.nc.semaphore("d_" + name))
        self.dcnt[name] = 0
        return name

    def dma_batch(self, eng, sem, items):
        n = len(items)
        toks = []
        for i, (fn, reads, writes) in enumerate(items):
            toks.append(self.op(eng, fn, reads=reads, writes=writes, dma=sem, dma_extra=n - 1 - i))
        return toks

    def op(self, eng, fn, reads=(), writes=(), dma=None, dma_extra=0):
        waits = {}
        wd = self.waited[eng]

        def need(dep):
            if dep is None:
                return
            sk, val = dep
            if sk == "pe" and eng == "pe":
                return
            if dma is not None and sk == dma and val > self.dcnt[dma]:
                return
            if wd.get(sk, 0) >= val:
                return
            if waits.get(sk, 0) < val:
                waits[sk] = val

        for b in reads:
            need(b.w)
            if b.psum:
                for sk, v in b.r.items():
                    if sk != eng:
                        need((sk, v))
        for b in writes:
            need(b.w)
            for sk, v in b.r.items():
                need((sk, v))
        for sk, v in waits.items():
            wd[sk] = v
        if dma is None:
            self.cnt[eng] += 1
            tok = (eng, self.cnt[eng])
            inc = (eng, 1)
        else:
            self.dcnt[dma] += 16
            tok = (dma, self.dcnt[dma] + 16 * dma_extra)
            inc = (dma, 16)
        for b in writes:
            b.w = tok
            b.r = {}
        for b in reads:
            if b.r.get(tok[0], 0) < tok[1]:
                b.r[tok[0]] = tok[1]
        self.ops[eng].append((tuple(waits.items()), fn, inc))
        return tok

    def barrier(self):
        toks = [(e, self.cnt[e]) for e in self.ENGS if self.cnt[e] > 0]
        toks += [(k, v) for k, v in self.dcnt.items() if v > 0]
        for e in self.ENGS:
            w = []
            for sk, v in toks:
                if sk == e:
                    continue
                if self.waited[e].get(sk, 0) < v:
                    self.waited[e][sk] = v
                    w.append((sk, v))
            self.ops[e].append((tuple(w), None, None))

    def emit(self):
        nc = self.nc
        sems = self.sems
        ops_now = self.ops
        self.ops = {e: [] for e in self.ENGS}

        def replay(ename, eng):
            for waits, fn, inc in ops_now[ename]:
                for sk, v in waits:
                    eng.wait_ge(sems[sk], v)
                if fn is None:
                    continue
                ins = fn(eng)
                ins.then_inc(sems[inc[0]], inc[1])

        with nc.Block() as block:
            @block.tensor
            def _(e):
                replay("pe", e)

            @block.scalar
            def _(e):
                replay("act", e)

            @block.vector
            def _(e):
                replay("dve", e)

            @block.gpsimd
            def _(e):
                replay("pool", e)

            @block.sync
            def _(e):
                replay("sp", e)


class Ring:
    def __init__(self, K, name, n, shape, dt, psum=False, dma=False):
        self.n = n
        self.t = []
        self.b = []
        self.s = []
        for i in range(n):
            if psum:
                self.t.append(K.psum(f"{name}{i}", shape, dt))
            else:
                self.t.append(K.sb(f"{name}{i}", shape, dt))
            self.b.append(Buf(f"{name}{i}", psum=psum))
            self.s.append(K.S.dma_sem(f"{name}{i}_{K.uid()}") if dma else None)
        self.i = 0

    def next(self):
        i = self.i % self.n
        self.i += 1
        return self.t[i], self.b[i], self.s[i]


class K:
    def __init__(self, nc, S, stack):
        self.nc = nc
        self.S = S
        self.stack = stack
        self.pstack = stack
        self._uid = 0

    def uid(self):
        self._uid += 1
        return self._uid

    def sb(self, name, shape, dt):
        return self.pstack.enter_context(self.nc.sbuf_tensor(f"{name}_{self.uid()}", shape, dt))

    def psum(self, name, shape, dt):
        return self.pstack.enter_context(self.nc.psum_tensor(f"{name}_{self.uid()}", shape, dt))


def build(stop_after=None, dbg_shape=None):
    nc = bass.Bass("TRN2", target_bir_lowering=False)

    def din(name, shape, dt=F32):
        return nc.dram_tensor(name, shape, dt, kind="ExternalInput").ap()

    x = din("x", [T, D])
    ccol = din("ccol", [P, 8])
    w_ada = din("w_ada", [D, 3 * D])
    b_ada = din("b_ada", [1, 3 * D])
    norm_w = din("norm_w", [1, D])
    w_in = din("w_in", [D, D_IN])
    qnw = din("qnw", [1, P])
    knw = din("knw", [1, P])
    utab = din("utab", [8, P, 1024])
    bias31 = din("bias31", [1, 8])
    convw = din("convw", [P, 16, 4])
    convb = din("convb", [P, 16])
    b_ig = din("b_ig", [4, 1])
    b_fg = din("b_fg", [4, 1])
    mlnw = din("mlnw", [1, D])
    w_att = din("w_att", [D, D])
    w_ml = din("w_ml", [D, D])
    w_out = din("w_out", [D, D])
    c_ident = din("c_ident", [P, P])
    c_caus = din("c_caus", [P, P])
    c_ind = din("c_ind", [16, 16 * P])
    out = nc.dram_tensor("out", [T, D], F32, kind="ExternalOutput").ap()
    dbg = None
    if dbg_shape is not None:
        dbg = nc.dram_tensor("dbg", list(dbg_shape), F32, kind="ExternalOutput").ap()
    yaT_d = nc.dram_tensor("yaT_d", [8, P, T], BF16, kind="Internal").ap()
    ymT_d = nc.dram_tensor("ymT_d", [8, P, T], BF16, kind="Internal").ap()
    sga_d = nc.dram_tensor("sga_d", [8, P, T], F32, kind="Internal").ap()
    sgm_d = nc.dram_tensor("sgm_d", [8, P, T], F32, kind="Internal").ap()

    w_in_r = w_in.rearrange("(kc p) n -> p kc n", p=P)

    with contextlib.ExitStack() as st:
        S = Sched(nc, st)
        k = K(nc, S, st)
        op = S.op
        ident = k.sb("ident", [P, P], F32)
        ident_bf = k.sb("ident_bf", [P, P], BF16)
        ones_bf = k.sb("ones_bf", [P, P], BF16)
        ones_f = k.sb("ones_f", [P, P], F32)
        caus = k.sb("caus", [P, P], F32)
        ind_bf = k.sb("ind_bf", [16, 16 * P], BF16)
        gate_half = k.sb("gate_half", [P, D], F32)
        mlw_bc = k.sb("mlw_bc", [P, D], F32)
        qw_bc = k.sb("qw_bc", [P, P], F32)
        kw_bc = k.sb("kw_bc", [P, P], F32)
        b31_bc = k.sb("b31_bc", [P, 8], F32)
        convw_sb = k.sb("convw_sb", [P, 16, 4], F32)
        convb_sb = k.sb("convb_sb", [P, 16], F32)
        wtok = k.sb("wtok", [P, 128], F32)
        cltok = k.sb("cltok", [P, 128], F32)
        decbc = k.sb("decbc", [P, 128], F32)
        Bc = Buf("consts")
        Bgate = Buf("gate_half")
        Bml = Buf("mlgates")
        hst = st.enter_context(contextlib.ExitStack())
        k.pstack = hst
        hT = k.sb("hT", [P, NKC, T], BF16)
        hTb = [Buf(f"hT{g}") for g in range(NG)]
        k.pstack = st
        dsem = S.dma_sem("const")
        dsem_out = S.dma_sem("dbgout")

        cl = [(ident[:], c_ident[:, :]), (caus[:], c_caus[:, :]),
              (mlw_bc[:], mlnw[0:1, :].partition_broadcast(P)), (qw_bc[:], qnw[0:1, :].partition_broadcast(P)),
              (kw_bc[:], knw[0:1, :].partition_broadcast(P)), (b31_bc[:], bias31[0:1, :].partition_broadcast(P)),
              (convw_sb[:], convw[:, :, :]), (convb_sb[:], convb[:, :])]
        S.dma_batch("sp", dsem, [((lambda e, d_=d_, s_=s_: e.dma_start(out=d_, in_=s_)), [], [Bc]) for d_, s_ in cl])
        op("dve", lambda e: e.tensor_copy(ident_bf[:], ident[:]), reads=[Bc], writes=[Bc])
        op("pool", lambda e: e.dma_start(out=ind_bf[:], in_=c_ind[:, :]), writes=[Bc], dma=S.dma_sem("cind"))
        op("dve", lambda e: e.memset(ones_bf[:], 1.0), writes=[Bc])
        op("dve", lambda e: e.memset(ones_f[:], 1.0), writes=[Bc])

        def finish(src_fn=None):
            toks = []
            if src_fn is not None:
                toks = src_fn()
            S.ops["sp"].append((tuple(toks), None, None))
            S.emit()

        def dump(dst, src, bufs):
            return op("sp", lambda e: e.dma_start(out=dst, in_=src), reads=bufs, dma=dsem_out)

        with contextlib.ExitStack() as ph:
            k.pstack = ph
            adarow = k.sb("adarow", [P, 3 * D], F32)
            Bada = Buf("adarow")
            ccol_sb = k.sb("ccol_sb", [P, 8], F32)
            cbc = k.sb("cbc", [P, 8, P], F32)
            nwbc = k.sb("nwbc", [P, D], F32)
            Abc = k.sb("Abc", [P, D], F32)
            junk = k.sb("junk", [P, D], F32)
            ssq = k.sb("ssq", [P, 32], F32)
            rs = k.sb("rs", [P, 32], F32)
            Bcc = Buf("ccol")
            Bcbc = Buf("cbc")
            Bnw = Buf("nwbc")
            BA = Buf("Abc")
            Bjunk = Buf("junk")
            wst = Ring(k, "wada", 4, [P, 8, 256], F32, dma=True)
            pa = Ring(k, "pa", 2, [P, 512], F32, psum=True)
            ptr = Ring(k, "ptr", 6, [P, 512], F32, psum=True)
            xr = Ring(k, "xt", 4, [P, D], F32, dma=True)
            xnr = Ring(k, "xn", 4, [P, D], F32)
            op("sp", lambda e: e.dma_start(out=ccol_sb[:], in_=ccol[:, :]), writes=[Bcc], dma=S.dma_sem("ccol"))
            op("sp", lambda e: e.dma_start(out=adarow[:], in_=b_ada[0:1, :].partition_broadcast(P)), writes=[Bada], dma=S.dma_sem("bada"))
            op("sp", lambda e: e.dma_start(out=nwbc[:], in_=norm_w[0:1, :].partition_broadcast(P)), writes=[Bnw], dma=S.dma_sem("nwbc"))
            op("dve", lambda e: e.tensor_copy(cbc[:], ccol_sb[:].unsqueeze(2).to_broadcast([P, 8, P])), reads=[Bcc], writes=[Bcbc])
            w_ada_r = w_ada.rearrange("(kc p) n -> p kc n", p=P)
            xpre = []
            for tt in range(4):
                xt, xb, xs = xr.next()
                op("sp", lambda e, xt=xt, tt=tt: e.dma_start(out=xt[:], in_=x[tt * P:(tt + 1) * P, :]), writes=[xb], dma=xs)
                xpre.append((xt, xb, xs))
            for cg in range(12):
                wt, wb, ws = wst.next()
                op("sp" if cg % 2 == 0 else "act", lambda e, wt=wt, cg=cg: e.dma_start(out=wt[:], in_=w_ada_r[:, :, cg * 256:(cg + 1) * 256]), writes=[wb], dma=ws)
                pt_, pb, _ = pa.next()
                for kc in range(8):
                    op("pe", lambda e, pt_=pt_, wt=wt, kc=kc: e.matmul(pt_[:, 0:256], lhsT=cbc[:, kc, :], rhs=wt[:, kc, :], start=(kc == 0), stop=(kc == 7)),
                       reads=[Bcbc, wb], writes=[pb])
                op("dve", lambda e, pt_=pt_, cg=cg: e.tensor_tensor(adarow[:, cg * 256:(cg + 1) * 256], pt_[:, 0:256], adarow[:, cg * 256:(cg + 1) * 256], ALU.add),
                   reads=[pb, Bada], writes=[Bada])
            op("dve", lambda e: e.scalar_tensor_tensor(out=Abc[:], in0=adarow[:, D:2 * D], scalar=1.0, in1=nwbc[:], op0=ALU.add, op1=ALU.mult),
               reads=[Bada, Bnw], writes=[BA])
            op("dve", lambda e: e.tensor_scalar(gate_half[:], adarow[:, 2 * D:3 * D], 0.5, None, ALU.mult), reads=[Bada], writes=[Bgate])
            Bsst = [Buf(f"ssq{t}") for t in range(32)]
            lagq = []

            def stage2(tt, banks):
                g = tt // 4
                for half, (pt_, pb) in enumerate(banks):
                    op("act", lambda e, pt_=pt_, half=half, tt=tt: e.activation(hT[:, half * 4:half * 4 + 4, tt * P:(tt + 1) * P],
                                                                               pt_[:, 0:512].rearrange("p (a b) -> p a b", b=P), AF.Copy),
                       reads=[pb], writes=[hTb[g]])

            for tt in range(32):
                if tt < 4:
                    xt, xb, xs = xpre[tt]
                else:
                    xt, xb, xs = xr.next()
                    op("sp", lambda e, xt=xt, tt=tt: e.dma_start(out=xt[:], in_=x[tt * P:(tt + 1) * P, :]), writes=[xb], dma=xs)
                op("act", lambda e, xt=xt, tt=tt: e.activation(junk[:], xt[:], AF.Square, accum_out=ssq[:, tt:tt + 1]), reads=[xb], writes=[Bjunk, Bsst[tt]])
                op("act", lambda e, tt=tt: e.activation(rs[:, tt:tt + 1], ssq[:, tt:tt + 1], AF.Sqrt, bias=EPS, scale=1.0 / D), reads=[Bsst[tt]], writes=[Bsst[tt]])
                op("dve", lambda e, tt=tt: e.reciprocal(rs[:, tt:tt + 1], rs[:, tt:tt + 1]), reads=[Bsst[tt]], writes=[Bsst[tt]])
                xn, xnb, _ = xnr.next()
                op("dve", lambda e, xn=xn, xt=xt, tt=tt: e.scalar_tensor_tensor(out=xn[:], in0=xt[:], scalar=rs[:, tt:tt + 1], in1=Abc[:], op0=ALU.mult, op1=ALU.mult),
                   reads=[xb, Bsst[tt], BA], writes=[xnb])
                op("pool", lambda e, xn=xn: e.tensor_tensor(xn[:], xn[:], adarow[:, 0:D], ALU.add), reads=[xnb, Bada], writes=[xnb])
                banks = []
                for half in range(2):
                    pt_, pb, _ = ptr.next()
                    for j in range(4):
                        kc = half * 4 + j
                        op("pe", lambda e, pt_=pt_, xn=xn, kc=kc, j=j: e.transpose(pt_[:, j * P:(j + 1) * P], xn[:, kc * P:(kc + 1) * P], ident[:]),
                           reads=[xnb, Bc], writes=[pb])
                    banks.append((pt_, pb))
                lagq.append((tt, banks))
                if len(lagq) > 2:
                    stage2(*lagq.pop(0))
            while lagq:
                stage2(*lagq.pop(0))
            S.barrier()
            if stop_after == "ph1":
                tmpf = k.sb("tmpf", [P, T], F32)
                Bt = Buf("tmpf")
                toks = []
                for kc in range(8):
                    op("dve", lambda e, kc=kc: e.tensor_copy(tmpf[:], hT[:, kc, :]), reads=hTb, writes=[Bt])
                    toks.append(dump(dbg[kc * P:(kc + 1) * P, :], tmpf[:], [Bt]))
                toks.append(dump(dbg[8 * P:9 * P, 0:D], gate_half[:], [Bgate]))
                S.ops["sp"].append((tuple(toks), None, None))
                S.emit()
                return nc
            S.emit()
        with contextlib.ExitStack() as ph:
            k.pstack = ph
            Wg = k.sb("Wg", [P, 8, 8], BF16)
            BWg = Buf("Wg")
            wsem = S.dma_sem("wg")
            op("pool", lambda e: e.dma_start(out=Wg[:], in_=w_in_r[:, :, COL_IG:COL_IG + 8]), writes=[BWg], dma=wsem)
            Wga = k.sb("Wga", [P, 8, D], BF16)
            Wgm = k.sb("Wgm", [P, 8, D], BF16)
            BWga = Buf("Wga")
            BWgm = Buf("Wgm")
            wsem2 = S.dma_sem("wga")
            wsem3 = S.dma_sem("wgm")
            op("pool", lambda e: e.dma_start(out=Wga[:], in_=w_in_r[:, :, COL_GA:COL_GA + D]), writes=[BWga], dma=wsem2)
            op("pool", lambda e: e.dma_start(out=Wgm[:], in_=w_in_r[:, :, COL_GM:COL_GM + D]), writes=[BWgm], dma=wsem3)
            bi_sb = k.sb("bi_sb", [4, 1], F32)
            bf_sb = k.sb("bf_sb", [4, 1], F32)
            negbf = k.sb("negbf", [4, 1], F32)
            Bb = Buf("gbias")
            S.dma_batch("sp", S.dma_sem("gb"), [((lambda e: e.dma_start(out=bi_sb[:], in_=b_ig[:, :])), [], [Bb]),
                                                ((lambda e: e.dma_start(out=bf_sb[:], in_=b_fg[:, :])), [], [Bb])])
            op("dve", lambda e: e.tensor_scalar(negbf[:], bf_sb[:], -1.0, None, ALU.mult), reads=[Bb], writes=[Bb])
            bufA = k.sb("bufA", [4, T], F32)
            bufB = k.sb("bufB", [4, T], F32)
            bufC = k.sb("bufC", [4, T], F32)
            BA_, BB_, BC_ = Buf("bufA"), Buf("bufB"), Buf("bufC")
            cm = k.sb("cm", [4, 32], F32)
            Mx = k.sb("Mx", [4, 32], F32)
            Mp = k.sb("Mp", [4, 32], F32)
            dec = k.sb("dec", [4, 32], F32)
            X4 = k.sb("X4", [4, 32, 4], F32)
            Bsm = Buf("gsmall")
            pg = Ring(k, "pg", 2, [P, 512], F32, psum=True)
            ptok = k.psum("ptok", [P, 512], F32)
            Bptok = Buf("ptok", psum=True)
            pdec = k.psum("pdec", [P, 512], F32)
            Bpdec = Buf("pdec", psum=True)
            for g in range(NG):
                pf, pfb, _ = pg.next()
                for kc in range(8):
                    op("pe", lambda e, pf=pf, kc=kc, g=g: e.matmul(pf[0:4, :], lhsT=Wg[:, kc, 4:8], rhs=hT[:, kc, g * GS:(g + 1) * GS], start=(kc == 0), stop=(kc == 7)),
                       reads=[BWg, hTb[g]], writes=[pfb])
                op("act", lambda e, pf=pf, g=g: e.activation(bufA[:, g * GS:(g + 1) * GS], pf[0:4, :], AF.Exp, bias=negbf[:], scale=-1.0), reads=[pfb, Bb], writes=[BA_])
                pi, pib, _ = pg.next()
                for kc in range(8):
                    op("pe", lambda e, pi=pi, kc=kc, g=g: e.matmul(pi[0:4, :], lhsT=Wg[:, kc, 0:4], rhs=hT[:, kc, g * GS:(g + 1) * GS], start=(kc == 0), stop=(kc == 7)),
                       reads=[BWg, hTb[g]], writes=[pib])
                op("act", lambda e, pi=pi, g=g: e.activation(bufC[:, g * GS:(g + 1) * GS], pi[0:4, :], AF.Identity, bias=bi_sb[:]), reads=[pib, Bb], writes=[BC_])
            op("act", lambda e: e.activation(bufA[:], bufA[:], AF.Ln, bias=1.0, scale=1.0), reads=[BA_], writes=[BA_])
            op("dve", lambda e: e.tensor_tensor_scan(bufB[:], bufA[:], bufA[:], 0.0, ALU.add, ALU.max), reads=[BA_], writes=[BB_])
            op("dve", lambda e: e.tensor_tensor(bufC[:], bufC[:], bufB[:], ALU.add), reads=[BC_, BB_], writes=[BC_])
            op("dve", lambda e: e.tensor_reduce(out=cm[:], in_=bufC[:].rearrange("p (c l) -> p c l", l=P), axis=AX.X, op=ALU.max), reads=[BC_], writes=[Bsm])
            op("dve", lambda e: e.tensor_tensor_scan(Mx[:], cm[:], cm[:], 0.0, ALU.max, ALU.max), reads=[Bsm], writes=[Bsm])
            op("dve", lambda e: e.memset(Mp[:, 0:1], 0.0), reads=[Bsm], writes=[Bsm])
            op("dve", lambda e: e.tensor_copy(Mp[:, 1:32], Mx[:, 0:31]), reads=[Bsm], writes=[Bsm])
            op("dve", lambda e: e.tensor_tensor(dec[:], Mp[:], Mx[:], ALU.subtract), reads=[Bsm], writes=[Bsm])
            op("act", lambda e: e.activation(dec[:], dec[:], AF.Exp), reads=[Bsm], writes=[Bsm])
            v3 = lambda t_: t_[:].rearrange("p (c l) -> p c l", l=P)
            Mb = lambda: Mx[:].unsqueeze(2).to_broadcast([4, 32, P])
            op("dve", lambda e: e.tensor_tensor(v3(bufA), v3(bufC), Mb(), ALU.subtract), reads=[BC_, Bsm, BA_], writes=[BA_])
            op("act", lambda e: e.activation(bufA[:], bufA[:], AF.Exp), reads=[BA_], writes=[BA_])
            op("dve", lambda e: e.tensor_tensor(v3(bufB), v3(bufB), Mb(), ALU.subtract), reads=[BB_, Bsm], writes=[BB_])
            op("act", lambda e: e.activation(bufB[:], bufB[:], AF.Exp), reads=[BB_], writes=[BB_])
            for c in range(32):
                op("pe", lambda e, c=c: e.transpose(ptok[:, c * 4:(c + 1) * 4], bufA[0:4, c * P:(c + 1) * P], ident[0:4, 0:4]), reads=[BA_, Bc], writes=[Bptok])
                op("pe", lambda e, c=c: e.transpose(ptok[:, 128 + c * 4:128 + (c + 1) * 4], bufB[0:4, c * P:(c + 1) * P], ident[0:4, 0:4]), reads=[BB_, Bc], writes=[Bptok])
            op("dve", lambda e: e.tensor_copy(wtok[:], ptok[:, 0:128]), reads=[Bptok], writes=[Bml])
            op("dve", lambda e: e.tensor_copy(cltok[:], ptok[:, 128:256]), reads=[Bptok], writes=[Bml])
            op("dve", lambda e: e.tensor_tensor(X4[:], dec[:].unsqueeze(2).to_broadcast([4, 32, 4]), ident[0:4, 0:4].unsqueeze(1).to_broadcast([4, 32, 4]), ALU.mult),
               reads=[Bsm, Bc], writes=[Bsm])
            op("pe", lambda e: e.matmul(pdec[:, 0:128], lhsT=ones_f[0:4, :], rhs=X4[:].rearrange("p c h -> p (c h)"), start=True, stop=True), reads=[Bsm, Bc], writes=[Bpdec])
            op("dve", lambda e: e.tensor_copy(decbc[:], pdec[:, 0:128]), reads=[Bpdec], writes=[Bml])
            pgg = Ring(k, "pgg", 4, [P, 512], F32, psum=True)
            sgr = Ring(k, "sg", 4, [P, 512], F32, dma=True)
            Bsga = Buf("sga_d")
            for g in range(NG):
                for cc in range(8):
                    for W_, Wb_, dst in ((Wga, BWga, sga_d), (Wgm, BWgm, sgm_d)):
                        pb_, pbb, _ = pgg.next()
                        for kc in range(8):
                            op("pe", lambda e, pb_=pb_, W_=W_, kc=kc, cc=cc, g=g: e.matmul(pb_[:], lhsT=W_[:, kc, cc * P:(cc + 1) * P], rhs=hT[:, kc, g * GS:(g + 1) * GS],
                                                                                        start=(kc == 0), stop=(kc == 7)),
                               reads=[Wb_, hTb[g]], writes=[pbb])
                        sg, sgb, sgs = sgr.next()
                        op("act", lambda e, sg=sg, pb_=pb_: e.activation(sg[:], pb_[:], AF.Tanh, scale=0.5), reads=[pbb], writes=[sgb])
                        op("sp", lambda e, sg=sg, dst=dst, cc=cc, g=g: e.dma_start(out=dst[cc, :, g * GS:(g + 1) * GS], in_=sg[:]), reads=[sgb], dma=sgs)
            S.barrier()
            if stop_after == "ph2":
                toks = []
                toks.append(dump(dbg[0:P, 0:128], wtok[:], [Bml]))
                toks.append(dump(dbg[0:P, 128:256], cltok[:], [Bml]))
                toks.append(dump(dbg[0:P, 256:384], decbc[:], [Bml]))
                tmp = k.sb("tmpd", [P, 512], F32)
                Bt = Buf("tmpd")
                ds2 = S.dma_sem("dbg2")
                for i, (src, cc, g) in enumerate(((sga_d, 3, 5), (sgm_d, 6, 2))):
                    op("sp", lambda e, src=src, cc=cc, g=g: e.dma_start(out=tmp[:], in_=src[cc, :, g * GS:(g + 1) * GS]), writes=[Bt], dma=ds2)
                    toks.append(dump(dbg[P * (i + 1):P * (i + 2), 0:512], tmp[:], [Bt]))
                S.ops["sp"].append((tuple(toks), None, None))
                S.emit()
                return nc
            S.emit()
        with contextlib.ExitStack() as ph:
            k.pstack = ph
            QKT = k.sb("QKT", [P, 4, T], BF16)
            Vall = k.sb("Vall", [P, 32, 256], BF16)
            kmS = k.sb("kmS", [P, 64], F32)
            tmpk = k.sb("tmpk", [P, 2, 16], F32)
            kmT = k.sb("kmT", [P, 2, 16], BF16)
            kmL = k.sb("kmL", [P, 2, 16], BF16)
            W3 = k.sb("W3", [P, 8, 3, 256], BF16)
            BW3 = Buf("W3")
            w3sem = S.dma_sem("w3")
            BQK = [Buf(f"QK{g}") for g in range(NG)]
            BV = [Buf(f"V{g}") for g in range(NG)]
            Bkm = Buf("km")
            Wzr = Ring(k, "Wz", 2, [P, 8, P], BF16, dma=True)
            Utr = Ring(k, "Ut", 2, [P, 1024], F32, dma=True)
            Bya = Buf("yaT_d")
            for hp in range(4):
                if hp == 0:
                    S.dma_batch("pool", w3sem, [((lambda e, j=j, col=col: e.dma_start(out=W3[:, :, j, :], in_=w_in_r[:, :, col:col + 256])), [], [BW3])
                                                for j, col in enumerate((COL_QA, COL_KA, COL_VA))])
                with contextlib.ExitStack() as sa:
                    k.pstack = sa
                    bX = Ring(k, "bX", 2, [P, 512], F32, psum=True)
                    bY = Ring(k, "bY", 2, [P, 512], F32, psum=True)
                    bZ = Ring(k, "bZ", 2, [P, 1024], BF16, psum=True)
                    pkm = k.psum("pkm", [P, 512], F32)
                    Bpkm = Buf("pkm", psum=True)
                    sqr = Ring(k, "sq", 2, [P, 512], F32)
                    s4r = Ring(k, "ssq4", 2, [P, 4], F32)
                    qknr = Ring(k, "qkn", 3, [P, 512], BF16)
                    prevT = None

                    def emitT(tt, qkn, qknb):
                        g = tt // 4
                        bz, bzb, _ = bZ.next()
                        for j in range(4):
                            op("pe", lambda e, bz=bz, qkn=qkn, j=j: e.transpose(bz[:, j * P:(j + 1) * P], qkn[:, j * P:(j + 1) * P], ident_bf[:]), reads=[qknb, Bc], writes=[bzb])
                        op("act", lambda e, bz=bz, tt=tt: e.activation(QKT[:, 0:4, tt * P:(tt + 1) * P], bz[:, 0:512].rearrange("p (a b) -> p a b", b=P), AF.Copy),
                           reads=[bzb], writes=[BQK[g]])
                        for hh in range(2):
                            op("pe", lambda e, qkn=qkn, hh=hh, tt=tt: e.matmul(pkm[:, hh * 32 + tt:hh * 32 + tt + 1], lhsT=qkn[:, (2 + hh) * P:(3 + hh) * P], rhs=ones_bf[:, 0:1],
                                                                            start=True, stop=True),
                               reads=[qknb, Bc], writes=[Bpkm])

                    for tt in range(32):
                        g = tt // 4
                        bx, bxb, _ = bX.next()
                        for j in range(2):
                            for kc in range(8):
                                op("pe", lambda e, bx=bx, j=j, kc=kc, tt=tt: e.matmul(bx[:, j * 256:(j + 1) * 256], lhsT=hT[:, kc, tt * P:(tt + 1) * P], rhs=W3[:, kc, j, :],
                                                                                    start=(kc == 0), stop=(kc == 7)),
                                   reads=[hTb[g], BW3], writes=[bxb])
                        by, byb, _ = bY.next()
                        for kc in range(8):
                            op("pe", lambda e, by=by, kc=kc, tt=tt: e.matmul(by[:, 0:256], lhsT=hT[:, kc, tt * P:(tt + 1) * P], rhs=W3[:, kc, 2, :], start=(kc == 0), stop=(kc == 7)),
                               reads=[hTb[g], BW3], writes=[byb])
                        sq, sqb, _ = sqr.next()
                        s4, s4b, _ = s4r.next()
                        op("act", lambda e, sq=sq, bx=bx: e.activation(sq[:], bx[:], AF.Square), reads=[bxb], writes=[sqb])
                        op("dve", lambda e, sq=sq, s4=s4: e.tensor_reduce(out=s4[:], in_=sq[:].rearrange("p (a b) -> p a b", b=P), axis=AX.X, op=ALU.add), reads=[sqb], writes=[s4b])
                        op("act", lambda e, s4=s4: e.activation(s4[:], s4[:], AF.Sqrt, bias=EPS, scale=1.0 / P), reads=[s4b], writes=[s4b])
                        op("dve", lambda e, s4=s4: e.reciprocal(s4[:], s4[:]), reads=[s4b], writes=[s4b])
                        qkn, qknb, _ = qknr.next()
                        for j in range(4):
                            wbc = qw_bc if j < 2 else kw_bc
                            op("dve", lambda e, qkn=qkn, bx=bx, s4=s4, j=j, wbc=wbc: e.scalar_tensor_tensor(out=qkn[:, j * P:(j + 1) * P], in0=bx[:, j * P:(j + 1) * P], scalar=s4[:, j:j + 1],
                                                                                                          in1=wbc[:], op0=ALU.mult, op1=ALU.mult),
                               reads=[bxb, s4b, Bc], writes=[qknb])
                        op("act", lambda e, by=by, tt=tt: e.activation(Vall[:, tt, :], by[:, 0:256], AF.Copy), reads=[byb], writes=[BV[g]])
                        if prevT is not None:
                            emitT(*prevT)
                        prevT = (tt, qkn, qknb)
                    emitT(*prevT)
                    op("dve", lambda e: e.tensor_copy(kmS[:], pkm[:, 0:64]), reads=[Bpkm], writes=[Bkm])
                    kv = lambda i: kmS[:].rearrange("p (h b two) -> p h b two", h=2, two=2)[:, :, :, i]
                    op("dve", lambda e: e.tensor_tensor(tmpk[:], kv(0), kv(1), ALU.add), reads=[Bkm], writes=[Bkm])
                    op("dve", lambda e: e.tensor_scalar(tmpk[:], tmpk[:], 1.0 / 256.0, None, ALU.mult), reads=[Bkm], writes=[Bkm])
                    op("dve", lambda e: e.tensor_copy(kmT[:], tmpk[:]), reads=[Bkm], writes=[Bkm])
                    op("dve", lambda e: e.tensor_tensor(kmL[:], tmpk[:], kmT[:], ALU.subtract), reads=[Bkm], writes=[Bkm])
                    S.barrier()
                    S.emit()
                with contextlib.ExitStack() as sbk:
                    k.pstack = sbk
                    bS = Ring(k, "bS", 3, [P, 512], F32, psum=True)
                    bO = Ring(k, "bO", 2, [P, 512], F32, psum=True)
                    bSumr = Ring(k, "bSum", 2, [P, 512], F32, psum=True)
                    pnm = k.psum("pnm", [P, 1024], BF16)
                    Bpnm = Buf("pnm", psum=True)
                    tzr = Ring(k, "tz", 2, [P, 512], F32)
                    zsr = Ring(k, "zs", 3, [P, 512], F32)
                    gsbr = Ring(k, "gsb", 2, [P, 16], F32)
                    gallr = Ring(k, "gall", 2, [P, 64], F32)
                    t8r = Ring(k, "top8", 2, [P, 8], F32)
                    nmr = Ring(k, "nm", 8, [P, 16], BF16)
                    nmTr = Ring(k, "nmT", 3, [16, 512], BF16)
                    PTr = Ring(k, "PT", 5, [P, 512], BF16)
                    tmpdr = Ring(k, "tmpd", 2, [P, 512], F32)
                    recr = Ring(k, "rec", 2, [P, 512], F32)
                    osbr = Ring(k, "osb", 2, [P, 512], F32)
                    yagr = Ring(k, "yag", 2, [P, 512], BF16, dma=True)
                    heads = []
                    for hh in range(2):
                        hd = 2 * hp + hh
                        Wz, Wzb, Wzs = Wzr.next()
                        op("pool", lambda e, Wz=Wz, hd=hd: e.dma_start(out=Wz[:], in_=w_in_r[:, :, COL_ZA + hd * P:COL_ZA + (hd + 1) * P]), writes=[Wzb], dma=Wzs)
                        Ut, Utb, Uts = Utr.next()
                        op("sp", lambda e, Ut=Ut, hd=hd: e.dma_start(out=Ut[:], in_=utab[hd, :, :]), writes=[Utb], dma=Uts)
                        heads.append((hd, Wz, Wzb, Ut, Utb))
                    if hp + 1 < 4:
                        S.dma_batch("pool", w3sem, [((lambda e, j=j, col=col, hp=hp: e.dma_start(out=W3[:, :, j, :], in_=w_in_r[:, :, col + (hp + 1) * 256:col + (hp + 2) * 256])), [], [BW3])
                                                    for j, col in enumerate((COL_QA, COL_KA, COL_VA))])
                    LOOK = 3
                    for hh in range(2):
                        hd, Wz, Wzb, Ut, Utb = heads[hh]
                        gst = {}

                        def proA(g, hh=hh, Wz=Wz, Wzb=Wzb, gst=gst):
                            st_ = {}
                            bz_, bzb_, _ = bS.next()
                            for kc in range(8):
                                op("pe", lambda e, bz_=bz_, kc=kc, Wz=Wz, g=g: e.matmul(bz_[:], lhsT=Wz[:, kc, :], rhs=hT[:, kc, g * GS:(g + 1) * GS], start=(kc == 0), stop=(kc == 7)),
                                   reads=[Wzb, hTb[g]], writes=[bzb_])
                            tz, tzb, _ = tzr.next()
                            zs, zsb, _ = zsr.next()
                            op("act", lambda e, tz=tz, bz_=bz_: e.activation(tz[:], bz_[:], AF.Tanh, scale=0.5), reads=[bzb_], writes=[tzb])
                            op("dve", lambda e, zs=zs, tz=tz, bz_=bz_: e.scalar_tensor_tensor(out=zs[:], in0=tz[:], scalar=1.0, in1=bz_[:], op0=ALU.add, op1=ALU.mult),
                               reads=[tzb, bzb_], writes=[zsb])
                            st_["zs"] = (zs, zsb)
                            st_["nms"] = []
                            if g >= 2:
                                pg_, pgb_, _ = bS.next()
                                for qi in range(4):
                                    tq = 4 * g + qi
                                    op("pe", lambda e, pg_=pg_, qi=qi, tq=tq, hh=hh: e.matmul(pg_[:, qi * 16:(qi + 1) * 16], lhsT=QKT[:, hh, tq * P:(tq + 1) * P], rhs=kmT[:, hh, :], start=True, stop=False),
                                       reads=[BQK[g], Bkm], writes=[pgb_])
                                    op("pe", lambda e, pg_=pg_, qi=qi, tq=tq, hh=hh: e.matmul(pg_[:, qi * 16:(qi + 1) * 16], lhsT=QKT[:, hh, tq * P:(tq + 1) * P], rhs=kmL[:, hh, :], start=False, stop=True),
                                       reads=[BQK[g], Bkm], writes=[pgb_])
                                gall, gallb, _ = gallr.next()
                                op("dve", lambda e, gall=gall, pg_=pg_: e.tensor_copy(gall[:], pg_[:, 0:64]), reads=[pgb_], writes=[gallb])
                                for qi in range(4):
                                    qblk = 2 * g + qi // 2
                                    gsb, gsbb, _ = gsbr.next()
                                    t8, t8b, _ = t8r.next()
                                    nm, nmb, _ = nmr.next()
                                    op("dve", lambda e, gsb=gsb: e.memset(gsb[:], -1.0e30), writes=[gsbb])
                                    op("dve", lambda e, gsb=gsb, qi=qi, qblk=qblk, gall=gall: e.tensor_copy(gsb[:, 0:qblk], gall[:, qi * 16:qi * 16 + qblk]), reads=[gallb], writes=[gsbb])
                                    op("dve", lambda e, gsb=gsb, t8=t8: e.max(t8[:], gsb[:]), reads=[gsbb], writes=[t8b])
                                    op("dve", lambda e, nm=nm: e.memset(nm[:], 0.0), writes=[nmb])
                                    op("dve", lambda e, nm=nm, gsb=gsb, t8=t8, qblk=qblk: e.tensor_scalar(nm[:, 0:qblk], gsb[:, 0:qblk], t8[:, 2:3], NEGV, ALU.is_lt, ALU.mult),
                                       reads=[gsbb, t8b], writes=[nmb])
                                    st_["nms"].append((nm, nmb))
                            gst[g] = st_

                        def proB(g, gst=gst):
                            st_ = gst[g]
                            st_["nmT"] = (None, None)
                            if g >= 2:
                                for qi, (nm, nmb) in enumerate(st_["nms"]):
                                    op("pe", lambda e, nm=nm, qi=qi: e.transpose(pnm[0:16, qi * P:(qi + 1) * P], nm[:], ident_bf[:]), reads=[nmb, Bc], writes=[Bpnm])
                                nmT, nmTb, _ = nmTr.next()
                                op("act", lambda e, nmT=nmT: e.activation(nmT[:], pnm[0:16, 0:512], AF.Copy), reads=[Bpnm], writes=[nmTb])
                                st_["nmT"] = (nmT, nmTb)

                        def epilogue(g, bo, bob, bsum, bsumb, hd=hd, gst=gst):
                            zs, zsb = gst[g]["zs"]
                            rec, recb, _ = recr.next()
                            osb, osbb, _ = osbr.next()
                            yag, yagb, yags = yagr.next()
                            op("act", lambda e, osb=osb, bo=bo: e.activation(osb[:], bo[:], AF.Copy), reads=[bob], writes=[osbb])
                            op("dve", lambda e, rec=rec, bsum=bsum: e.reciprocal(rec[:], bsum[:]), reads=[bsumb], writes=[recb])
                            op("pool", lambda e, osb=osb, rec=rec: e.tensor_tensor(osb[:], osb[:], rec[:], ALU.mult), reads=[osbb, recb], writes=[osbb])
                            op("pool", lambda e, yag=yag, osb=osb, zs=zs: e.tensor_tensor(yag[:], osb[:], zs[:], ALU.mult), reads=[osbb, zsb], writes=[yagb])
                            op("sp", lambda e, yag=yag, g=g, hd=hd: e.dma_start(out=yaT_d[hd, :, g * GS:(g + 1) * GS], in_=yag[:]), reads=[yagb], writes=[Bya], dma=yags)

                        pending = []

                        def pop_pv(hh=hh):
                            g, kt, NKT, c0, Nq, pt, ptb, bo, bob, bsum, bsumb = pending.pop(0)
                            op("pe", lambda e, bo=bo, c0=c0, kt=kt, hh=hh, pt=pt, Nq=Nq, NKT=NKT: e.matmul(bo[:, c0:GS], lhsT=Vall[:, kt, hh * P:(hh + 1) * P], rhs=pt[:, 0:Nq], start=(kt == 0), stop=(kt == NKT - 1)),
                               reads=[BV[kt // 4], ptb], writes=[bob])
                            op("pe", lambda e, bsum=bsum, c0=c0, kt=kt, pt=pt, Nq=Nq, NKT=NKT: e.matmul(bsum[:, c0:GS], lhsT=ones_bf[:], rhs=pt[:, 0:Nq], start=(kt == 0), stop=(kt == NKT - 1)),
                               reads=[Bc, ptb], writes=[bsumb])
                            if kt == NKT - 1:
                                epilogue(g, bo, bob, bsum, bsumb)

                        proA(0)
                        proB(0)
                        proA(1)
                        for g in range(NG):
                            NKT = 4 * (g + 1)
                            bo, bob, _ = bO.next()
                            bsum, bsumb, _ = bSumr.next()
                            for kt in range(NKT):
                                if kt == max(2, NKT - 2) and g + 1 < NG:
                                    proB(g + 1)
                                nmT, nmTb = gst[g]["nmT"]
                                j = kt - 4 * g
                                c0 = P * j if j >= 1 else 0
                                Nq = GS - c0
                                masked = (g >= 2) and (kt // 2 <= 2 * g)
                                bs, bsb, _ = bS.next()
                                op("pe", lambda e, bs=bs, kt=kt, g=g, c0=c0, Nq=Nq, masked=masked, hh=hh: e.matmul(bs[:, 0:Nq], lhsT=QKT[:, 2 + hh, kt * P:(kt + 1) * P],
                                                                                                         rhs=QKT[:, hh, g * GS + c0:(g + 1) * GS], start=True, stop=(not masked)),
                                   reads=[BQK[kt // 4], BQK[g]], writes=[bsb])
                                if masked:
                                    bk = kt // 2
                                    op("pe", lambda e, bs=bs, bk=bk, nmT=nmT, c0=c0, Nq=Nq: e.matmul(bs[:, 0:Nq], lhsT=ind_bf[0:16, bk * P:(bk + 1) * P], rhs=nmT[0:16, c0:GS],
                                                                                                  start=False, stop=True),
                                       reads=[Bc, nmTb], writes=[bsb])
                                pt, ptb, _ = PTr.next()
                                if j >= -1:
                                    td, tdb, _ = tmpdr.next()
                                    u0 = 384 - P * j + c0
                                    op("dve", lambda e, td=td, bs=bs, u0=u0, Nq=Nq, Ut=Ut: e.scalar_tensor_tensor(out=td[:, 0:Nq], in0=bs[:, 0:Nq], scalar=ATT_SCALE, in1=Ut[:, u0:u0 + Nq],
                                                                                                        op0=ALU.mult, op1=ALU.add),
                                       reads=[bsb, Utb], writes=[tdb])
                                    op("act", lambda e, pt=pt, td=td, Nq=Nq: e.activation(pt[:, 0:Nq], td[:, 0:Nq], AF.Exp), reads=[tdb], writes=[ptb])
                                else:
                                    op("act", lambda e, pt=pt, bs=bs, hd=hd: e.activation(pt[:], bs[:], AF.Exp, bias=b31_bc[:, hd:hd + 1], scale=ATT_SCALE), reads=[bsb, Bc], writes=[ptb])
                                pending.append((g, kt, NKT, c0, Nq, pt, ptb, bo, bob, bsum, bsumb))
                                if len(pending) > LOOK:
                                    pop_pv()
                            if g + 2 < NG:
                                proA(g + 2)
                        while pending:
                            pop_pv()
                    S.barrier()
                    if stop_after == "ph3" and hp == 0:
                        toks = []
                        tb = k.sb("tmpd", [P, 1024], BF16)
                        tf = k.sb("tmpf", [P, 1024], F32)
                        Bt = Buf("tmpd")
                        Bt2 = Buf("tmpf")
                        ds2 = S.dma_sem("dbg2")
                        for hd in range(2):
                            for q4 in range(4):
                                op("sp", lambda e, hd=hd, q4=q4: e.dma_start(out=tb[:], in_=yaT_d[hd, :, q4 * 1024:(q4 + 1) * 1024]), reads=[Bya], writes=[Bt], dma=ds2)
                                op("dve", lambda e: e.tensor_copy(tf[:], tb[:]), reads=[Bt], writes=[Bt2])
                                toks.append(dump(dbg[hd * P:(hd + 1) * P, q4 * 1024:(q4 + 1) * 1024], tf[:], [Bt2]))
                        for i in range(4):
                            for q4 in range(4):
                                op("dve", lambda e, i=i, q4=q4: e.tensor_copy(tf[:], QKT[:, i, q4 * 1024:(q4 + 1) * 1024]), reads=BQK, writes=[Bt2])
                                toks.append(dump(dbg[(2 + i) * P:(3 + i) * P, q4 * 1024:(q4 + 1) * 1024], tf[:], [Bt2]))
                        S.ops["sp"].append((tuple(toks), None, None))
                        S.emit()
                        return nc
                    S.emit()
                k.pstack = ph
        k.pstack = hst
        Wa = k.sb("Wa", [P, 8, D], BF16)
        Wm = k.sb("Wm", [P, 8, D], BF16)
        Wo = k.sb("Wo", [P, 8, D], BF16)
        BWa, BWm, BWo = Buf("Wa"), Buf("Wm"), Buf("Wo")
        with contextlib.ExitStack() as ph:
            k.pstack = ph
            W5 = k.sb("W5", [P, 8, 5, 256], BF16)
            BW5s = [Buf(f"W5_{j}") for j in range(5)]
            w5sems = [S.dma_sem(f"w5_{j}") for j in range(5)]
            Cst = k.sb("Cst", [P, 2, 258], F32)
            Cbf = k.sb("Cbf", [P, 2, 258], BF16)
            BC = Buf("Cst")
            BCbf = Buf("Cbf")
            pre = [[k.sb(f"pre{w}{dc}", [P, 516], BF16) for dc in range(2)] for w in range(2)]
            Bpre = [[Buf(f"pre{w}{dc}") for dc in range(2)] for w in range(2)]
            thr_ = Ring(k, "th", 2, [P, 512], F32)
            QmTr = Ring(k, "QmT", 2, [P, 2, 512], BF16)
            KmTr = Ring(k, "KmT", 2, [P, 2, 512], BF16)
            Ktokr = Ring(k, "Ktok", 2, [P, 4, 256], BF16)
            Vpr = Ring(k, "Vp", 2, [P, 258], BF16)
            Smr = Ring(k, "Sm", 2, [P, P], BF16)
            thozr = Ring(k, "thoz", 2, [P, 512], F32)
            dnr = Ring(k, "dn", 2, [P, 1], F32)
            hm2r = Ring(k, "hm2", 2, [P, 4, 256], F32)
            szr = Ring(k, "sz", 2, [P, 4, 256], F32)
            ssqmr = Ring(k, "ssqm", 2, [P, 4], F32)
            junk2 = k.sb("junk2", [P, 256], F32)
            Bjunk2 = Buf("junk2")
            t1r = Ring(k, "t1", 2, [P, 256], F32)
            ymgr = Ring(k, "ymg", 2, [P, 256], BF16)
            ymTsr = Ring(k, "ymTs", 2, [P, 2, 512], BF16, dma=True)
            bQK = Ring(k, "bQK", 2, [P, 512], F32, psum=True)
            bT = k.psum("bT", [P, 1024], BF16)
            BbT = Buf("bT", psum=True)
            bT2 = bT
            BbT2 = BbT
            diag = k.sb("diag", [P, 16, P], BF16)
            Bdiag = Buf("diag")
            caus_s = k.sb("caus_s", [P, P], F32)
            op("dve", lambda e: e.tensor_scalar(caus_s[:], caus[:], ML_KSCALE, None, ALU.mult), reads=[Bc], writes=[Bdiag])
            bV = k.psum("bV", [P, 512], F32)
            BbV = Buf("bV", psum=True)
            bOZ = k.psum("bOZ", [P, 512], F32)
            BbOZ = Buf("bOZ", psum=True)
            bOut = k.psum("bOut", [P, 512], F32)
            BbOut = Buf("bOut", psum=True)
            bCp = [k.psum(f"bC{i}", [P, 512], F32) for i in range(2)]
            BbCp = [Buf(f"bC{i}", psum=True) for i in range(2)]
            Bym = Buf("ymT_d")
            for mh in range(4):
                for j, col in enumerate((COL_QM, COL_KM, COL_VM, COL_ZM, COL_OM)):
                    op("pool", lambda e, j=j, col=col, mh=mh: e.dma_start(out=W5[:, :, j, :], in_=w_in_r[:, :, col + mh * 256:col + (mh + 1) * 256]), writes=[BW5s[j]], dma=w5sems[j])
                if mh == 0:
                    for W_, Wb_, src in ((Wa, BWa, w_att), (Wm, BWm, w_ml), (Wo, BWo, w_out)):
                        sm_ = S.dma_sem(f"w5_{k.uid()}")
                        op("pool", lambda e, W_=W_, src=src: e.dma_start(out=W_[:], in_=src.rearrange("(kc p) n -> p kc n", p=P)), writes=[Wb_], dma=sm_)
                if mh == 2:
                    op("dve", lambda e: e.tensor_scalar(Wa[:], Wa[:], 0.5, None, ALU.mult), reads=[BWa], writes=[BWa])
                op("dve", lambda e: e.memset(Cst[:], 0.0), writes=[BC])
                for w in range(2):
                    for dc in range(2):
                        op("dve", lambda e, w=w, dc=dc: e.memset(pre[w][dc][:, 0:3], 0.0), writes=[Bpre[w][dc]])
                for w in range(2):
                    for dc in range(2):
                        for j in range(4):
                            op("dve", lambda e, w=w, dc=dc, j=j, mh=mh: e.tensor_scalar(diag[:, (w * 2 + dc) * 4 + j, :], ident[:], convw_sb[:, w * 8 + mh * 2 + dc, j:j + 1], None, ALU.mult),
                               reads=[Bc], writes=[Bdiag])

                def qk_piece(g, w, dc, dstT, dstb, mh=mh):
                    bq, bqb, _ = bQK.next()
                    for kc in range(8):
                        op("pe", lambda e, bq=bq, kc=kc: e.matmul(bq[:], lhsT=W5[:, kc, w, dc * P:(dc + 1) * P], rhs=hT[:, kc, g * GS:(g + 1) * GS], start=(kc == 0), stop=(kc == 7)),
                           reads=[BW5s[w], hTb[g]], writes=[bqb])
                    pr = pre[w][dc]
                    prb = Bpre[w][dc]
                    cidx = w * 8 + mh * 2 + dc
                    didx = (w * 2 + dc) * 4
                    op("act", lambda e: e.activation(pr[:, 3:515], bq[:], AF.Copy), reads=[bqb], writes=[prb])

                    def part2():
                        bcv, bcvb, _ = bQK.next()
                        for j in range(4):
                            op("pe", lambda e, j=j: e.matmul(bcv[:], lhsT=diag[:, didx + j, :], rhs=pr[:, j:j + 512], start=(j == 0), stop=(j == 3)), reads=[Bdiag, prb], writes=[bcvb])
                        op("pool", lambda e: e.tensor_copy(pr[:, 0:3], pr[:, 512:515]), reads=[prb], writes=[prb])
                        op("act", lambda e: e.activation(dstT[:, dc, :], bcv[:], AF.Silu, bias=convb_sb[:, cidx:cidx + 1], scale=1.0), reads=[bcvb, Bc], writes=[dstb])
                    return part2

                pieces = [(w, dc) for w in range(2) for dc in range(2)]
                nxt = (QmTr.next(), KmTr.next())
                for (w, dc) in pieces:
                    dd = nxt[w]
                    qk_piece(0, w, dc, dd[0], dd[1])()

                def gend_dve(gs, c4, mh=mh):
                    hm2, hm2b, sz, szb, ssqm, ssqmb = gs["bufs"]
                    t1, t1b, _ = t1r.next()
                    ymg, ymgb, _ = ymgr.next()
                    op("dve", lambda e: e.scalar_tensor_tensor(out=t1[:], in0=hm2[:, c4, :], scalar=ssqm[:, c4:c4 + 1], in1=mlw_bc[:, mh * 256:(mh + 1) * 256], op0=ALU.mult, op1=ALU.mult),
                       reads=[hm2b, ssqmb, Bc], writes=[t1b])
                    op("pool", lambda e: e.tensor_tensor(ymg[:], t1[:], sz[:, c4, :], ALU.mult), reads=[t1b, szb], writes=[ymgb])
                    gs.setdefault("ymgs", {})[c4] = (ymg, ymgb)

                def gend_sqrt(gs):
                    hm2, hm2b, sz, szb, ssqm, ssqmb = gs["bufs"]
                    op("act", lambda e: e.activation(ssqm[:], ssqm[:], AF.Sqrt, bias=EPS, scale=1.0 / 256.0), reads=[ssqmb], writes=[ssqmb])
                    op("dve", lambda e: e.reciprocal(ssqm[:], ssqm[:]), reads=[ssqmb], writes=[ssqmb])

                def gend_pe(gs, c4, mh=mh):
                    ymg, ymgb = gs["ymgs"][c4]
                    g_ = gs["g"]
                    for dc in range(2):
                        op("pe", lambda e, dc=dc: e.transpose(bT2[:, (dc * 4 + c4) * P:(dc * 4 + c4 + 1) * P], ymg[:, dc * P:(dc + 1) * P], ident_bf[:]), reads=[ymgb, Bc], writes=[BbT2])
                    if c4 == 3:
                        ymTs, ymTsb, ymTss = ymTsr.next()
                        op("act", lambda e: e.activation(ymTs[:], bT2[:, 0:1024].rearrange("p (a b) -> p a b", b=512), AF.Copy), reads=[BbT2], writes=[ymTsb])
                        S.dma_batch("sp", ymTss, [((lambda e, dc=dc: e.dma_start(out=ymT_d[mh * 2 + dc, :, g_ * GS:(g_ + 1) * GS], in_=ymTs[:, dc, :])), [ymTsb], [Bym])
                                                  for dc in range(2)])

                prevg = None
                pend_tail = []
                for g in range(NG):
                    (QmT, QmTb, _), (KmT, KmTb, _) = nxt
                    if g + 1 < NG:
                        nxt = (QmTr.next(), KmTr.next())
                    Ktok, Ktokb, _ = Ktokr.next()
                    for c4 in range(4):
                        for dc in range(2):
                            op("pe", lambda e, KmT=KmT, c4=c4, dc=dc: e.transpose(bT[:, c4 * 256 + dc * P:c4 * 256 + (dc + 1) * P], KmT[:, dc, c4 * P:(c4 + 1) * P], ident_bf[:]),
                               reads=[KmTb, Bc], writes=[BbT])
                    op("act", lambda e, Ktok=Ktok: e.activation(Ktok[:], bT[:, 0:1024].rearrange("p (a b) -> p a b", b=256), AF.Copy), reads=[BbT], writes=[Ktokb])
                    hm2, hm2b, _ = hm2r.next()
                    sz, szb, _ = szr.next()
                    ssqm, ssqmb, _ = ssqmr.next()
                    curg = {"g": g, "bufs": (hm2, hm2b, sz, szb, ssqm, ssqmb)}
                    for c4 in range(4):
                        ci = g * 4 + c4
                        t0 = ci * P
                        col = ci * 4 + mh
                        part2 = None
                        if g + 1 < NG:
                            w, dc = pieces[c4]
                            dd = nxt[w]
                            part2 = qk_piece(g + 1, w, dc, dd[0], dd[1])
                        while pend_tail:
                            pend_tail.pop(0)()
                        if prevg is not None:
                            if c4 == 1:
                                gend_dve(prevg, 0)
                            if c4 >= 1:
                                gend_dve(prevg, c4)
                        for dc in range(2):
                            op("pe", lambda e, KmT=KmT, QmT=QmT, dc=dc, c4=c4: e.matmul(bV[:, 256:384], lhsT=KmT[:, dc, c4 * P:(c4 + 1) * P], rhs=QmT[:, dc, c4 * P:(c4 + 1) * P],
                                                                                     start=(dc == 0), stop=(dc == 1)),
                               reads=[KmTb, QmTb], writes=[BbV])
                        for kc in range(8):
                            op("pe", lambda e, kc=kc, t0=t0: e.matmul(bV[:, 0:256], lhsT=hT[:, kc, t0:t0 + P], rhs=W5[:, kc, 2, :], start=(kc == 0), stop=(kc == 7)),
                               reads=[hTb[g], BW5s[2]], writes=[BbV])
                        Sm, Smb, _ = Smr.next()
                        op("dve", lambda e, Sm=Sm: e.tensor_tensor(Sm[:], bV[:, 256:384], caus_s[:], ALU.mult), reads=[BbV, Bdiag], writes=[Smb])
                        Vp, Vpb, _ = Vpr.next()
                        op("act", lambda e, Vp=Vp, col=col: e.activation(Vp[:, 0:256], bV[:, 0:256], AF.Copy, scale=wtok[:, col:col + 1]), reads=[BbV, Bml], writes=[Vpb])
                        op("pool", lambda e, Vp=Vp, col=col: e.tensor_copy(Vp[:, 256:258], wtok[:, col:col + 1].to_broadcast([P, 2])), reads=[Bml], writes=[Vpb])
                        if ci > 0:
                            op("pool", lambda e, col=col: e.tensor_scalar(Cbf[:], Cst[:], decbc[:, col:col + 1], ML_KSCALE, ALU.mult, ALU.mult), reads=[BC, Bml], writes=[BCbf])
                        if part2 is not None:
                            part2()
                        for jj, wi in enumerate((4, 3)):
                            for kc in range(8):
                                op("pe", lambda e, kc=kc, t0=t0, jj=jj, wi=wi: e.matmul(bOZ[:, jj * 256:(jj + 1) * 256], lhsT=hT[:, kc, t0:t0 + P], rhs=W5[:, kc, wi, :], start=(kc == 0), stop=(kc == 7)),
                                   reads=[hTb[g], BW5s[wi]], writes=[BbOZ])
                        thoz, thozb, _ = thozr.next()
                        op("act", lambda e, thoz=thoz: e.activation(thoz[:, 0:256], bOZ[:, 0:256], AF.Tanh, scale=0.5), reads=[BbOZ], writes=[thozb])
                        op("act", lambda e, sz=sz, c4=c4: e.activation(sz[:, c4, :], bOZ[:, 256:512], AF.Silu), reads=[BbOZ], writes=[szb])
                        op("pool", lambda e, thoz=thoz: e.tensor_scalar(thoz[:, 0:256], thoz[:, 0:256], 0.5, 0.5, ALU.mult, ALU.add), reads=[thozb], writes=[thozb])
                        if ci > 0:
                            for dc in range(2):
                                op("pe", lambda e, QmT=QmT, dc=dc, c4=c4: e.matmul(bOut[:, 0:257], lhsT=QmT[:, dc, c4 * P:(c4 + 1) * P], rhs=Cbf[:, dc, 0:257], start=(dc == 0), stop=False),
                                   reads=[QmTb, BCbf], writes=[BbOut])
                        op("pe", lambda e, Sm=Sm, Vp=Vp, ci=ci: e.matmul(bOut[:, 0:257], lhsT=Sm[:], rhs=Vp[:, 0:257], start=(ci == 0), stop=True), reads=[Smb, Vpb], writes=[BbOut])
                        for dkc in range(2):
                            op("pe", lambda e, Ktok=Ktok, c4=c4, dkc=dkc, Vp=Vp: e.matmul(bCp[dkc][:, 0:257], lhsT=Ktok[:, c4, dkc * P:(dkc + 1) * P], rhs=Vp[:, 0:257], start=True, stop=True),
                               reads=[Ktokb, Vpb], writes=[BbCp[dkc]])
                            op("dve", lambda e, dkc=dkc, col=col: e.scalar_tensor_tensor(out=Cst[:, dkc, 0:257], in0=Cst[:, dkc, 0:257], scalar=decbc[:, col:col + 1], in1=bCp[dkc][:, 0:257],
                                                                                      op0=ALU.mult, op1=ALU.add),
                               reads=[BC, Bml, BbCp[dkc]], writes=[BC])
                        def tail(hm2=hm2, hm2b=hm2b, ssqm=ssqm, ssqmb=ssqmb, c4=c4, col=col, thoz=thoz, thozb=thozb):
                            dn, dnb, _ = dnr.next()
                            op("dve", lambda e: e.tensor_scalar(dn[:], bOut[:, 256:257], -1.0, cltok[:, col:col + 1], ALU.mult, ALU.max), reads=[BbOut, Bml], writes=[dnb])
                            op("dve", lambda e: e.tensor_tensor(dn[:], dn[:], bOut[:, 256:257], ALU.max), reads=[BbOut, dnb], writes=[dnb])
                            op("dve", lambda e: e.reciprocal(dn[:], dn[:]), reads=[dnb], writes=[dnb])
                            op("dve", lambda e: e.scalar_tensor_tensor(out=hm2[:, c4, :], in0=bOut[:, 0:256], scalar=dn[:, 0:1], in1=thoz[:, 0:256], op0=ALU.mult, op1=ALU.mult),
                               reads=[BbOut, dnb, thozb], writes=[hm2b])
                            op("act", lambda e: e.activation(junk2[:], hm2[:, c4, :], AF.Square, accum_out=ssqm[:, c4:c4 + 1]), reads=[hm2b], writes=[Bjunk2, ssqmb])
                        pend_tail.append(tail)
                        if prevg is not None:
                            if c4 == 0:
                                gend_sqrt(prevg)
                            else:
                                gend_pe(prevg, c4 - 1)
                                if c4 == 3:
                                    gend_pe(prevg, 3)
                    prevg = curg
                while pend_tail:
                    pend_tail.pop(0)()
                gend_sqrt(prevg)
                for c4 in range(4):
                    gend_dve(prevg, c4)
                    gend_pe(prevg, c4)
                if stop_after == "ph4" and mh == 0:
                    S.barrier()
                    toks = []
                    tb = k.sb("tmpd", [P, 1024], BF16)
                    tf = k.sb("tmpf", [P, 1024], F32)
                    Bt = Buf("tmpd")
                    Bt2 = Buf("tmpf")
                    ds2 = S.dma_sem("dbg2")
                    for kc in range(2):
                        for q4 in range(4):
                            op("sp", lambda e, kc=kc, q4=q4: e.dma_start(out=tb[:], in_=ymT_d[kc, :, q4 * 1024:(q4 + 1) * 1024]), reads=[Bym], writes=[Bt], dma=ds2)
                            op("dve", lambda e: e.tensor_copy(tf[:], tb[:]), reads=[Bt], writes=[Bt2])
                            toks.append(dump(dbg[kc * P:(kc + 1) * P, q4 * 1024:(q4 + 1) * 1024], tf[:], [Bt2]))
                    S.ops["sp"].append((tuple(toks), None, None))
                    S.emit()
                    return nc
            S.barrier()
            S.emit()
        with contextlib.ExitStack() as ph:
            k.pstack = ph
            yaTr = Ring(k, "yaTg", 2, [P, 8, GS], BF16, dma=True)
            ymTr = Ring(k, "ymTg", 2, [P, 8, GS], BF16, dma=True)
            sgar = Ring(k, "sga", 3, [P, GS], F32, dma=True)
            sgmr = Ring(k, "sgm", 3, [P, GS], F32, dma=True)
            y1r = Ring(k, "y1", 2, [P, GS], F32)
            y2r = Ring(k, "y2", 2, [P, GS], F32)
            yTr = Ring(k, "yT", 1, [P, 8, GS], BF16)
            xr5 = Ring(k, "x5", 2, [P, D], F32, dma=True)
            otr = Ring(k, "ot", 2, [P, D], F32, dma=True)
            bA = Ring(k, "bA", 3, [P, 512], F32, psum=True)
            bM = Ring(k, "bM", 3, [P, 512], F32, psum=True)
            bF = Ring(k, "bF", 2, [P, 512], F32, psum=True)
            def load_branch(g):
                yaTg, yaTgb, yas = yaTr.next()
                ymTg, ymTgb, yms = ymTr.next()
                op("sp", lambda e: e.dma_start(out=yaTg[:], in_=yaT_d[:, :, g * GS:(g + 1) * GS].rearrange("k p t -> p k t")), writes=[yaTgb], dma=yas)
                op("sp", lambda e: e.dma_start(out=ymTg[:], in_=ymT_d[:, :, g * GS:(g + 1) * GS].rearrange("k p t -> p k t")), writes=[ymTgb], dma=yms)
                return yaTg, yaTgb, ymTg, ymTgb

            nxt_br = load_branch(0)
            for g in range(NG):
                yaTg, yaTgb, ymTg, ymTgb = nxt_br
                if g + 1 < NG:
                    nxt_br = load_branch(g + 1)
                yT, yTb, _ = yTr.next()
                for cc in range(8):
                    sga, sgab, sgas = sgar.next()
                    sgm, sgmb, sgms = sgmr.next()
                    op("sp", lambda e, sga=sga, cc=cc, g=g: e.dma_start(out=sga[:], in_=sga_d[cc, :, g * GS:(g + 1) * GS]), writes=[sgab], dma=sgas)
                    op("sp", lambda e, sgm=sgm, cc=cc, g=g: e.dma_start(out=sgm[:], in_=sgm_d[cc, :, g * GS:(g + 1) * GS]), writes=[sgmb], dma=sgms)
                    ba, bab, _ = bA.next()
                    bm, bmb, _ = bM.next()
                    for kc in range(8):
                        op("pe", lambda e, ba=ba, kc=kc, cc=cc, yaTg=yaTg: e.matmul(ba[:], lhsT=Wa[:, kc, cc * P:(cc + 1) * P], rhs=yaTg[:, kc, :], start=(kc == 0), stop=(kc == 7)),
                           reads=[BWa, yaTgb], writes=[bab])
                    for kc in range(8):
                        op("pe", lambda e, bm=bm, kc=kc, cc=cc, ymTg=ymTg: e.matmul(bm[:], lhsT=Wm[:, kc, cc * P:(cc + 1) * P], rhs=ymTg[:, kc, :], start=(kc == 0), stop=(kc == 7)),
                           reads=[BWm, ymTgb], writes=[bmb])
                    y1, y1b, _ = y1r.next()
                    y2, y2b, _ = y2r.next()
                    op("dve", lambda e, y1=y1, sga=sga, ba=ba: e.scalar_tensor_tensor(out=y1[:], in0=sga[:], scalar=1.0, in1=ba[:], op0=ALU.add, op1=ALU.mult), reads=[sgab, bab], writes=[y1b])
                    op("dve", lambda e, y2=y2, sgm=sgm, bm=bm: e.scalar_tensor_tensor(out=y2[:], in0=sgm[:], scalar=1.0, in1=bm[:], op0=ALU.add, op1=ALU.mult), reads=[sgmb, bmb], writes=[y2b])
                    op("pool", lambda e, yT=yT, cc=cc, y1=y1, y2=y2: e.tensor_tensor(yT[:, cc, :], y1[:], y2[:], ALU.add), reads=[y1b, y2b], writes=[yTb])
                for tt in range(4):
                    ti = g * 4 + tt
                    xt, xb, xs = xr5.next()
                    ot, otb, ots = otr.next()
                    op("sp", lambda e, xt=xt, ti=ti: e.dma_start(out=xt[:], in_=x[ti * P:(ti + 1) * P, :]), writes=[xb], dma=xs)
                    for og in range(2):
                        bf_, bfb, _ = bF.next()
                        for cc in range(8):
                            op("pe", lambda e, bf_=bf_, cc=cc, tt=tt, og=og, yT=yT: e.matmul(bf_[:], lhsT=yT[:, cc, tt * P:(tt + 1) * P], rhs=Wo[:, cc, og * 512:(og + 1) * 512], start=(cc == 0), stop=(cc == 7)),
                               reads=[yTb, BWo], writes=[bfb])
                        op("dve", lambda e, ot=ot, bf_=bf_, og=og: e.tensor_tensor(ot[:, og * 512:(og + 1) * 512], bf_[:], gate_half[:, og * 512:(og + 1) * 512], ALU.mult), reads=[bfb, Bgate], writes=[otb])
                    op("pool", lambda e, ot=ot, xt=xt: e.tensor_tensor(ot[:], ot[:], xt[:], ALU.add), reads=[otb, xb], writes=[otb])
                    op("act", lambda e, ot=ot, ti=ti: e.dma_start(out=out[ti * P:(ti + 1) * P, :], in_=ot[:]), reads=[otb], dma=ots)
            S.barrier()
            S.emit()
    return nc


def _host_inputs(inputs):
    f = np.float32
    x = np.ascontiguousarray(inputs["x"], dtype=f)
    c = np.asarray(inputs["c"], dtype=f)
    rel_bias = np.asarray(inputs["rel_bias"], dtype=f)
    dist = np.arange(0, 1024)
    max_exact = 16
    nf = np.maximum(dist, 1).astype(np.float32)
    large = max_exact + (np.log(nf / max_exact) / np.log(128 / max_exact) * (32 - max_exact)).astype(np.int32)
    large = np.minimum(large, 31)
    bucket = np.where(dist < max_exact, dist, large)
    kk = np.arange(128)[:, None]
    jj = np.arange(1024)[None, :]
    dd = jj - 384 - kk
    valid = dd >= 0
    bidx = bucket[np.clip(dd, 0, 1023)]
    utab = np.empty((8, 128, 1024), dtype=f)
    for h in range(8):
        g = rel_bias[:, h][bidx]
        utab[h] = np.where(valid, g, f(NEGV))
    conv_w = np.asarray(inputs["conv_w"], dtype=f)[0]
    conv_b = np.asarray(inputs["conv_b"], dtype=f)[0]
    convw = np.ascontiguousarray(conv_w.T.reshape(16, 128, 4).transpose(1, 0, 2))
    convb = np.ascontiguousarray(conv_b.reshape(16, 128).T)
    ident = np.eye(128, dtype=f)
    caus = np.triu(np.ones((128, 128), dtype=f))
    ind = np.zeros((16, 16, 128), dtype=f)
    for b in range(16):
        ind[b, b, :] = 1.0
    common = {
        "w_ada": np.ascontiguousarray(inputs["w_ada"][0], dtype=f),
        "b_ada": np.ascontiguousarray(inputs["b_ada"][0:1], dtype=f),
        "norm_w": np.ascontiguousarray(inputs["norm_w"][0:1], dtype=f),
        "w_in": np.ascontiguousarray(inputs["w_in"][0], dtype=f),
        "qnw": np.ascontiguousarray(inputs["q_norm_w"][0:1], dtype=f),
        "knw": np.ascontiguousarray(inputs["k_norm_w"][0:1], dtype=f),
        "utab": utab,
        "bias31": np.ascontiguousarray(rel_bias[31:32, :]),
        "convw": convw,
        "convb": convb,
        "b_ig": np.ascontiguousarray(np.asarray(inputs["b_igate"], dtype=f)[0].reshape(4, 1)),
        "b_fg": np.ascontiguousarray(np.asarray(inputs["b_fgate"], dtype=f)[0].reshape(4, 1)),
        "mlnw": np.ascontiguousarray(inputs["ml_norm_w"][0:1], dtype=f),
        "w_att": np.ascontiguousarray(inputs["w_att_proj"][0], dtype=f),
        "w_ml": np.ascontiguousarray(inputs["w_ml_proj"][0], dtype=f),
        "w_out": np.ascontiguousarray(inputs["w_out"][0], dtype=f),
        "c_ident": ident,
        "c_caus": caus,
        "c_ind": ind.reshape(16, 16 * 128),
    }
    maps = []
    for b in range(x.shape[0]):
        m = dict(common)
        m["x"] = x[b]
        m["ccol"] = np.ascontiguousarray(c[b].reshape(8, 128).T)
        maps.append(m)
    return maps


def kernel(**inputs):
    maps = _host_inputs(inputs)
    nc = build()
    res = run_bass_kernel_spmd(nc, maps, core_ids=list(range(8)))
    return np.stack([np.asarray(r["out"], dtype=np.float32) for r in res.results], axis=0)
```
